# Optimizing a Trainium2 kernel written in Bass

```python
import math
import jax, jax.numpy as jnp
from jax import lax
import numpy as np

D_MODEL = 1024
BATCH = 16
SEQ = 256
DEPTH = 2
DEC_BATCH = 8
DEC_SEQ = 1024
PAST_LEN = 256

GRID_W = 64
N_EVEN = (DEPTH + 1) // 2
N_ODD = DEPTH // 2
NA_HEADS = 8
NA_HEAD_DIM = 64
NA_WIDTH = NA_HEADS * NA_HEAD_DIM
NA_ROWS_MAX = 8
NA_COLS = 16
RPB_ROWS = 2 * NA_ROWS_MAX - 1
RPB_COLS = 2 * NA_COLS - 1
LRU_WIDTH = 512
LRU_BLOCKS = 8
LRU_BLOCK = LRU_WIDTH // LRU_BLOCKS
LRU_C = 8.0
CONV_WIDTH = 4
CONV_PAD_L = 2
CONV_PAD_R = 1
DIFF_HEADS = 8
DIFF_HEAD_DIM = 64
DIFF_WIDTH = DIFF_HEADS * 2 * DIFF_HEAD_DIM
EVEN_IN = 4 * NA_WIDTH + 2 * LRU_WIDTH
EVEN_OUT = NA_WIDTH + LRU_WIDTH
ODD_IN = 4 * DIFF_WIDTH
ROPE_BASE = 10000.0
Q_BLOCK = 128
EPS = 1e-6
NEG_INF = -1e30

kernel_name = 'hybrid_natten_rglru_diffattn_prefix_step'


def rmsnorm(x, g):
    xf = x.astype(jnp.float32)
    y = xf * lax.rsqrt(jnp.mean(xf * xf, axis=-1, keepdims=True) + EPS)
    return y.astype(x.dtype) * g


def adaln_input(x, cvec, norm_g, ada_w, ada_b):
    m = (jax.nn.silu(cvec) @ ada_w + ada_b)[..., None, :]
    shift, scale, gate = jnp.split(m, 3, axis=-1)
    return rmsnorm(x, norm_g) * (1 + scale) + shift, gate


def _query_blocks(q):
    b, t = q.shape[:2]
    return jnp.moveaxis(q.reshape((b, t // Q_BLOCK, Q_BLOCK) + q.shape[2:]), 1, 0)


def _merge_blocks(o):
    nb, b, qb = o.shape[:3]
    return jnp.moveaxis(o, 0, 1).reshape((b, nb * qb) + o.shape[3:])


def dense_attention(q, k, v):
    scale = q.shape[-1] ** -0.5
    def one(qb):
        s = jnp.einsum('bqhd,bkhd->bhqk', qb, k).astype(jnp.float32) * scale
        p = jax.nn.softmax(s, axis=-1).astype(v.dtype)
        return jnp.einsum('bhqk,bkhd->bqhd', p, v)
    return _merge_blocks(lax.map(one, _query_blocks(q)))


def diff_attention(q, k, v, lam):
    scale = DIFF_HEAD_DIM ** -0.5
    k1, k2 = jnp.split(k, 2, axis=-1)
    def one(qb):
        q1, q2 = jnp.split(qb, 2, axis=-1)
        s1 = jnp.einsum('bqhd,bkhd->bhqk', q1, k1).astype(jnp.float32) * scale
        s2 = jnp.einsum('bqhd,bkhd->bhqk', q2, k2).astype(jnp.float32) * scale
        p = jax.nn.softmax(s1, axis=-1) - lam * jax.nn.softmax(s2, axis=-1)
        return jnp.einsum('bhqk,bkhd->bqhd', p.astype(v.dtype), v)
    return _merge_blocks(lax.map(one, _query_blocks(q)))


def rope_2d(x):
    t_len = x.shape[1]
    half = DIFF_HEAD_DIM // 2
    m = half // 2
    t = jnp.arange(t_len)
    rows = (t // GRID_W).astype(jnp.float32)
    cols = (t % GRID_W).astype(jnp.float32)
    inv = ROPE_BASE ** (-jnp.arange(m, dtype=jnp.float32) / m)
    def rot(xh, pos):
        ang = pos[:, None] * inv[None, :]
        cos = jnp.cos(ang)[None, :, None, None, :].astype(x.dtype)
        sin = jnp.sin(ang)[None, :, None, None, :].astype(x.dtype)
        x1, x2 = jnp.split(xh, 2, axis=-1)
        return jnp.concatenate([x1 * cos - x2 * sin, x2 * cos + x1 * sin], axis=-1)
    return jnp.concatenate([rot(x[..., :half], rows), rot(x[..., half:], cols)], axis=-1)


def neighbourhood_attention(q, k, v, ck, cv, rpb):
    b, t, h, d = q.shape
    n_rows = t // GRID_W
    kr = min(NA_ROWS_MAX, n_rows)
    kc = NA_COLS
    scale = d ** -0.5
    rows = jnp.arange(n_rows)
    cols = jnp.arange(GRID_W)
    row_start = jnp.clip(rows - kr // 2, 0, n_rows - kr)
    row_idx = row_start[:, None] + jnp.arange(kr)[None, :]
    col_start = jnp.clip(cols - kc // 2, 0, GRID_W - kc)
    col_in = (cols[None, :] >= col_start[:, None]) & (cols[None, :] < col_start[:, None] + kc)
    dr = row_idx - rows[:, None] + (NA_ROWS_MAX - 1)
    dc = jnp.clip(cols[None, :] - cols[:, None] + (kc - 1), 0, 2 * kc - 2)
    bias = rpb[:, dr[:, None, :, None], dc[None, :, None, :]]
    qg = q.reshape(b, n_rows, GRID_W, h, d)
    kg = jnp.take(k.reshape(b, n_rows, GRID_W, h, d), row_idx, axis=1)
    vg = jnp.take(v.reshape(b, n_rows, GRID_W, h, d), row_idx, axis=1)
    s_win = jnp.einsum('brqhd,brkwhd->bhrqkw', qg, kg).astype(jnp.float32) * scale + bias[None].astype(jnp.float32)
    s_win = jnp.where(col_in[:, None, :], s_win, NEG_INF)
    s_ctx = jnp.einsum('brqhd,bkhd->bhrqk', qg, ck).astype(jnp.float32) * scale
    n_win = kr * GRID_W
    s = jnp.concatenate([s_win.reshape(b, h, n_rows, GRID_W, n_win), s_ctx], axis=-1)
    p = jax.nn.softmax(s, axis=-1).astype(v.dtype)
    p_win = p[..., :n_win].reshape(b, h, n_rows, GRID_W, kr, GRID_W)
    p_ctx = p[..., n_win:]
    o = jnp.einsum('bhrqkw,brkwhd->brqhd', p_win, vg) + jnp.einsum('bhrqk,bkhd->brqhd', p_ctx, cv)
    return o.reshape(b, t, h, d)


def dwconv(x, w, bias):
    y = lax.conv_general_dilated(x, w[:, None, :], window_strides=(1,), padding=[(CONV_PAD_L, CONV_PAD_R)],
                                 dimension_numbers=('NWC', 'WIO', 'NWC'), feature_group_count=x.shape[-1])
    return y + bias


def rglru(x, w_a, b_a, w_x, b_x, lam, h0, reverse):
    b, t, wd = x.shape
    xb = x.reshape(b, t, LRU_BLOCKS, LRU_BLOCK)
    r = jax.nn.sigmoid(jnp.einsum('btnc,ncd->btnd', xb, w_a).reshape(b, t, wd) + b_a)
    i = jax.nn.sigmoid(jnp.einsum('btnc,ncd->btnd', xb, w_x).reshape(b, t, wd) + b_x)
    log_a = -LRU_C * r * jax.nn.softplus(-lam)
    a = jnp.exp(log_a)
    u = jnp.sqrt(-jnp.expm1(2 * log_a)) * (i * x)
    def combine(e1, e2):
        a1, b1 = e1
        a2, b2 = e2
        return a1 * a2, a2 * b1 + b2
    a_acc, u_acc = lax.associative_scan(combine, (a, u), reverse=reverse, axis=1)
    hs = u_acc + a_acc * h0[:, None, :]
    final = hs[:, 0] if reverse else hs[:, -1]
    return hs, final


def lru_branch(xb, conv_w, conv_b, wa, ba, wx, bx, lam, h0f, h0b):
    xc = dwconv(xb, conv_w, conv_b)
    hf, ff = rglru(xc, wa[0], ba[0], wx[0], bx[0], lam[0], h0f, False)
    hb, fb = rglru(xc, wa[1], ba[1], wx[1], bx[1], lam[1], h0b, True)
    return hf + hb, ff, fb


def even_split(h, w_in):
    return jnp.split(h @ w_in, [NA_WIDTH, 2 * NA_WIDTH, 3 * NA_WIDTH, 4 * NA_WIDTH, 4 * NA_WIDTH + LRU_WIDTH], axis=-1)


def even_layer_ctx(x, cvec, norm_g, ada_w, ada_b, w_in, conv_w, conv_b, wa, ba, wx, bx, lam, w_out):
    b, t, _ = x.shape
    h, gate = adaln_input(x, cvec, norm_g, ada_w, ada_b)
    q, k, v, ga, xb, gb = even_split(h, w_in)
    q = q.reshape(b, t, NA_HEADS, NA_HEAD_DIM)
    k = k.reshape(b, t, NA_HEADS, NA_HEAD_DIM)
    v = v.reshape(b, t, NA_HEADS, NA_HEAD_DIM)
    o_a = dense_attention(q, k, v).reshape(b, t, NA_WIDTH)
    zeros = jnp.zeros((b, LRU_WIDTH), x.dtype)
    o_b, ff, fb = lru_branch(xb, conv_w, conv_b, wa, ba, wx, bx, lam, zeros, zeros)
    o = jnp.concatenate([o_a * jax.nn.silu(ga), o_b * jax.nn.silu(gb)], axis=-1) @ w_out
    return x + gate * o, k, v, jnp.stack([ff, fb], axis=1)


def even_layer_lat(x, cvec, ck, cv, lru_state, rpb, norm_g, ada_w, ada_b, w_in, conv_w, conv_b, wa, ba, wx, bx, lam, w_out):
    b, t, _ = x.shape
    h, gate = adaln_input(x, cvec, norm_g, ada_w, ada_b)
    q, k, v, ga, xb, gb = even_split(h, w_in)
    q = q.reshape(b, t, NA_HEADS, NA_HEAD_DIM)
    k = k.reshape(b, t, NA_HEADS, NA_HEAD_DIM)
    v = v.reshape(b, t, NA_HEADS, NA_HEAD_DIM)
    o_a = neighbourhood_attention(q, k, v, ck, cv, rpb).reshape(b, t, NA_WIDTH)
    o_b, _, _ = lru_branch(xb, conv_w, conv_b, wa, ba, wx, bx, lam, lru_state[:, 0], lru_state[:, 1])
    o = jnp.concatenate([o_a * jax.nn.silu(ga), o_b * jax.nn.silu(gb)], axis=-1) @ w_out
    return x + gate * o


def diff_lambda(lq1, lk1, lq2, lk2, lam_init):
    return jnp.exp(jnp.sum(lq1 * lk1)) - jnp.exp(jnp.sum(lq2 * lk2)) + lam_init


def diff_output(o, g, sub_g, w_out, lam_init):
    b, t = o.shape[:2]
    o = rmsnorm(o, sub_g) * (1 - lam_init)
    return (o.reshape(b, t, DIFF_WIDTH) * jax.nn.silu(g)) @ w_out


def odd_layer_ctx(x, cvec, lam_init, norm_g, ada_w, ada_b, w_in, lq1, lk1, lq2, lk2, sub_g, w_out):
    b, t, _ = x.shape
    h, gate = adaln_input(x, cvec, norm_g, ada_w, ada_b)
    q, k, v, g = jnp.split(h @ w_in, 4, axis=-1)
    q = q.reshape(b, t, DIFF_HEADS, 2 * DIFF_HEAD_DIM)
    k = k.reshape(b, t, DIFF_HEADS, 2 * DIFF_HEAD_DIM)
    v = v.reshape(b, t, DIFF_HEADS, 2 * DIFF_HEAD_DIM)
    lam = diff_lambda(lq1, lk1, lq2, lk2, lam_init)
    o = diff_attention(q, k, v, lam)
    return x + gate * diff_output(o, g, sub_g, w_out, lam_init), k, v


def odd_layer_lat(x, cvec, ck, cv, lam_init, norm_g, ada_w, ada_b, w_in, lq1, lk1, lq2, lk2, sub_g, w_out):
    b, t, _ = x.shape
    h, gate = adaln_input(x, cvec, norm_g, ada_w, ada_b)
    q, k, v, g = jnp.split(h @ w_in, 4, axis=-1)
    q = rope_2d(q.reshape(b, t, DIFF_HEADS, 2, DIFF_HEAD_DIM)).reshape(b, t, DIFF_HEADS, 2 * DIFF_HEAD_DIM)
    k = rope_2d(k.reshape(b, t, DIFF_HEADS, 2, DIFF_HEAD_DIM)).reshape(b, t, DIFF_HEADS, 2 * DIFF_HEAD_DIM)
    v = v.reshape(b, t, DIFF_HEADS, 2 * DIFF_HEAD_DIM)
    k_all = jnp.concatenate([ck, k], axis=1)
    v_all = jnp.concatenate([cv, v], axis=1)
    lam = diff_lambda(lq1, lk1, lq2, lk2, lam_init)
    o = diff_attention(q, k_all, v_all, lam)
    return x + gate * diff_output(o, g, sub_g, w_out, lam_init)


def setup_inputs(seed: int = 0) -> dict:
    key = jax.random.key(seed)
    kit = iter(jax.random.split(key, 40))
    def nrm(shape, s):
        return jax.random.normal(next(kit), shape, jnp.float32) * s
    def gain(shape):
        return 1.0 + nrm(shape, 0.02)
    a0 = jax.random.uniform(next(kit), (N_EVEN, 2, LRU_WIDTH), jnp.float32, minval=0.9, maxval=0.999)
    a_base = a0 ** (1.0 / LRU_C)
    lru_lam = jnp.log(a_base) - jnp.log1p(-a_base)
    return {
        'x_prompt': nrm((BATCH, SEQ, D_MODEL), 1.0),
        'x_sample': nrm((DEC_BATCH, DEC_SEQ, D_MODEL), 1.0),
        'c': nrm((DEC_BATCH, D_MODEL), 1.0),
        'cache_na_k': nrm((DEC_BATCH, N_EVEN, PAST_LEN, NA_HEADS, NA_HEAD_DIM), 1.0),
        'cache_na_v': nrm((DEC_BATCH, N_EVEN, PAST_LEN, NA_HEADS, NA_HEAD_DIM), 1.0),
        'state_lru': nrm((DEC_BATCH, N_EVEN, 2, LRU_WIDTH), 0.3),
        'cache_diff_k': nrm((DEC_BATCH, N_ODD, PAST_LEN, DIFF_HEADS, 2 * DIFF_HEAD_DIM), 1.0),
        'cache_diff_v': nrm((DEC_BATCH, N_ODD, PAST_LEN, DIFF_HEADS, 2 * DIFF_HEAD_DIM), 1.0),
        'c_ctx': nrm((D_MODEL,), 1.0),
        'e_norm': gain((N_EVEN, D_MODEL)),
        'e_ada_w': nrm((N_EVEN, D_MODEL, 3 * D_MODEL), D_MODEL ** -0.5),
        'e_ada_b': nrm((N_EVEN, 3 * D_MODEL), 0.02),
        'e_w_in': nrm((N_EVEN, D_MODEL, EVEN_IN), D_MODEL ** -0.5),
        'e_rpb': nrm((N_EVEN, NA_HEADS, RPB_ROWS, RPB_COLS), 0.2),
        'e_conv_w': nrm((N_EVEN, CONV_WIDTH, LRU_WIDTH), 0.5),
        'e_conv_b': nrm((N_EVEN, LRU_WIDTH), 0.02),
        'e_lru_wa': nrm((N_EVEN, 2, LRU_BLOCKS, LRU_BLOCK, LRU_BLOCK), LRU_BLOCK ** -0.5),
        'e_lru_ba': nrm((N_EVEN, 2, LRU_WIDTH), 0.02),
        'e_lru_wx': nrm((N_EVEN, 2, LRU_BLOCKS, LRU_BLOCK, LRU_BLOCK), LRU_BLOCK ** -0.5),
        'e_lru_bx': nrm((N_EVEN, 2, LRU_WIDTH), 0.02),
        'e_lru_lam': lru_lam,
        'e_w_out': nrm((N_EVEN, EVEN_OUT, D_MODEL), EVEN_OUT ** -0.5),
        'o_norm': gain((N_ODD, D_MODEL)),
        'o_ada_w': nrm((N_ODD, D_MODEL, 3 * D_MODEL), D_MODEL ** -0.5),
        'o_ada_b': nrm((N_ODD, 3 * D_MODEL), 0.02),
        'o_w_in': nrm((N_ODD, D_MODEL, ODD_IN), D_MODEL ** -0.5),
        'o_lq1': nrm((N_ODD, DIFF_HEAD_DIM), 0.1),
        'o_lk1': nrm((N_ODD, DIFF_HEAD_DIM), 0.1),
        'o_lq2': nrm((N_ODD, DIFF_HEAD_DIM), 0.1),
        'o_lk2': nrm((N_ODD, DIFF_HEAD_DIM), 0.1),
        'o_sub_g': gain((N_ODD, 2 * DIFF_HEAD_DIM)),
        'o_w_out': nrm((N_ODD, DIFF_WIDTH, D_MODEL), DIFF_WIDTH ** -0.5),
        'final_norm': gain((D_MODEL,)),
    }


def reference(x_prompt, x_sample, c, cache_na_k, cache_na_v, state_lru, cache_diff_k, cache_diff_v, c_ctx,
              e_norm, e_ada_w, e_ada_b, e_w_in, e_rpb, e_conv_w, e_conv_b, e_lru_wa, e_lru_ba, e_lru_wx, e_lru_bx,
              e_lru_lam, e_w_out, o_norm, o_ada_w, o_ada_b, o_w_in, o_lq1, o_lk1, o_lq2, o_lk2, o_sub_g, o_w_out,
              final_norm):
    xp = x_prompt
    xs = x_sample
    na_k, na_v, lru_s, df_k, df_v = [], [], [], [], []
    for i in range(DEPTH):
        j = i // 2
        if i % 2 == 0:
            ep = (e_norm[j], e_ada_w[j], e_ada_b[j], e_w_in[j], e_conv_w[j], e_conv_b[j], e_lru_wa[j], e_lru_ba[j],
                  e_lru_wx[j], e_lru_bx[j], e_lru_lam[j], e_w_out[j])
            xp, k, v, s = even_layer_ctx(xp, c_ctx, *ep)
            xs = even_layer_lat(xs, c, cache_na_k[:, j], cache_na_v[:, j], state_lru[:, j], e_rpb[j], *ep)
            na_k.append(k)
            na_v.append(v)
            lru_s.append(s)
        else:
            lam_init = 0.8 - 0.6 * math.exp(-0.3 * i)
            op = (o_norm[j], o_ada_w[j], o_ada_b[j], o_w_in[j], o_lq1[j], o_lk1[j], o_lq2[j], o_lk2[j], o_sub_g[j], o_w_out[j])
            xp, k, v = odd_layer_ctx(xp, c_ctx, lam_init, *op)
            xs = odd_layer_lat(xs, c, cache_diff_k[:, j], cache_diff_v[:, j], lam_init, *op)
            df_k.append(k)
            df_v.append(v)
    y_prompt = rmsnorm(xp, final_norm)
    y_sample = rmsnorm(xs, final_norm)
    new_na_k = jnp.stack(na_k, axis=1)
    new_na_v = jnp.stack(na_v, axis=1)
    new_lru = jnp.stack(lru_s, axis=1)
    new_diff_k = jnp.stack(df_k, axis=1)
    new_diff_v = jnp.stack(df_v, axis=1)
    return (y_prompt, y_sample, new_na_k, new_na_v, new_lru, new_diff_k, new_diff_v)
```

```python
import math
import os
from contextlib import ExitStack

import numpy as np
import concourse.bass as bass
import concourse.mybir as mybir
from concourse.bass_utils import run_bass_kernel_spmd

F32 = mybir.dt.float32
BF16 = mybir.dt.bfloat16
AF = mybir.ActivationFunctionType
ALU = mybir.AluOpType
AX = mybir.AxisListType

ENGS = ("pe", "act", "dve", "pool", "sp")
EPS = 1e-6
NEG = -30000.0
T = 1536
TS = 1024
NCORES = 8
LAM_INIT = 0.8 - 0.6 * math.exp(-0.3 * 1)


class Sched:
    N_DMA_SEMS = 32
    QSEMS = {"sp": list(range(0, 16)), "pool": list(range(16, 28)), "act": list(range(28, 32))}

    def __init__(self):
        self.ops = {e: [] for e in ENGS}
        self.cnt = {e: 0 for e in ENGS}
        self.obs = {e: {f: 0 for f in ENGS} for e in ENGS}
        self.obs_dma = {e: 0 for e in ENGS}
        self.last_w = {}
        self.readers = {}
        self.n_dma = 0
        self.dma_tok = []
        self.q_hist = {q: [] for q in self.QSEMS}
        self.inherit = {}
        self.seen = set()
        self.frozen = False

    def _need(self, eng, tok, waits):
        if tok is None:
            return
        if tok[0] == "e":
            _, e2, n, vc, dm = tok
            if eng == "pe" and e2 == "pe":
                return
            if self.obs[eng][e2] >= n:
                return
            waits.append(("e", e2, n))
            o = self.obs[eng]
            for f in ENGS:
                if vc[f] > o[f]:
                    o[f] = vc[f]
            if o[e2] < n:
                o[e2] = n
            self.obs_dma[eng] |= dm
        else:
            _, did, vc, dm = tok
            if (self.obs_dma[eng] >> did) & 1:
                return
            waits.append(("d", did))
            o = self.obs[eng]
            for f in ENGS:
                if vc[f] > o[f]:
                    o[f] = vc[f]
            self.obs_dma[eng] |= dm | (1 << did)

    def _deps(self, eng, reads, writes):
        waits = []
        for k in list(reads) + list(writes):
            if k not in self.seen:
                self.seen.add(k)
                p = k.split(".")[0]
                for t in self.inherit.get(p, ()):
                    self._need(eng, t, waits)
        for k in reads:
            self._need(eng, self.last_w.get(k), waits)
        for k in writes:
            self._need(eng, self.last_w.get(k), waits)
            for t in self.readers.get(k, ()):
                self._need(eng, t, waits)
        return waits

    def _commit(self, tok, reads, writes):
        for k in reads:
            self.readers.setdefault(k, []).append(tok)
        for k in writes:
            self.last_w[k] = tok
            self.readers[k] = []

    def op(self, eng, fn, reads=(), writes=()):
        if self.frozen:
            return None
        pr = [k for k in reads if k.startswith("ps") and k not in writes]
        if pr:
            writes = list(writes) + pr
        waits = self._deps(eng, reads, writes)
        self.cnt[eng] += 1
        n = self.cnt[eng]
        vc = dict(self.obs[eng])
        vc[eng] = n
        tok = ("e", eng, n, vc, self.obs_dma[eng])
        self.ops[eng].append((waits, fn, ("e", n)))
        self._commit(tok, reads, writes)
        return tok

    def dma(self, eng, fn, reads=(), writes=()):
        if self.frozen:
            return None
        waits = self._deps(eng, reads, writes)
        did = self.n_dma
        self.n_dma += 1
        hist = self.q_hist[eng]
        qs = self.QSEMS[eng]
        k = len(hist)
        if k >= len(qs):
            prev = hist[k - len(qs)]
            if not ((self.obs_dma[eng] >> prev) & 1):
                waits.append(("d", prev))
                self.obs_dma[eng] |= (1 << prev)
        hist.append(did)
        self.dma_tok.append((qs[k % len(qs)], 16 * (k // len(qs) + 1)))
        tok = ("d", did, dict(self.obs[eng]), self.obs_dma[eng])
        self.ops[eng].append((waits, fn, ("d", did)))
        self._commit(tok, reads, writes)
        return tok

    def alias(self, new_prefixes, old_prefixes):
        toks = []
        olds = tuple(old_prefixes)
        for k, t in self.last_w.items():
            if k.split(".")[0] in olds and t is not None:
                toks.append(t)
        for k, ts in self.readers.items():
            if k.split(".")[0] in olds:
                toks.extend(ts)
        best = {}
        dm = []
        for t in toks:
            if t[0] == "e":
                if t[1] not in best or best[t[1]][2] < t[2]:
                    best[t[1]] = t
            else:
                dm.append(t)
        toks = list(best.values()) + dm
        for p in new_prefixes:
            self.inherit[p] = self.inherit.get(p, []) + toks
            for k in [k for k in self.seen if k.split(".")[0] == p]:
                self.seen.discard(k)

    def wait_all(self, eng, toks):
        waits = []
        for t in toks:
            self._need(eng, t, waits)
        self.ops[eng].append((waits, None, None))

    def emit(self, nc, sems, dsems):
        engmap = {"pe": "tensor", "act": "scalar", "dve": "vector", "pool": "gpsimd", "sp": "sync"}
        sched = self
        with nc.Block() as block:
            for e in ENGS:
                def body(eobj, e=e):
                    for waits, fn, inc in sched.ops[e]:
                        for w in waits:
                            if w[0] == "e":
                                eobj.wait_ge(sems[w[1]], w[2])
                            else:
                                si, tgt = sched.dma_tok[w[1]]
                                eobj.wait_ge(dsems[si], tgt)
                        if fn is None:
                            continue
                        ins = fn(eobj)
                        if inc[0] == "e":
                            ins.then_inc(sems[e], 1)
                        else:
                            si, tgt = sched.dma_tok[inc[1]]
                            ins.then_inc(dsems[si], 16)
                getattr(block, engmap[e])(body)


def _rope_tables():
    p = np.arange(128)
    d = p % 64
    half = d // 32
    idx = d % 32
    m = idx % 16
    first = idx < 16
    t = np.arange(TS)
    rows = (t // 64).astype(np.float32)
    cols = (t % 64).astype(np.float32)
    inv = (10000.0 ** (-np.arange(16, dtype=np.float32) / 16.0)).astype(np.float32)
    pos = np.where(half[:, None] == 0, rows[None, :], cols[None, :]).astype(np.float32)
    ang = (pos * inv[m][:, None]).astype(np.float32)
    C = np.cos(ang).astype(np.float32)
    Sg = np.sin(ang).astype(np.float32)
    Sg = np.where(first[:, None], -Sg, Sg).astype(np.float32)
    partner = np.where(first, p + 16, p - 16)
    perm = np.zeros((128, 128), np.float32)
    perm[partner, p] = 1.0
    return C, Sg, perm


NA_J = {0: list(range(0, 6)), 1: list(range(2, 8))}


def _na_tables():
    qc = np.arange(64)
    cs = np.clip(qc - 8, 0, 48)
    kc = np.arange(64)
    inwin = (kc[:, None] >= cs[None, :]) & (kc[:, None] < cs[None, :] + 16)
    colneg = np.where(inwin, 0.0, NEG).astype(np.float32)
    colneg = np.concatenate([colneg, colneg], axis=0)
    rowB = np.zeros((128, 12, 8), np.float32)
    for q in (0, 1):
        for ji, j in enumerate(NA_J[q]):
            for krl in range(2):
                for qrl in range(8):
                    kr = 2 * j + krl
                    qr = 8 * q + qrl
                    rs = min(max(qr - 4, 0), 8)
                    ok = rs <= kr <= rs + 7
                    rowB[krl, q * 6 + ji, qrl] = 0.0 if ok else NEG
    halfsel = np.zeros((128, 128), np.float32)
    halfsel[0, :64] = 1.0
    halfsel[1, 64:] = 1.0
    anti = np.zeros((128, 128), np.float32)
    for m in range(128):
        anti[(m // 64) * 64 + 63 - (m % 64), m] = 1.0
    return colneg, rowB, halfsel, anti


def build_nc():
    nc = bass.Bass("TRN2", target_bir_lowering=False)
    S = Sched()
    es = ExitStack()

    def din(name, shape):
        return nc.dram_tensor(name, list(shape), F32, kind="ExternalInput")

    def dout(name, shape):
        return nc.dram_tensor(name, list(shape), F32, kind="ExternalOutput")

    d_xs = din("xs", [1024, 1024]).ap()
    d_xp = din("xp", [512, 1024]).ap()
    d_cvec = din("cvec", [2, 1024]).ap()
    d_cnk = din("cnk", [256, 512]).ap()
    d_cnv = din("cnv", [256, 512]).ap()
    d_slru = din("slru", [2, 512]).ap()
    d_cdk = din("cdk", [256, 1024]).ap()
    d_cdv = din("cdv", [256, 1024]).ap()
    d_enorm = din("e_norm", [1024]).ap()
    d_eadaw = din("e_ada_w", [1024, 3072]).ap()
    d_eadab = din("e_ada_b", [3072]).ap()
    d_ewin = din("e_w_in", [1024, 3072]).ap()
    d_rpb = din("e_rpb", [8, 15, 31]).ap()
    d_convw = din("e_conv_w", [4, 512]).ap()
    d_convb = din("e_conv_b", [512]).ap()
    d_wa = din("e_lru_wa", [2, 8, 64, 64]).ap()
    d_ba = din("e_lru_ba", [2, 512]).ap()
    d_wx = din("e_lru_wx", [2, 8, 64, 64]).ap()
    d_bx = din("e_lru_bx", [2, 512]).ap()
    d_lam = din("e_lru_lam", [2, 512]).ap()
    d_ewout = din("e_w_out", [1024, 1024]).ap()
    d_onorm = din("o_norm", [1024]).ap()
    d_oadaw = din("o_ada_w", [1024, 3072]).ap()
    d_oadab = din("o_ada_b", [3072]).ap()
    d_owin = din("o_w_in", [1024, 4096]).ap()
    d_lq1 = din("o_lq1", [64]).ap()
    d_lk1 = din("o_lk1", [64]).ap()
    d_lq2 = din("o_lq2", [64]).ap()
    d_lk2 = din("o_lk2", [64]).ap()
    d_subg = din("o_sub_g", [128]).ap()
    d_owout = din("o_w_out", [1024, 1024]).ap()
    d_fnorm = din("final_norm", [1024]).ap()
    d_ident = din("c_ident", [128, 128]).ap()
    d_ropeC = din("c_ropeC", [128, 1024]).ap()
    d_ropeS = din("c_ropeS", [128, 1024]).ap()
    d_perm = din("c_perm", [128, 128]).ap()
    d_colneg = din("c_colneg", [128, 64]).ap()
    d_rowB = din("c_rowB", [128, 12, 8]).ap()
    d_halfsel = din("c_halfsel", [128, 128]).ap()
    d_anti = din("c_anti", [128, 128]).ap()

    o_ys = dout("y_s", [1024, 1024]).ap()
    o_yp = dout("y_p", [512, 1024]).ap()
    o_nak = dout("na_k", [512, 512]).ap()
    o_nav = dout("na_v", [512, 512]).ap()
    o_lru = dout("lru_o", [2, 2, 512]).ap()
    o_dfk = dout("df_k", [512, 1024]).ap()
    o_dfv = dout("df_v", [512, 1024]).ap()

    padr_t = nc.dram_tensor("padr", [8, 24, 128], F32)

    class Mem:
        def __init__(self, base, limit):
            self.p = base
            self.limit = limit
            self.n = 0

        def at(self, off, shape, dt, name=None):
            self.n += 1
            nm = "%s_%d" % (name or "t", self.n)
            n = int(np.prod(shape[1:])) * (4 if dt == F32 else 2)
            assert off % 4 == 0 and off + n <= self.limit, (nm, off, n, self.limit)
            return nc.alloc_sbuf_tensor_at(nm, list(shape), dt, offset=off)

        def new(self, shape, dt, name=None):
            n = int(np.prod(shape[1:])) * (4 if dt == F32 else 2)
            n = (n + 63) // 64 * 64
            off = self.p
            self.p += n
            return self.at(off, shape, dt, name)

        def region(self, nbytes):
            off = self.p
            self.p += (nbytes + 63) // 64 * 64
            assert self.p <= self.limit, self.p
            return off

    M = Mem(16512, 229312)
    xT = M.new([128, 8, T], F32, "xT")
    ring = [M.new([128, 8, 512], BF16, "ring") for _ in range(3)]
    hT = M.new([128, 8, T], BF16, "hT")
    oT = M.new([128, 8, T], BF16, "oT")
    ident = M.new([128, 128], F32, "ident")
    identb = M.new([128, 128], BF16, "identb")
    onesb = M.new([128, 128], BF16, "onesb")
    ones32 = M.new([128, 64], F32, "ones32")
    permb = M.new([128, 128], BF16, "permb")
    antib = M.new([128, 128], BF16, "antib")
    halfsel = M.new([128, 128], BF16, "halfsel")
    rowB = M.new([128, 12, 8], BF16, "rowB")
    colneg = M.new([128, 64], BF16, "colneg")
    wblk = M.new([128, 16, 128], BF16, "wblk")
    pvec = M.new([128, 84], F32, "pvec")
    lstT = M.new([16, 128], F32, "lstT")
    pvec2 = M.new([128, 2, 24], F32, "pvec2")
    cTb = M.new([128, 8, 2], BF16, "cTb")
    normg = [pvec[:, 0:8], pvec[:, 8:16]]
    mT = [M.new([128, 24, 2], F32, "mT") for _ in range(2)]
    gsc = [M.new([128, 8, 2], F32, "gsc") for _ in range(2)]
    cw = pvec[:, 16:32].rearrange("p (j c) -> p c j", c=4)
    cb = pvec[:, 32:36]
    lba = pvec[:, 36:44].rearrange("p (d c) -> p d c", c=4)
    lbx = pvec[:, 44:52].rearrange("p (d c) -> p d c", c=4)
    llam = pvec[:, 52:60].rearrange("p (d c) -> p d c", c=4)
    cT = pvec[:, 68:84].rearrange("p (s c) -> p c s", c=8)
    m8sp = M.new([128, 2, 4], F32, "m8sp")
    hm8 = M.new([128, 2, 4], F32, "hm8")
    hba = M.new([128, 2, 4], F32, "hba")
    hbx = M.new([128, 2, 4], F32, "hbx")
    dg = M.new([128, 16, 128], BF16, "dg")
    h0 = pvec[:, 60:68].rearrange("p (d c) -> p d c", c=4)
    lst = M.new([128, 16], F32, "lst")
    lsum = M.new([128, 2], F32, "lsum")
    neglam = M.new([128, 1], F32, "neglam")
    sgs = M.new([128, 1], F32, "sgs")
    sstat = M.new([128, 8], F32, "sstat")
    U1 = M.region(26624)
    U2 = M.region(28672)
    U3 = M.region(12288)
    W = M.p
    WSZ = M.limit - W
    assert WSZ >= 8192, WSZ
    lqk = M.at(U3, [128, 4, 64], F32, "lqk")
    lprod = M.at(U3 + 1024, [128, 2, 64], F32, "lprod")
    zt = M.at(U3 + 1536, [128, 192], F32, "zt")
    rp = M.at(U3 + 2304, [8, 15, 31], F32, "rp")
    pstg = M.at(U3 + 4224, [84, 128], F32, "pstg")
    rpr = M.at(U3 + 4736, [8, 15, 31], F32, "rpr")
    pstg2 = M.at(U3 + 6656, [48, 128], F32, "pstg2")
    xstage = [M.at(U1 + i * 4096, [128, 1024], F32, "xstage") for i in range(2)]
    fnbc = M.at(U1 + 8192, [128, 1024], F32, "fnbc")
    adab = [M.at(U2 + i * 2048, [2, 512], F32, "adab") for i in range(2)]
    msb = [M.at(U2 + 4096 + i * 2048, [2, 512], F32, "msb") for i in range(2)]
    rstd = M.at(U1 + 12288, [128, T], F32, "rstd")
    sq = [M.at(U1 + 18432 + i * 1024, [128, 512], BF16, "sq") for i in range(3)]
    ntmp = [M.at(U1 + 21504 + i * 2048, [128, 512], F32, "ntmp") for i in range(2)] + [None]
    QT = M.at(U1, [128, 4, T], BF16, "QT")
    KT = M.at(U1 + 12288, [128, 4, 1792], BF16, "KT")
    VA = M.at(U2, [128, 14, 8, 66], BF16, "VA")
    Tb = [M.at(U2 + 14848 + i * 3072, [128, 24, 64], BF16, "Tb") for i in range(2)]
    xcb = M.at(U2 + 20992, [128, T], BF16, "xcb")
    Traw = M.at(U2 + 24064, [128, 24, 64], BF16, "Traw")
    xbT = M.at(U3, [128, 4, T], BF16, "xbT")
    hT_off = nc.lookup_mloc(hT).addr
    la = M.at(hT_off, [128, T], F32, "la")
    lu = M.at(hT_off + 6144, [128, T], F32, "lu")
    lhf = M.at(hT_off + 12288, [128, T], F32, "lhf")
    lhb = M.at(hT_off + 18432, [128, T], F32, "lhb")
    V1 = M.at(U2, [128, 14, 1024], BF16, "V1")
    QTh = [M.at(U1 + i * 3072, [128, T], BF16, "QTh") for i in range(2)]
    KTh = [M.at(U1 + 6144 + i * 3072, [128, T], BF16, "KTh") for i in range(2)]
    KTc = M.at(U1 + 12288, [128, 8, 256], BF16, "KTc")
    ptmp = [M.at(U1 + 16384 + i * 2048, [128, 512], F32, "ptmp") for i in range(5)]
    ropeC = M.at(U3, [128, 1024], F32, "ropeC")
    ropeS = M.at(U3 + 4096, [128, 1024], F32, "ropeS")
    rawb = [M.at(U3 + 8192 + i * 1024, [128, 512], BF16, "rawb") for i in range(2)]
    kvst = [M.at(W + i * 2048, [128, 512], F32, "kvst") for i in range(3)]
    ckst0 = M.at(U2 + 24064, [128, 2, 512], F32, "ckst0")
    ckst1 = M.at(U1 + 16384, [128, 2, 1024], F32, "ckst1")
    pbuf = [M.at(W + i * 1024, [128, 512], BF16, "pbuf") for i in range(4)]
    rden = [M.at(W + 4096 + i * 2048, [128, 512], F32, "rden") for i in range(2)]
    rdb = [M.at(W + 4096 + i * 1024, [128, 512], BF16, "rdb") for i in range(2)]
    g2 = [M.at(W + 6144, [128, 512], F32, "g2")]
    qm = [M.at(W + 3072, [128, 512], BF16, "qm"), M.at(W + 8192, [128, 512], BF16, "qm")]

    PS = [es.enter_context(nc.psum_tensor("ps%d" % i, [128, 512], F32)) for i in range(8)]
    sems = {e: es.enter_context(nc.semaphore("s_" + e)) for e in ENGS}
    dsems = [es.enter_context(nc.semaphore("d%d" % i)) for i in range(Sched.N_DMA_SEMS)]

    rr = {"s": 0, "o": 0, "ring": 0, "pb": 0, "kv": 0, "sq": 0, "nt": 0, "g2": 0, "rd": 0, "pt": 0, "rb": 0}

    def sbank():
        rr["s"] = (rr["s"] + 1) % 4
        return rr["s"]

    def rot(name, n):
        rr[name] = (rr[name] + 1) % n
        return rr[name]

    def pk(b):
        return "ps%d" % b

    alt = {"n": 0}

    def evac_eng():
        alt["n"] += 1
        return "dve" if alt["n"] % 2 else "act"

    def copy_op(eng, out, in_, reads, writes):
        if eng == "act":
            S.op("act", lambda e: e.copy(out, in_), reads=reads, writes=writes)
        elif eng == "dve":
            S.op("dve", lambda e: e.tensor_copy(out, in_), reads=reads, writes=writes)
        else:
            S.op("pool", lambda e: e.tensor_copy(out, in_), reads=reads, writes=writes)

    def dma(q, out, in_, reads=(), writes=()):
        return S.dma(q, lambda e: e.dma_start(out=out, in_=in_), reads=reads, writes=writes)

    wstate = {"n": 0}

    def wload(parts):
        slot = wstate["n"] % 3
        wstate["n"] += 1
        for (src, q0) in parts:
            ncols = src.shape[1]
            nq = (ncols + 127) // 128
            keys = ["ring%d.q%d" % (slot, q0 + i) for i in range(nq)]
            dma("pool", ring[slot][:, :, q0 * 128:q0 * 128 + ncols], src.rearrange("(c p) n -> p c n", p=128), writes=keys)
        return slot

    def rkeys(slot, qs=(0, 1, 2, 3)):
        return ["ring%d.q%d" % (slot, q) for q in qs]


    def norm_stats(tc):
        b = sbank()
        for c in range(8):
            i = rot("sq", 3)
            S.op("act", lambda e, i=i, c=c, tc=tc: e.activation(sq[i][:], xT[:, c, tc * 512:(tc + 1) * 512], AF.Square),
                 reads=["xT.%d.%d" % (c, tc)], writes=["sq%d" % i])
            S.op("pe", lambda e, i=i, c=c, b=b: e.matmul(PS[b][:], onesb[:], sq[i][:], start=(c == 0), stop=(c == 7)),
                 reads=["sq%d" % i, "onesb"], writes=[pk(b)])
        rs = rstd[:, tc * 512:(tc + 1) * 512]
        S.op("dve", lambda e, rs=rs, b=b: e.tensor_scalar(rs, PS[b][:], 1.0 / 1024.0, EPS, ALU.mult, ALU.add),
             reads=[pk(b)], writes=["rstd.%d" % tc])
        S.op("act", lambda e, rs=rs: e.activation(rs, rs, AF.Sqrt), reads=["rstd.%d" % tc], writes=["rstd.%d" % tc])
        S.op("dve", lambda e, rs=rs: e.reciprocal(rs, rs), reads=["rstd.%d" % tc], writes=["rstd.%d" % tc])

    def norm_apply(L, tc):
        s = 0 if tc < 2 else 1
        rs = rstd[:, tc * 512:(tc + 1) * 512]
        for c in range(8):
            i = rot("nt", 2)
            S.op("dve", lambda e, i=i, c=c, tc=tc, rs=rs: e.tensor_tensor(ntmp[i][:], xT[:, c, tc * 512:(tc + 1) * 512], rs, ALU.mult),
                 reads=["xT.%d.%d" % (c, tc), "rstd.%d" % tc], writes=["ntmp%d" % i])
            dst = hT[:, c, tc * 512:(tc + 1) * 512]
            S.op("act", lambda e, i=i, dst=dst, c=c, s=s: e.activation(dst, ntmp[i][:], AF.Identity, bias=mT[L][:, c, s:s + 1],
                                                                      scale=gsc[L][:, c, s:s + 1]),
                 reads=["ntmp%d" % i, "mT%d" % L, "gsc%d.%d" % (L, s)], writes=["hT.%d.%d" % (c, tc)])

    def norm_phase(L, tcs=(0, 1, 2)):
        for tc in tcs:
            norm_stats(tc)
            norm_apply(L, tc)

    def mod_units(L, d_adaw):
        mtb = 4 + L
        units = []
        for g in range(6):
            def u(g=g):
                slot = wload([(d_adaw[:, g * 512:(g + 1) * 512], 0)])

                def mm(e):
                    ins = None
                    for j in range(4):
                        col = (g * 4 + j) * 2
                        for c in range(8):
                            ins = e.matmul(PS[mtb][:, col:col + 2], ring[slot][:, c, j * 128:(j + 1) * 128], cTb[:, c, :],
                                           start=(c == 0), stop=(c == 7))
                    return ins
                S.op("pe", mm, reads=["cTb"] + rkeys(slot), writes=[pk(mtb)])
            units.append(u)
        return units

    def mod_finish(L):
        mtb = 4 + L
        bb_ = pvec2[:, L, :].unsqueeze(2).to_broadcast([128, 24, 2])
        S.op("dve", lambda e: e.tensor_tensor(mT[L][:], PS[mtb][:, 0:48].rearrange("p (a s) -> p a s", s=2), bb_, ALU.add),
             reads=[pk(mtb), "pvec2"], writes=["mT%d" % L])
        for s_ in range(2):
            S.op("dve", lambda e, s_=s_: e.scalar_tensor_tensor(gsc[L][:, :, s_], mT[L][:, 8:16, s_], 1.0, normg[L],
                                                               ALU.add, ALU.mult),
                 reads=["mT%d" % L, "pvec"], writes=["gsc%d.%d" % (L, s_)])

    dma("sp", ident[:], d_ident, writes=["ident"])
    rows = [(0, d_enorm.rearrange("(c p) -> c p", p=128), 8), (8, d_onorm.rearrange("(c p) -> c p", p=128), 8),
            (16, d_convw.rearrange("j (c p) -> (j c) p", p=128), 16), (32, d_convb.rearrange("(c p) -> c p", p=128), 4),
            (36, d_ba.rearrange("d (c p) -> (d c) p", p=128), 8), (44, d_bx.rearrange("d (c p) -> (d c) p", p=128), 8),
            (52, d_lam.rearrange("d (c p) -> (d c) p", p=128), 8), (60, d_slru.rearrange("d (c p) -> (d c) p", p=128), 8),
            (68, d_cvec.rearrange("s (c p) -> (s c) p", p=128), 16)]
    for (r0, src_, nr) in rows:
        dma("sp", pstg[r0:r0 + nr, :], src_, writes=["pstg.%d" % r0])
    S.op("pe", lambda e: e.transpose(PS[0][:, 0:84], pstg[0:84, :], ident[0:84, 0:84]),
         reads=["pstg.%d" % r0 for (r0, _, _) in rows] + ["ident"], writes=[pk(0)])
    S.op("dve", lambda e: e.tensor_copy(pvec[:], PS[0][:, 0:84]), reads=[pk(0)], writes=["pvec"])
    dma("sp", pstg2[0:24, :], d_eadab.rearrange("(c p) -> c p", p=128), writes=["pstg2.0"])
    dma("sp", pstg2[24:48, :], d_oadab.rearrange("(c p) -> c p", p=128), writes=["pstg2.1"])
    S.op("pe", lambda e: e.transpose(PS[1][:, 0:48], pstg2[0:48, :], ident[0:48, 0:48]),
         reads=["pstg2.0", "pstg2.1", "ident"], writes=[pk(1)])
    S.op("dve", lambda e: e.tensor_copy(pvec2[:].rearrange("p l a -> p (l a)"), PS[1][:, 0:48]), reads=[pk(1)], writes=["pvec2"])
    S.op("act", lambda e: e.activation(cTb[:], cT, AF.Silu), reads=["pvec"], writes=["cTb"])
    S.op("pool", lambda e: e.memset(onesb[:], 1.0), writes=["onesb"])
    m0 = mod_units(0, d_eadaw)
    for t in range(12):
        src = d_xs[t * 128:(t + 1) * 128, :] if t < 8 else d_xp[(t - 8) * 128:(t - 7) * 128, :]
        st = t % 2
        dma("sp", xstage[st][:], src, writes=["xstage%d" % st])
        for half in range(2):
            b = sbank()

            def tr(e, st=st, half=half, b=b):
                for j in range(4):
                    c = half * 4 + j
                    ins = e.transpose(PS[b][:, j * 128:(j + 1) * 128], xstage[st][:, c * 128:(c + 1) * 128], ident[:])
                return ins
            S.op("pe", tr, reads=["xstage%d" % st, "ident"], writes=[pk(b)])
            dst = xT[:, half * 4:half * 4 + 4, t * 128:(t + 1) * 128]
            srcp = PS[b][:].rearrange("p (j n) -> p j n", n=128)
            copy_op("dve" if half == 0 else "act", dst, srcp, [pk(b)], ["xT.%d.%d" % (half * 4 + j, t // 4) for j in range(4)])
        if t % 2 == 1:
            m0[t // 2]()
        if t % 4 == 3:
            norm_stats(t // 4)
    mod_finish(0)
    S.op("dve", lambda e: e.tensor_copy(identb[:], ident[:]), reads=["ident"], writes=["identb"])
    S.op("pool", lambda e: e.memset(ones32[:], 1.0), writes=["ones32"])
    S.op("pool", lambda e: e.memset(zt[:], 0.0), writes=["zt"])
    S.op("pool", lambda e: e.memset(wblk[:], 0.0), writes=["wblk"])
    dma("sp", rp[:], d_rpb, writes=["rp"])
    for i, dq in enumerate((d_lq1, d_lk1, d_lq2, d_lk2)):
        dma("sp", lqk[:, i, :], dq.partition_broadcast(128), writes=["lqk%d" % i])
    dma("sp", sgs[:], d_subg.rearrange("(p o) -> p o", o=1), writes=["sgs"])
    dma("pool", permb[:], d_perm, writes=["permb"])
    dma("pool", antib[:], d_anti, writes=["antib"])
    dma("pool", halfsel[:], d_halfsel, writes=["halfsel"])
    dma("pool", rowB[:], d_rowB, writes=["rowB"])
    dma("pool", colneg[:], d_colneg, writes=["colneg"])
    for g, dw in enumerate((d_wa, d_wx)):
        for par in range(2):
            for d in range(2):
                src = dw[d].rearrange("(c two) k m -> two k c m", two=2)[par]
                dst = wblk[par * 64:(par + 1) * 64, g * 8 + d * 4:g * 8 + d * 4 + 4, par * 64:(par + 1) * 64]
                dma("pool", dst, src, reads=["wblk"], writes=["wblk.%d%d%d" % (g, par, d)])
    WBK = ["wblk"] + ["wblk.%d%d%d" % (g, par, d) for g in range(2) for par in range(2) for d in range(2)]

    S.op("act", lambda e: e.activation(m8sp[:], llam, AF.Exp, scale=-1.0), reads=["pvec"], writes=["m8sp"])
    S.op("act", lambda e: e.activation(m8sp[:], m8sp[:], AF.Ln, bias=1.0), reads=["m8sp"], writes=["m8sp"])
    S.op("dve", lambda e: e.tensor_scalar(m8sp[:], m8sp[:], -8.0, None, ALU.mult), reads=["m8sp"], writes=["m8sp"])
    S.op("dve", lambda e: e.tensor_scalar(hm8[:], m8sp[:], 0.5, None, ALU.mult), reads=["m8sp"], writes=["hm8"])
    S.op("dve", lambda e: e.tensor_scalar(hba[:], lba, 0.5, None, ALU.mult), reads=["pvec"], writes=["hba"])
    S.op("dve", lambda e: e.tensor_scalar(hbx[:], lbx, 0.5, None, ALU.mult), reads=["pvec"], writes=["hbx"])
    for c_ in range(4):
        for j_ in range(4):
            S.op("dve", lambda e, c_=c_, j_=j_: e.tensor_scalar(dg[:, c_ * 4 + j_, :], ident[:], cw[:, c_, j_:j_ + 1], None, ALU.mult),
                 reads=["ident", "pvec"], writes=["dg.%d" % (c_ * 4 + j_)])

    S.op("dve", lambda e: e.tensor_tensor(lprod[:, 0, :], lqk[:, 0, :], lqk[:, 1, :], ALU.mult), reads=["lqk0", "lqk1"], writes=["lprod0"])
    S.op("dve", lambda e: e.tensor_tensor(lprod[:, 1, :], lqk[:, 2, :], lqk[:, 3, :], ALU.mult), reads=["lqk2", "lqk3"], writes=["lprod1"])
    S.op("dve", lambda e: e.reduce_sum(lsum[:], lprod[:], axis=AX.X), reads=["lprod0", "lprod1"], writes=["lsum"])
    S.op("act", lambda e: e.activation(lsum[:], lsum[:], AF.Exp), reads=["lsum"], writes=["lsum"])
    S.op("dve", lambda e: e.tensor_tensor(neglam[:], lsum[:, 1:2], lsum[:, 0:1], ALU.subtract), reads=["lsum"], writes=["neglam"])
    S.op("dve", lambda e: e.tensor_scalar(neglam[:], neglam[:], -LAM_INIT, None, ALU.add), reads=["neglam"], writes=["neglam"])
    S.op("dve", lambda e: e.tensor_scalar(sgs[:], sgs[:], 0.5 * (1.0 - LAM_INIT), None, ALU.mult), reads=["sgs"], writes=["sgs"])

    S.op("dve", lambda e: e.tensor_scalar(rpr[:], rp[:, :, ::-1], 8.0, None, ALU.mult), reads=["rp"], writes=["rpr"])
    padr_flat = bass.AP(padr_t, 0, [[192, 128], [1, 192]])
    dma("sp", padr_flat, zt[:], reads=["zt"], writes=["padr"])
    padr_mid = bass.AP(padr_t, 4 * 128 + 48, [[24 * 128, 8], [128, 15], [1, 31]])
    dma("sp", padr_mid, rpr[:], reads=["rpr", "padr"], writes=["padr.0"])
    PADK = ["padr", "padr.0"]

    def load_tb(h):
        slot = h % 2
        k = "Tb%d" % slot
        src0 = bass.AP(padr_t, h * 24 * 128, [[1, 64], [128, 24], [1, 64]])
        src1 = bass.AP(padr_t, h * 24 * 128 + 128, [[1, 64], [128, 23], [1, 64]])
        dma("pool", Traw[0:64, :, :], src0, reads=PADK + ["Traw"], writes=["Traw.a"])
        dma("pool", Traw[64:128, 0:23, :], src1, reads=PADK + ["Traw"], writes=["Traw.b"])
        cn = colneg[:].unsqueeze(1).to_broadcast([128, 8, 64])
        for n3 in range(3):
            b = sbank()
            S.op("pe", lambda e, b=b, n3=n3: e.matmul(PS[b][:], antib[:], Traw[:, n3 * 8:(n3 + 1) * 8, :], start=True, stop=True),
                 reads=["Traw", "Traw.a", "Traw.b", "antib"], writes=[pk(b)])
            S.op("dve", lambda e, b=b, n3=n3: e.tensor_tensor(Tb[slot][:, n3 * 8:(n3 + 1) * 8, :],
                                                             PS[b][:].rearrange("p (a q) -> p a q", q=64), cn, ALU.add),
                 reads=[pk(b), "colneg"], writes=[k + ".%d" % n3])
        return slot

    HKEYS = lambda tc: ["hT.%d.%d" % (c, tc) for c in range(8)]

    def proj_fm(slot, q, tc, b, ncols=128):
        def mm(e):
            for c in range(8):
                ins = e.matmul(PS[b][:], ring[slot][:, c, q * 128:(q + 1) * 128], hT[:, c, tc * 512:(tc + 1) * 512],
                               start=(c == 0), stop=(c == 7))
            return ins
        S.op("pe", mm, reads=HKEYS(tc) + rkeys(slot, (q,)), writes=[pk(b)])

    def proj_tm(slot, t, b):
        def mm(e):
            for c in range(8):
                ins = e.matmul(PS[b][:], hT[:, c, t * 128:(t + 1) * 128], ring[slot][:, c, :], start=(c == 0), stop=(c == 7))
            return ins
        S.op("pe", mm, reads=HKEYS(t // 4) + rkeys(slot), writes=[pk(b)])

    def stage_out(b, dst, extra=()):
        i = rot("kv", 3)
        S.op("act", lambda e, i=i, b=b: e.copy(kvst[i][:], PS[b][:]), reads=[pk(b)] + list(extra), writes=["kvst%d" % i])
        return dma("sp", dst, kvst[i][:], reads=["kvst%d" % i])

    out_toks = []
    KSTOP = float(os.environ.get("KSTOP", "99"))

    def ckpt(k):
        if KSTOP <= k:
            S.frozen = True

    ckpt(3)
    for tc_ in range(3):
        norm_apply(0, tc_)
    ckpt(4)
    S.alias(["QT", "KT"], ["rstd", "sq0", "sq1", "sq2", "ntmp0", "ntmp1", "xstage0", "xstage1", "fnbc"])
    S.alias(["xbT"], ["lqk0", "lqk1", "lqk2", "lqk3", "lprod0", "lprod1", "zt", "rp", "rpr", "pstg", "pstg2"])

    S.op("pool", lambda e: e.memset(VA[:, :, :, 64:65], 1.0), writes=["VA.ones"])
    for t in range(2):
        dma("pool", VA[:, 12 + t, :, 0:64], d_cnv[t * 128:(t + 1) * 128, :].rearrange("p (h d) -> p h d", d=64), writes=["VA.%d" % (12 + t)])
    dma("sp", ckst0[:], d_cnk.rearrange("(t p) f -> p t f", p=128), writes=["ckst0"])
    for t in range(2):
        b = sbank()

        def trk(e, t=t, b=b):
            for oc in range(4):
                ins = e.transpose(PS[b][:, oc * 128:(oc + 1) * 128], ckst0[:, t, oc * 128:(oc + 1) * 128], ident[:])
            return ins
        S.op("pe", trk, reads=["ckst0", "ident"], writes=[pk(b)])
        copy_op("dve", KT[:, :, 1536 + t * 128:1536 + (t + 1) * 128], PS[b][:].rearrange("p (j n) -> p j n", n=128),
                [pk(b)], ["KT.c%d" % t])

    ckpt(4.1)
    m1 = mod_units(1, d_oadaw)
    for gi, gname in enumerate(("q", "k", "v", "ga", "xb", "gb")):
        slot = wload([(d_ewin[:, gi * 512:(gi + 1) * 512], 0)])
        if gname in ("q", "k", "xb", "ga", "gb"):
            for oc in range(4):
                for tc in range(3):
                    b = sbank()
                    proj_fm(slot, oc, tc, b)
                    cs = slice(tc * 512, (tc + 1) * 512)
                    if gname == "q":
                        copy_op(evac_eng(), QT[:, oc, cs], PS[b][:], [pk(b)], ["QT.%d.%d" % (oc, tc)])
                    elif gname == "k":
                        copy_op(evac_eng(), KT[:, oc, cs], PS[b][:], [pk(b)], ["KT.%d.%d" % (oc, tc)])
                    elif gname == "xb":
                        copy_op(evac_eng(), xbT[:, oc, cs], PS[b][:], [pk(b)], ["xbT.%d.%d" % (oc, tc)])
                    else:
                        cc = oc if gname == "ga" else 4 + oc
                        S.op("act", lambda e, cc=cc, cs=cs, b=b: e.activation(oT[:, cc, cs], PS[b][:], AF.Silu),
                             reads=[pk(b)], writes=["oT.%d.%d" % (cc, tc)])
        if gname == "k":
            for t in range(8, 12):
                b = sbank()
                proj_tm(slot, t, b)
                out_toks.append(stage_out(b, o_nak[(t - 8) * 128:(t - 7) * 128, :]))
        if gname == "v":
            for t in range(12):
                b = sbank()
                proj_tm(slot, t, b)
                if os.environ.get("KVAR", "") != "A":
                    S.op("dve", lambda e, t=t, b=b: e.tensor_copy(VA[:, t, :, 0:64], PS[b][:].rearrange("p (h d) -> p h d", d=64)),
                         reads=[pk(b)], writes=["VA.%d" % t])
                if t >= 8:
                    out_toks.append(stage_out(b, o_nav[(t - 8) * 128:(t - 7) * 128, :], extra=["VA.%d" % t]))
        m1[gi]()
        ckpt(4.2 + 0.1 * gi)
    mod_finish(1)

    ckpt(5)
    S.alias(["la", "lu", "lhf", "lhb"], ["hT"])
    S.alias(["pbuf0", "pbuf1", "pbuf2", "pbuf3", "rden0", "rden1", "g20", "qm0", "qm1", "qmz", "rdb0", "rdb1"], ["kvst0", "kvst1", "kvst2"])

    SEGS = [(0, 1024), (1024, 1280), (1280, 1536)]
    CWK = ["pvec"]

    def lru_chunk(c):
        XB = ["xbT.%d.%d" % (c, tc) for tc in range(3)]
        DG = ["dg.%d" % (c * 4 + j) for j in range(4)]
        for tc in range(3):
            c0 = tc * 512
            pieces = [(c0, c0 + 512, 0, 1024)] if tc < 2 else [(1024, 1280, 1024, 1280), (1280, 1536, 1280, 1536)]
            b = sbank()

            def cv(e, pieces=pieces, c0=c0, b=b):
                ins = None
                for (a, bb, lo, hi) in pieces:
                    e.matmul(PS[b][:, a - c0:bb - c0], dg[:, c * 4 + 2, :], xbT[:, c, a:bb], start=True, stop=False)
                    taps = ((0, -2), (1, -1), (3, 1))
                    for ti, (j, off) in enumerate(taps):
                        a2 = max(a, lo - off)
                        b2 = min(bb, hi - off)
                        ins = e.matmul(PS[b][:, a2 - c0:b2 - c0], dg[:, c * 4 + j, :], xbT[:, c, a2 + off:b2 + off],
                                       start=False, stop=(ti == 2))
                return ins
            S.op("pe", cv, reads=XB + DG, writes=[pk(b)])
            S.op("act", lambda e, b=b, c0=c0: e.activation(xcb[:, c0:c0 + 512], PS[b][:], AF.Identity, bias=cb[:, c:c + 1]),
                 reads=[pk(b), "pvec"], writes=["xcb"])
            yield
        for d in range(2):
            for tc in range(3):
                cs = slice(tc * 512, (tc + 1) * 512)
                b1 = sbank()
                S.op("pe", lambda e, b1=b1, cs=cs, d=d: e.matmul(PS[b1][:], wblk[:, 0 * 8 + d * 4 + c, :], xcb[:, cs], start=True, stop=True),
                     reads=["xcb"] + WBK, writes=[pk(b1)])
                S.op("act", lambda e, b1=b1, cs=cs, d=d: e.activation(la[:, cs], PS[b1][:], AF.Tanh, bias=hba[:, d, c:c + 1], scale=0.5),
                     reads=[pk(b1), "hba"], writes=["la"])
                yield
                b2 = sbank()
                S.op("pe", lambda e, b2=b2, cs=cs, d=d: e.matmul(PS[b2][:], wblk[:, 1 * 8 + d * 4 + c, :], xcb[:, cs], start=True, stop=True),
                     reads=["xcb"] + WBK, writes=[pk(b2)])
                S.op("act", lambda e, b2=b2, cs=cs, d=d: e.activation(lu[:, cs], PS[b2][:], AF.Tanh, bias=hbx[:, d, c:c + 1], scale=0.5),
                     reads=[pk(b2), "hbx"], writes=["lu"])
                yield
            dst = lhf if d == 0 else lhb
            dk = "lhf" if d == 0 else "lhb"
            S.op("act", lambda e, d=d: e.activation(la[:], la[:], AF.Exp, scale=hm8[:, d, c:c + 1], bias=hm8[:, d, c:c + 1]),
                 reads=["la", "hm8"], writes=["la"])
            yield
            S.op("dve", lambda e: e.scalar_tensor_tensor(lu[:], lu[:], 1.0, xcb[:], ALU.add, ALU.mult), reads=["lu", "xcb"], writes=["lu"])
            yield
            S.op("act", lambda e, dst=dst: e.activation(dst[:], la[:], AF.Square), reads=["la"], writes=[dk])
            yield
            S.op("act", lambda e, dst=dst: e.activation(dst[:], dst[:], AF.Sqrt, scale=-0.25, bias=0.25), reads=[dk], writes=[dk])
            yield
            S.op("dve", lambda e, dst=dst: e.tensor_tensor(lu[:], lu[:], dst[:], ALU.mult), reads=["lu", dk], writes=["lu"])
            yield
            for si, (lo, hi) in enumerate(SEGS):
                init = h0[:, d, c:c + 1] if si == 0 else 0.0
                if d == 0:
                    S.op("dve", lambda e, lo=lo, hi=hi, init=init, dst=dst: e.tensor_tensor_scan(
                        dst[:, lo:hi], la[:, lo:hi], lu[:, lo:hi], init, ALU.mult, ALU.add),
                        reads=["la", "lu", "pvec", dk], writes=[dk])
                    yield
                else:
                    S.op("dve", lambda e, lo=lo, hi=hi, init=init, dst=dst: e.tensor_tensor_scan(
                        dst[:, lo:hi][:, ::-1], la[:, lo:hi][:, ::-1], lu[:, lo:hi][:, ::-1], init, ALU.mult, ALU.add),
                        reads=["la", "lu", "pvec", dk], writes=[dk])
                    yield
            for s in range(2):
                lo, hi = SEGS[1 + s]
                col = hi - 1 if d == 0 else lo
                S.op("pool", lambda e, s=s, d=d, col=col, dst=dst: e.tensor_copy(lst[:, (s * 2 + d) * 4 + c:(s * 2 + d) * 4 + c + 1], dst[:, col:col + 1]),
                     reads=[dk], writes=["lst.%d%d%d" % (c, s, d)])
            yield
        S.op("pool", lambda e: e.tensor_tensor(lhf[:], lhf[:], lhb[:], ALU.add), reads=["lhf", "lhb"], writes=["lhf"])
        yield
        OK_ = ["oT.%d.%d" % (4 + c, tc) for tc in range(3)]
        S.op("pool", lambda e: e.tensor_tensor(oT[:, 4 + c, :], oT[:, 4 + c, :], lhf[:], ALU.mult), reads=["lhf"] + OK_, writes=OK_)
        yield

    def lru_all():
        for c in range(4):
            yield from lru_chunk(c)
    lru_it = lru_all()

    def pump():
        try:
            next(lru_it)
        except StopIteration:
            pass

    from collections import deque
    LQ = deque()

    def attn64(h, segs, tc, na=None):
        ch, pb = h // 2, (h % 2) * 64
        ob = 4 + rot("o", 2)
        qi = h % 2
        N = sum(q_.stop - q_.start for (q_, _, _) in segs)
        qcols = slice(segs[0][0].start, segs[-1][0].stop)
        S.op("pool", lambda e: e.tensor_copy(qm[qi][pb:pb + 64, 0:N], QT[pb:pb + 64, ch, qcols]),
             reads=["QT.%d.%d" % (ch, tc), "qmz"], writes=["qm%d" % qi])
        flat = []
        for (q_, off, tiles) in segs:
            for i, t_ in enumerate(tiles):
                flat.append((off, q_.stop - q_.start, t_, i == 0, i == len(tiles) - 1))
        n = len(flat)
        sb_ = [None] * n

        def s_mm(i):
            off, Ns, (kc0, vt, j), _, _ = flat[i]
            b = sbank()
            sb_[i] = b
            kkey = "KT.c%d" % ((kc0 - 1536) // 128) if kc0 >= 1536 else "KT.%d.%d" % (ch, kc0 // 512)

            def mm(e):
                last = (na is None) or (j is None)
                ins = e.matmul(PS[b][:, 0:Ns], KT[:, ch, kc0:kc0 + 128], qm[qi][:, off:off + Ns], start=True, stop=last)
                if not last:
                    tbs, qc = na
                    ai0 = 2 * j - 8 * qc + 11
                    rhs = Tb[tbs][:, ai0 - 7:ai0 + 1, :][:, ::-1, :]
                    e.matmul(PS[b][:, 0:Ns], identb[:], rhs, start=False, stop=False)
                    pidx = qc * 6 + NA_J[qc].index(j)
                    rb = rowB[:, pidx, :].unsqueeze(2).to_broadcast([128, 8, 64])
                    ins = e.matmul(PS[b][:, 0:Ns], halfsel[:], rb, start=False, stop=True)
                return ins
            rd = ["qm%d" % qi, kkey]
            if na is not None and j is not None:
                rd += ["Tb%d.%d" % (na[0], n3) for n3 in range(3)] + ["identb", "halfsel", "rowB"]
            S.op("pe", mm, reads=rd, writes=[pk(b)])

        s_mm(0)
        if n > 1:
            s_mm(1)
        for i in range(n):
            off, Ns, (kc0, vt, j), st, sp_ = flat[i]
            b = sb_[i]
            pi = rot("pb", 3)
            S.op("act", lambda e, b=b, pi=pi, Ns=Ns: e.activation(pbuf[pi][:, 0:Ns], PS[b][:, 0:Ns], AF.Exp, scale=0.125),
                 reads=[pk(b)], writes=["pbuf%d" % pi])
            if i + 1 < n:
                pump()
                pump()
                if LQ and i >= 1:
                    LQ.popleft()()
                if i + 2 < n:
                    s_mm(i + 2)
            S.op("pe", lambda e, pi=pi, vt=vt, st=st, sp_=sp_, off=off, Ns=Ns: e.matmul(
                PS[ob][0:65, off:off + Ns], VA[:, vt, h, 0:65], pbuf[pi][:, 0:Ns], start=st, stop=sp_),
                reads=["pbuf%d" % pi, "VA.%d" % vt, "VA.ones"], writes=[pk(ob)])
        ri = ob - 4
        S.op("dve", lambda e: e.reciprocal(rdb[ri][64:65, 0:N], PS[ob][64:65, 0:N]), reads=[pk(ob)], writes=["rdb%d" % ri])
        bb = 6 + (ob - 4)
        ok = "oT.%d.%d" % (ch, tc)

        def post():
            S.op("pe", lambda e: e.matmul(PS[bb][0:64, 0:N], onesb[64:65, 0:64], rdb[ri][64:65, 0:N], start=True, stop=True),
                 reads=["rdb%d" % ri, "onesb"], writes=[pk(bb)])
            S.op("dve", lambda e: e.tensor_tensor(g2[0][pb:pb + 64, 0:N], oT[pb:pb + 64, ch, qcols], PS[bb][0:64, 0:N], ALU.mult),
                 reads=[ok, pk(bb)], writes=["g20"])
            S.op("dve", lambda e: e.tensor_tensor(oT[pb:pb + 64, ch, qcols], g2[0][pb:pb + 64, 0:N], PS[ob][0:64, 0:N], ALU.mult),
                 reads=["g20", pk(ob), ok], writes=[ok])
        LQ.append(post)

    S.alias(["Traw"], ["ckst0"])
    S.op("pool", lambda e: e.memset(Traw[:], 0.0), writes=["Traw"])
    S.op("pool", lambda e: e.memset(qm[0][:], 0.0), writes=["qmz", "qm0"])
    S.op("pool", lambda e: e.memset(qm[1][:], 0.0), writes=["qmz", "qm1"])
    load_tb(0)
    for h in range(8):
        tbs = h % 2
        if h + 1 < 8:
            load_tb(h + 1)
        psegs = []
        for s in range(2):
            q0 = 1024 + s * 256
            psegs.append((slice(q0, q0 + 256), s * 256, [(q0 + t * 128, 8 + 2 * s + t, None) for t in range(2)]))
        attn64(h, psegs, 2)
        for qc in range(2):
            tiles = [(j * 128, j, j) for j in NA_J[qc]] + [(1536 + t * 128, 12 + t, None) for t in range(2)]
            attn64(h, [(slice(qc * 512, (qc + 1) * 512), 0, tiles)], qc, na=(tbs, qc))
    while LQ:
        LQ.popleft()()
    for _ in range(400):
        pump()
    bl = sbank()
    S.op("pe", lambda e: e.transpose(PS[bl][0:16, 0:128], lst[:, 0:16], ident[:]),
         reads=["lst.%d%d%d" % (c, s, d) for c in range(4) for s in range(2) for d in range(2)] + ["ident"], writes=[pk(bl)])
    S.op("dve", lambda e: e.tensor_copy(lstT[:], PS[bl][0:16, 0:128]), reads=[pk(bl)], writes=["lstT"])
    out_toks.append(dma("sp", o_lru.rearrange("s d (c p) -> (s d c) p", p=128), lstT[:], reads=["lstT"]))

    def wout_phase(L, d_wout, after_tc=None):
        slots = [wload([(d_wout[:, g * 512:(g + 1) * 512], 0)]) for g in range(2)]
        for tc in range(3):
            s = 0 if tc < 2 else 1
            for oc in range(8):
                slot, q = slots[oc // 4], oc % 4
                b = sbank()

                def mm(e, slot=slot, q=q, tc=tc, b=b):
                    for c in range(8):
                        ins = e.matmul(PS[b][:], ring[slot][:, c, q * 128:(q + 1) * 128], oT[:, c, tc * 512:(tc + 1) * 512],
                                       start=(c == 0), stop=(c == 7))
                    return ins
                S.op("pe", mm, reads=["oT.%d.%d" % (c, tc) for c in range(8)] + rkeys(slot, (q,)), writes=[pk(b)])
                xs_ = xT[:, oc, tc * 512:(tc + 1) * 512]
                S.op("dve", lambda e, xs_=xs_, b=b, oc=oc, s=s: e.scalar_tensor_tensor(
                    xs_, PS[b][:], mT[L][:, 16 + oc, s:s + 1], xs_, ALU.mult, ALU.add),
                    reads=[pk(b), "mT%d" % L, "xT.%d.%d" % (oc, tc)], writes=["xT.%d.%d" % (oc, tc)])
            if after_tc is not None:
                after_tc(tc)

    ckpt(6)
    S.alias(["hT"], ["la", "lu", "lhf", "lhb"])
    S.alias(["rstd", "sq0", "sq1", "sq2", "ntmp0", "ntmp1"], ["QT", "KT"])
    S.alias(["kvst0", "kvst1", "kvst2"], ["pbuf0", "pbuf1", "pbuf2", "pbuf3", "rden0", "rden1", "g20", "qm0", "qm1", "qmz", "rdb0", "rdb1"])
    wout_phase(0, d_ewout, after_tc=lambda tc: norm_phase(1, (tc,)))
    ckpt(8)

    S.alias(["V1"], ["VA", "Tb0", "Tb1", "xcb", "ckst0", "Traw"])
    S.alias(["QTh0", "QTh1", "KTh0", "KTh1", "KTc", "ptmp0", "ptmp1", "ptmp2", "ptmp3", "ptmp4", "ckst1"],
            ["rstd", "sq0", "sq1", "sq2", "ntmp0", "ntmp1", "QT", "KT"])
    S.alias(["ropeC", "ropeS", "rawb0", "rawb1"], ["xbT"])
    dma("sp", ropeC[:], d_ropeC, writes=["ropeC"])
    dma("sp", ropeS[:], d_ropeS, writes=["ropeS"])
    dma("pool", V1[:, 12:14, :], d_cdv.rearrange("(t p) f -> p t f", p=128), writes=["V1.12", "V1.13"])
    dma("sp", ckst1[:], d_cdk.rearrange("(t p) f -> p t f", p=128), writes=["ckst1"])
    for t in range(2):
        for hh in range(2):
            b = sbank()

            def trk1(e, t=t, hh=hh, b=b):
                for j in range(4):
                    h = hh * 4 + j
                    ins = e.transpose(PS[b][:, j * 128:(j + 1) * 128], ckst1[:, t, h * 128:(h + 1) * 128], ident[:])
                return ins
            S.op("pe", trk1, reads=["ckst1", "ident"], writes=[pk(b)])
            copy_op(evac_eng(), KTc[:, hh * 4:hh * 4 + 4, t * 128:(t + 1) * 128], PS[b][:].rearrange("p (j n) -> p j n", n=128),
                    [pk(b)], ["KTc.%d%d" % (t, hh)])
    KTCK = ["KTc.%d%d" % (t, hh) for t in range(2) for hh in range(2)]

    for g in range(2):
        slot = wload([(d_owin[:, 2048 + g * 512:2048 + (g + 1) * 512], 0)])
        for t in range(12):
            b = sbank()
            proj_tm(slot, t, b)
            S.op("dve", lambda e, t=t, b=b, g=g: e.tensor_copy(V1[:, t, g * 512:(g + 1) * 512], PS[b][:]),
                 reads=[pk(b)], writes=["V1.%d.%d" % (t, g)])
            if t >= 8:
                out_toks.append(stage_out(b, o_dfv[(t - 8) * 128:(t - 7) * 128, g * 512:(g + 1) * 512], extra=["V1.%d.%d" % (t, g)]))
    S.alias(["ptmp0", "ptmp1", "ptmp2", "ptmp3", "ptmp4"], ["ckst1"])

    def rope_evac(b, dst, dkey, tc):
        ri = rot("rb", 2)
        S.op("act", lambda e: e.copy(rawb[ri][:], PS[b][:]), reads=[pk(b)], writes=["rawb%d" % ri])
        b2 = sbank()
        S.op("pe", lambda e: e.matmul(PS[b2][:], permb[:], rawb[ri][:], start=True, stop=True),
             reads=["rawb%d" % ri, "permb"], writes=[pk(b2)])
        p1 = rot("pt", 3)
        S.op("dve", lambda e: e.tensor_tensor(ptmp[p1][:], PS[b][:], ropeC[:, tc * 512:(tc + 1) * 512], ALU.mult),
             reads=[pk(b), "ropeC"], writes=["ptmp%d" % p1])
        p2 = rot("pt", 3)
        S.op("dve", lambda e: e.tensor_tensor(ptmp[p2][:], PS[b2][:], ropeS[:, tc * 512:(tc + 1) * 512], ALU.mult),
             reads=[pk(b2), "ropeS"], writes=["ptmp%d" % p2])
        S.op("pool", lambda e: e.tensor_tensor(dst, ptmp[p1][:], ptmp[p2][:], ALU.add),
             reads=["ptmp%d" % p1, "ptmp%d" % p2], writes=[dkey])

    from collections import deque
    PQ = deque()
    TQ = deque()

    def tick(first=False):
        if TQ:
            TQ.popleft()()
        if first:
            for _ in range(3):
                if PQ:
                    PQ.popleft()()

    def drain(q):
        while q:
            q.popleft()()

    def diff_attn(h, hs, segs, tc):
        qk = "QTh%d.%d" % (hs, tc)
        flat = []
        for (qc_, off, tiles) in segs:
            for i, t_ in enumerate(tiles):
                flat.append((qc_, off, qc_.stop - qc_.start, t_, i == 0, i == len(tiles) - 1))
        n = len(flat)
        sb1 = [None] * n
        sb2 = [None] * n

        def s_mm(i):
            qc_, off, Ns, (kfn, kkey, vt), _, _ = flat[i]
            b1 = sbank()
            b2 = sbank()
            sb1[i], sb2[i] = b1, b2

            def mm(e):
                e.matmul(PS[b1][:, 0:Ns], kfn(slice(0, 64)), QTh[hs][0:64, qc_], start=True, stop=True)
                return e.matmul(PS[b2][:, 0:Ns], kfn(slice(64, 128)), QTh[hs][64:128, qc_], start=True, stop=True)
            S.op("pe", mm, reads=[qk] + kkey, writes=[pk(b1), pk(b2)])

        s_mm(0)
        for i in range(n):
            qc_, off, Ns, (kfn, kkey, vt), st, sp_ = flat[i]
            b1, b2 = sb1[i], sb2[i]
            p1 = rot("pb", 4)
            S.op("act", lambda e, b1=b1, p1=p1, Ns=Ns: e.activation(pbuf[p1][:, 0:Ns], PS[b1][:, 0:Ns], AF.Exp, scale=0.125),
                 reads=[pk(b1)], writes=["pbuf%d" % p1])
            p2 = rot("pb", 4)
            S.op("act", lambda e, b2=b2, p2=p2, Ns=Ns: e.activation(pbuf[p2][:, 0:Ns], PS[b2][:, 0:Ns], AF.Exp, scale=0.125),
                 reads=[pk(b2)], writes=["pbuf%d" % p2])
            if i + 1 < n:
                tick(i == 0)
                s_mm(i + 1)
            vk = ["V1.%d" % vt] if vt >= 12 else ["V1.%d.%d" % (vt, h // 4)]

            def pv(e, p1=p1, p2=p2, vt=vt, st=st, sp_=sp_, off=off, Ns=Ns):
                vv = V1[:, vt, h * 128:(h + 1) * 128]
                e.matmul(PS[4][:, off:off + Ns], vv, pbuf[p1][:, 0:Ns], start=st, stop=sp_)
                e.matmul(PS[5][:, off:off + Ns], vv, pbuf[p2][:, 0:Ns], start=st, stop=sp_)
                e.matmul(PS[6][:, off:off + Ns], onesb[:], pbuf[p1][:, 0:Ns], start=st, stop=sp_)
                return e.matmul(PS[7][:, off:off + Ns], onesb[:], pbuf[p2][:, 0:Ns], start=st, stop=sp_)
            S.op("pe", pv, reads=["pbuf%d" % p1, "pbuf%d" % p2, "onesb"] + vk, writes=[pk(4), pk(5), pk(6), pk(7)])
        N = sum(q_.stop - q_.start for (q_, _, _) in segs)
        qcols = slice(segs[0][0].start, segs[-1][0].stop)
        r1 = rden[0][:, 0:N]
        r2 = rden[1][:, 0:N]
        a1 = ptmp[3][:, 0:N]
        a2 = ptmp[4][:, 0:N]
        drain(TQ)
        S.op("act", lambda e: e.copy(a1, PS[4][:, 0:N]), reads=[pk(4)], writes=["ptmp3"])
        S.op("dve", lambda e: e.tensor_copy(r1, PS[6][:, 0:N]), reads=[pk(6)], writes=["rden0"])
        S.op("act", lambda e: e.copy(a2, PS[5][:, 0:N]), reads=[pk(5)], writes=["ptmp4"])
        S.op("dve", lambda e: e.tensor_copy(r2, PS[7][:, 0:N]), reads=[pk(7)], writes=["rden1"])
        ok = "oT.%d.%d" % (h, tc)

        def stB():
            S.op("dve", lambda e: e.reciprocal(r2, r2), reads=["rden1"], writes=["rden1"])
            S.op("dve", lambda e: e.scalar_tensor_tensor(r2, r1, neglam[:, 0:1], r2, ALU.mult, ALU.mult),
                 reads=["rden0", "rden1", "neglam"], writes=["rden1"])
            S.op("dve", lambda e: e.tensor_tensor(a2, a2, r2, ALU.mult), reads=["ptmp4", "rden1"], writes=["ptmp4"])
            S.op("dve", lambda e: e.tensor_tensor(a1, a1, a2, ALU.add), reads=["ptmp3", "ptmp4"], writes=["ptmp3"])

        def stC():
            pi = rot("pb", 4)
            S.op("act", lambda e: e.activation(pbuf[pi][:, 0:N], a1, AF.Square), reads=["ptmp3"], writes=["pbuf%d" % pi])
            bs = sbank()
            S.op("pe", lambda e: e.matmul(PS[bs][:, 0:N], onesb[:], pbuf[pi][:, 0:N], start=True, stop=True),
                 reads=["pbuf%d" % pi, "onesb"], writes=[pk(bs)])
            S.op("dve", lambda e: e.scalar_tensor_tensor(a2, r1, EPS, r1, ALU.mult, ALU.mult), reads=["rden0", "ptmp4"], writes=["ptmp4"])
            S.op("dve", lambda e: e.scalar_tensor_tensor(r1, PS[bs][:, 0:N], 1.0 / 128.0, a2, ALU.mult, ALU.add),
                 reads=[pk(bs), "ptmp4", "rden0"], writes=["rden0"])

        I32 = mybir.dt.int32

        def stD():
            S.op("dve", lambda e: e.tensor_scalar(a2.bitcast(I32), r1.bitcast(I32), 1, None, ALU.arith_shift_right),
                 reads=["rden0", "ptmp4"], writes=["ptmp4"])
            S.op("dve", lambda e: e.tensor_scalar(r2.bitcast(I32), a2.bitcast(I32), -1.0, float(0x5f3759df), ALU.mult, ALU.add),
                 reads=["ptmp4", "rden1"], writes=["rden1"])
            for _ in range(2):
                S.op("dve", lambda e: e.tensor_tensor(a2, r2, r2, ALU.mult), reads=["rden1", "ptmp4"], writes=["ptmp4"])
                S.op("dve", lambda e: e.scalar_tensor_tensor(a2, a2, -0.5, r1, ALU.mult, ALU.mult), reads=["ptmp4", "rden0"], writes=["ptmp4"])
                S.op("dve", lambda e: e.scalar_tensor_tensor(r2, a2, 1.5, r2, ALU.add, ALU.mult), reads=["ptmp4", "rden1"], writes=["rden1"])
            S.op("dve", lambda e: e.scalar_tensor_tensor(a1, a1, sgs[:, 0:1], r2, ALU.mult, ALU.mult),
                 reads=["ptmp3", "sgs", "rden1"], writes=["ptmp3"])
            S.op("pool", lambda e: e.tensor_tensor(oT[:, h, qcols], oT[:, h, qcols], a1, ALU.mult), reads=["ptmp3", ok], writes=[ok])
        TQ.append(stB)
        TQ.append(stC)
        TQ.append(stD)

    def proj_units(h):
        hs = h % 2
        slot = wload([(d_owin[:, h * 128:(h + 1) * 128], 0), (d_owin[:, 1024 + h * 128:1024 + (h + 1) * 128], 1),
                      (d_owin[:, 3072 + h * 128:3072 + (h + 1) * 128], 2)])
        units = []
        for tc in range(3):
            cs = slice(tc * 512, (tc + 1) * 512)
            for q, (dstT, nm) in enumerate(((QTh[hs], "QTh%d" % hs), (KTh[hs], "KTh%d" % hs))):
                def u(q=q, dstT=dstT, nm=nm, tc=tc, cs=cs):
                    b = sbank()
                    proj_fm(slot, q, tc, b)
                    if tc < 2:
                        rope_evac(b, dstT[:, cs], "%s.%d" % (nm, tc), tc)
                    else:
                        copy_op("dve", dstT[:, cs], PS[b][:], [pk(b)], ["%s.%d" % (nm, tc)])
                units.append(u)

            def ug(tc=tc, cs=cs):
                b = sbank()
                proj_fm(slot, 2, tc, b)
                p1 = rot("pt", 3)
                S.op("act", lambda e: e.activation(ptmp[p1][:], PS[b][:], AF.Tanh, scale=0.5), reads=[pk(b)], writes=["ptmp%d" % p1])
                S.op("dve", lambda e: e.scalar_tensor_tensor(oT[:, h, cs], ptmp[p1][:], 1.0, PS[b][:], ALU.add, ALU.mult),
                     reads=[pk(b), "ptmp%d" % p1], writes=["oT.%d.%d" % (h, tc)])
            units.append(ug)
        return units

    kslots = [wload([(d_owin[:, 1024 + g * 512:1024 + (g + 1) * 512], 0)]) for g in range(2)]
    PQ.extend(proj_units(0))
    for g in range(2):
        for t in range(8, 12):
            b = sbank()
            proj_tm(kslots[g], t, b)
            out_toks.append(stage_out(b, o_dfk[(t - 8) * 128:(t - 7) * 128, g * 512:(g + 1) * 512]))
            if PQ:
                PQ.popleft()()
    drain(PQ)
    ckpt(9)
    S.alias(["pbuf0", "pbuf1", "pbuf2", "pbuf3", "rden0", "rden1", "g20", "qm0", "qm1", "qmz", "rdb0", "rdb1"], ["kvst0", "kvst1", "kvst2"])
    for h in range(8):
        hs = h % 2
        if h + 1 < 8:
            PQ.extend(proj_units(h + 1))
        psegs = []
        for s in range(2):
            q0 = 1024 + s * 256
            tiles = [((lambda ps_, c0=q0 + t * 128, hs=hs: KTh[hs][ps_, c0:c0 + 128]), ["KTh%d.2" % hs], 8 + 2 * s + t) for t in range(2)]
            psegs.append((slice(q0, q0 + 256), s * 256, tiles))
        diff_attn(h, hs, psegs, 2)
        for qc in range(2):
            tiles = [((lambda ps_, t=t, h=h: KTc[ps_, h, t * 128:(t + 1) * 128]), KTCK, 12 + t) for t in range(2)]
            tiles += [((lambda ps_, j=j, hs=hs: KTh[hs][ps_, j * 128:(j + 1) * 128]), ["KTh%d.%d" % (hs, j // 4)], j) for j in range(8)]
            diff_attn(h, hs, [(slice(qc * 512, (qc + 1) * 512), 0, tiles)], qc)
        drain(PQ)
        ckpt(9.1 + 0.1 * h)
    drain(TQ)

    ckpt(10)
    S.alias(["xstage0", "xstage1", "fnbc"], ["QTh0", "QTh1", "KTh0", "KTh1", "KTc", "ptmp0", "ptmp1", "ptmp2", "ptmp3", "ptmp4", "ckst1"])
    dma("sp", fnbc[:], d_fnorm.partition_broadcast(128), writes=["fnbc"])

    def final_tc(tc):
        for t in range(4 * tc, 4 * tc + 4):
            st = t % 2
            bA, bB = sbank(), sbank()
            for half, b in ((0, bA), (1, bB)):
                def trf(e, half=half, b=b, t=t):
                    for j in range(4):
                        c = half * 4 + j
                        ins = e.transpose(PS[b][:, j * 128:(j + 1) * 128], xT[:, c, t * 128:(t + 1) * 128], ident[:])
                    return ins
                S.op("pe", trf, reads=["xT.%d.%d" % (half * 4 + j, tc) for j in range(4)] + ["ident"], writes=[pk(b)])
            sk = "sstat%d" % st
            so = st * 4
            S.op("act", lambda e, bA=bA, st=st, so=so: e.activation(xstage[st][:, 0:512], PS[bA][:], AF.Square, accum_out=sstat[:, so:so + 1]),
                 reads=[pk(bA)], writes=["xstage%d" % st, sk + ".0"])
            S.op("act", lambda e, bB=bB, st=st, so=so: e.activation(xstage[st][:, 512:1024], PS[bB][:], AF.Square, accum_out=sstat[:, so + 1:so + 2]),
                 reads=[pk(bB), "xstage%d" % st], writes=["xstage%d" % st, sk + ".1"])
            S.op("dve", lambda e, so=so: e.tensor_tensor(sstat[:, so + 2:so + 3], sstat[:, so:so + 1], sstat[:, so + 1:so + 2], ALU.add),
                 reads=[sk + ".0", sk + ".1"], writes=[sk + ".2"])
            S.op("dve", lambda e, so=so: e.tensor_scalar(sstat[:, so + 2:so + 3], sstat[:, so + 2:so + 3], 1.0 / 1024.0, EPS, ALU.mult, ALU.add),
                 reads=[sk + ".2"], writes=[sk + ".2"])
            S.op("act", lambda e, so=so: e.activation(sstat[:, so + 2:so + 3], sstat[:, so + 2:so + 3], AF.Sqrt), reads=[sk + ".2"], writes=[sk + ".2"])
            S.op("dve", lambda e, so=so: e.reciprocal(sstat[:, so + 3:so + 4], sstat[:, so + 2:so + 3]), reads=[sk + ".2"], writes=[sk + ".3"])
            S.op("dve", lambda e, bA=bA, st=st, so=so: e.scalar_tensor_tensor(xstage[st][:, 0:512], PS[bA][:], sstat[:, so + 3:so + 4], fnbc[:, 0:512], ALU.mult, ALU.mult),
                 reads=[pk(bA), sk + ".3", "fnbc", "xstage%d" % st], writes=["xstage%d" % st])
            S.op("dve", lambda e, bB=bB, st=st, so=so: e.scalar_tensor_tensor(xstage[st][:, 512:1024], PS[bB][:], sstat[:, so + 3:so + 4], fnbc[:, 512:1024], ALU.mult, ALU.mult),
                 reads=[pk(bB), sk + ".3", "fnbc", "xstage%d" % st], writes=["xstage%d" % st])
            dst = o_ys[t * 128:(t + 1) * 128, :] if t < 8 else o_yp[(t - 8) * 128:(t - 7) * 128, :]
            out_toks.append(dma("sp", dst, xstage[st][:], reads=["xstage%d" % st]))

    wout_phase(1, d_owout, after_tc=final_tc)
    ckpt(11)

    S.wait_all("sp", out_toks)
    with nc.allow_non_contiguous_dma(reason="small strided parameter / state vectors"), \
            nc.allow_low_precision(reason="bf16 copies of fp32-computed values that feed bf16 matmul operands"):
        S.emit(nc, sems, dsems)
    es.close()
    return nc


_CONST = {}


def _consts():
    if not _CONST:
        C, Sg, perm = _rope_tables()
        colneg, rowB, halfsel, anti = _na_tables()
        _CONST.update(dict(c_ident=np.eye(128, dtype=np.float32), c_ropeC=C, c_ropeS=Sg, c_perm=perm,
                           c_colneg=colneg, c_rowB=rowB, c_halfsel=halfsel, c_anti=anti))
    return _CONST


def kernel(x_prompt, x_sample, c, cache_na_k, cache_na_v, state_lru, cache_diff_k, cache_diff_v, c_ctx,
           e_norm, e_ada_w, e_ada_b, e_w_in, e_rpb, e_conv_w, e_conv_b, e_lru_wa, e_lru_ba, e_lru_wx, e_lru_bx,
           e_lru_lam, e_w_out, o_norm, o_ada_w, o_ada_b, o_w_in, o_lq1, o_lk1, o_lq2, o_lk2, o_sub_g, o_w_out,
           final_norm):
    f = lambda a: np.ascontiguousarray(np.asarray(a, dtype=np.float32))
    x_prompt, x_sample, c = f(x_prompt), f(x_sample), f(c)
    shared = dict(
        e_norm=f(e_norm)[0], e_ada_w=f(e_ada_w)[0], e_ada_b=f(e_ada_b)[0], e_w_in=f(e_w_in)[0], e_rpb=f(e_rpb)[0],
        e_conv_w=f(e_conv_w)[0], e_conv_b=f(e_conv_b)[0], e_lru_wa=f(e_lru_wa)[0], e_lru_ba=f(e_lru_ba)[0],
        e_lru_wx=f(e_lru_wx)[0], e_lru_bx=f(e_lru_bx)[0], e_lru_lam=f(e_lru_lam)[0], e_w_out=f(e_w_out)[0],
        o_norm=f(o_norm)[0], o_ada_w=f(o_ada_w)[0], o_ada_b=f(o_ada_b)[0], o_w_in=f(o_w_in)[0],
        o_lq1=f(o_lq1)[0], o_lk1=f(o_lk1)[0], o_lq2=f(o_lq2)[0], o_lk2=f(o_lk2)[0], o_sub_g=f(o_sub_g)[0],
        o_w_out=f(o_w_out)[0], final_norm=f(final_norm))
    shared = {k: np.ascontiguousarray(v) for k, v in shared.items()}
    shared.update(_consts())
    cna_k, cna_v, slru = f(cache_na_k), f(cache_na_v), f(state_lru)
    cdk, cdv, cctx = f(cache_diff_k), f(cache_diff_v), f(c_ctx)
    in_maps = []
    for i in range(NCORES):
        m = dict(shared)
        m["xs"] = x_sample[i]
        m["xp"] = np.ascontiguousarray(x_prompt[2 * i:2 * i + 2].reshape(512, 1024))
        m["cvec"] = np.ascontiguousarray(np.stack([c[i], cctx], axis=0))
        m["cnk"] = np.ascontiguousarray(cna_k[i, 0].reshape(256, 512))
        m["cnv"] = np.ascontiguousarray(cna_v[i, 0].reshape(256, 512))
        m["slru"] = np.ascontiguousarray(slru[i, 0])
        m["cdk"] = np.ascontiguousarray(cdk[i, 0].reshape(256, 1024))
        m["cdv"] = np.ascontiguousarray(cdv[i, 0].reshape(256, 1024))
        in_maps.append(m)
    nc = build_nc()
    res = run_bass_kernel_spmd(nc, in_maps, core_ids=list(range(NCORES)))
    R = res.results
    y_prompt = np.concatenate([R[i]["y_p"].reshape(2, 256, 1024) for i in range(NCORES)], axis=0)
    y_sample = np.stack([R[i]["y_s"] for i in range(NCORES)], axis=0)
    na_k = np.concatenate([R[i]["na_k"].reshape(2, 1, 256, 8, 64) for i in range(NCORES)], axis=0)
    na_v = np.concatenate([R[i]["na_v"].reshape(2, 1, 256, 8, 64) for i in range(NCORES)], axis=0)
    lru = np.concatenate([R[i]["lru_o"].reshape(2, 1, 2, 512) for i in range(NCORES)], axis=0)
    df_k = np.concatenate([R[i]["df_k"].reshape(2, 1, 256, 8, 128) for i in range(NCORES)], axis=0)
    df_v = np.concatenate([R[i]["df_v"].reshape(2, 1, 256, 8, 128) for i in range(NCORES)], axis=0)
    return (y_prompt.astype(np.float32), y_sample.astype(np.float32), na_k.astype(np.float32), na_v.astype(np.float32),
            lru.astype(np.float32), df_k.astype(np.float32), df_v.astype(np.float32))
```

```python
import math
import os
from contextlib import ExitStack

import numpy as np
import concourse.bass as bass
import concourse.mybir as mybir
from concourse.bass_utils import run_bass_kernel_spmd

F32 = mybir.dt.float32
BF16 = mybir.dt.bfloat16
AF = mybir.ActivationFunctionType
ALU = mybir.AluOpType
AX = mybir.AxisListType

ENGS = ("pe", "act", "dve", "pool", "sp")
EPS = 1e-6
NEG = -30000.0
T = 1536
TS = 1024
NCORES = 8
LAM_INIT = 0.8 - 0.6 * math.exp(-0.3 * 1)


class Sched:
    N_DMA_SEMS = 32
    QSEMS = {"sp": list(range(0, 16)), "pool": list(range(16, 28)), "act": list(range(28, 32))}

    def __init__(self):
        self.ops = {e: [] for e in ENGS}
        self.cnt = {e: 0 for e in ENGS}
        self.obs = {e: {f: 0 for f in ENGS} for e in ENGS}
        self.obs_dma = {e: 0 for e in ENGS}
        self.last_w = {}
        self.readers = {}
        self.n_dma = 0
        self.dma_tok = []
        self.q_hist = {q: [] for q in self.QSEMS}
        self.inherit = {}
        self.seen = set()
        self.frozen = False

    def _need(self, eng, tok, waits):
        if tok is None:
            return
        if tok[0] == "e":
            _, e2, n, vc, dm = tok
            if eng == "pe" and e2 == "pe":
                return
            if self.obs[eng][e2] >= n:
                return
            waits.append(("e", e2, n))
            o = self.obs[eng]
            for f in ENGS:
                if vc[f] > o[f]:
                    o[f] = vc[f]
            if o[e2] < n:
                o[e2] = n
            self.obs_dma[eng] |= dm
        else:
            _, did, vc, dm = tok
            if (self.obs_dma[eng] >> did) & 1:
                return
            waits.append(("d", did))
            o = self.obs[eng]
            for f in ENGS:
                if vc[f] > o[f]:
                    o[f] = vc[f]
            self.obs_dma[eng] |= dm | (1 << did)

    def _deps(self, eng, reads, writes):
        waits = []
        for k in list(reads) + list(writes):
            if k not in self.seen:
                self.seen.add(k)
                p = k.split(".")[0]
                for t in self.inherit.get(p, ()):
                    self._need(eng, t, waits)
        for k in reads:
            self._need(eng, self.last_w.get(k), waits)
        for k in writes:
            self._need(eng, self.last_w.get(k), waits)
            for t in self.readers.get(k, ()):
                self._need(eng, t, waits)
        return waits

    def _commit(self, tok, reads, writes):
        for k in reads:
            self.readers.setdefault(k, []).append(tok)
        for k in writes:
            self.last_w[k] = tok
            self.readers[k] = []

    def op(self, eng, fn, reads=(), writes=()):
        if self.frozen:
            return None
        pr = [k for k in reads if k.startswith("ps") and k not in writes]
        if pr:
            writes = list(writes) + pr
        waits = self._deps(eng, reads, writes)
        self.cnt[eng] += 1
        n = self.cnt[eng]
        vc = dict(self.obs[eng])
        vc[eng] = n
        tok = ("e", eng, n, vc, self.obs_dma[eng])
        self.ops[eng].append((waits, fn, ("e", n)))
        self._commit(tok, reads, writes)
        return tok

    def dma(self, eng, fn, reads=(), writes=()):
        if self.frozen:
            return None
        waits = self._deps(eng, reads, writes)
        did = self.n_dma
        self.n_dma += 1
        hist = self.q_hist[eng]
        qs = self.QSEMS[eng]
        k = len(hist)
        if k >= len(qs):
            prev = hist[k - len(qs)]
            if not ((self.obs_dma[eng] >> prev) & 1):
                waits.append(("d", prev))
                self.obs_dma[eng] |= (1 << prev)
        hist.append(did)
        self.dma_tok.append((qs[k % len(qs)], 16 * (k // len(qs) + 1)))
        tok = ("d", did, dict(self.obs[eng]), self.obs_dma[eng])
        self.ops[eng].append((waits, fn, ("d", did)))
        self._commit(tok, reads, writes)
        return tok

    def alias(self, new_prefixes, old_prefixes):
        toks = []
        olds = tuple(old_prefixes)
        for k, t in self.last_w.items():
            if k.split(".")[0] in olds and t is not None:
                toks.append(t)
        for k, ts in self.readers.items():
            if k.split(".")[0] in olds:
                toks.extend(ts)
        best = {}
        dm = []
        for t in toks:
            if t[0] == "e":
                if t[1] not in best or best[t[1]][2] < t[2]:
                    best[t[1]] = t
            else:
                dm.append(t)
        toks = list(best.values()) + dm
        for p in new_prefixes:
            self.inherit[p] = self.inherit.get(p, []) + toks
            for k in [k for k in self.seen if k.split(".")[0] == p]:
                self.seen.discard(k)

    def wait_all(self, eng, toks):
        waits = []
        for t in toks:
            self._need(eng, t, waits)
        self.ops[eng].append((waits, None, None))

    def emit(self, nc, sems, dsems):
        engmap = {"pe": "tensor", "act": "scalar", "dve": "vector", "pool": "gpsimd", "sp": "sync"}
        sched = self
        with nc.Block() as block:
            for e in ENGS:
                def body(eobj, e=e):
                    for waits, fn, inc in sched.ops[e]:
                        for w in waits:
                            if w[0] == "e":
                                eobj.wait_ge(sems[w[1]], w[2])
                            else:
                                si, tgt = sched.dma_tok[w[1]]
                                eobj.wait_ge(dsems[si], tgt)
                        if fn is None:
                            continue
                        ins = fn(eobj)
                        if inc[0] == "e":
                            ins.then_inc(sems[e], 1)
                        else:
                            si, tgt = sched.dma_tok[inc[1]]
                            ins.then_inc(dsems[si], 16)
                getattr(block, engmap[e])(body)


def _rope_tables():
    p = np.arange(128)
    d = p % 64
    half = d // 32
    idx = d % 32
    m = idx % 16
    first = idx < 16
    t = np.arange(TS)
    rows = (t // 64).astype(np.float32)
    cols = (t % 64).astype(np.float32)
    inv = (10000.0 ** (-np.arange(16, dtype=np.float32) / 16.0)).astype(np.float32)
    pos = np.where(half[:, None] == 0, rows[None, :], cols[None, :]).astype(np.float32)
    ang = (pos * inv[m][:, None]).astype(np.float32)
    C = np.cos(ang).astype(np.float32)
    Sg = np.sin(ang).astype(np.float32)
    Sg = np.where(first[:, None], -Sg, Sg).astype(np.float32)
    partner = np.where(first, p + 16, p - 16)
    perm = np.zeros((128, 128), np.float32)
    perm[partner, p] = 1.0
    return C, Sg, perm


NA_J = {0: list(range(0, 6)), 1: list(range(2, 8))}


def _na_tables():
    qc = np.arange(64)
    cs = np.clip(qc - 8, 0, 48)
    kc = np.arange(64)
    inwin = (kc[:, None] >= cs[None, :]) & (kc[:, None] < cs[None, :] + 16)
    colneg = np.where(inwin, 0.0, NEG).astype(np.float32)
    colneg = np.concatenate([colneg, colneg], axis=0)
    rowB = np.zeros((128, 12, 8), np.float32)
    for q in (0, 1):
        for ji, j in enumerate(NA_J[q]):
            for krl in range(2):
                for qrl in range(8):
                    kr = 2 * j + krl
                    qr = 8 * q + qrl
                    rs = min(max(qr - 4, 0), 8)
                    ok = rs <= kr <= rs + 7
                    rowB[krl, q * 6 + ji, qrl] = 0.0 if ok else NEG
    halfsel = np.zeros((128, 128), np.float32)
    halfsel[0, :64] = 1.0
    halfsel[1, 64:] = 1.0
    anti = np.zeros((128, 128), np.float32)
    for m in range(128):
        anti[(m // 64) * 64 + 63 - (m % 64), m] = 1.0
    return colneg, rowB, halfsel, anti


def build_nc():
    nc = bass.Bass("TRN2", target_bir_lowering=False)
    S = Sched()
    es = ExitStack()

    def din(name, shape):
        return nc.dram_tensor(name, list(shape), F32, kind="ExternalInput")

    def dout(name, shape):
        return nc.dram_tensor(name, list(shape), F32, kind="ExternalOutput")

    d_xs = din("xs", [1024, 1024]).ap()
    d_xp = din("xp", [512, 1024]).ap()
    d_cvec = din("cvec", [2, 1024]).ap()
    d_cnk = din("cnk", [256, 512]).ap()
    d_cnv = din("cnv", [256, 512]).ap()
    d_slru = din("slru", [2, 512]).ap()
    d_cdk = din("cdk", [256, 1024]).ap()
    d_cdv = din("cdv", [256, 1024]).ap()
    d_enorm = din("e_norm", [1024]).ap()
    d_eadaw = din("e_ada_w", [1024, 3072]).ap()
    d_eadab = din("e_ada_b", [3072]).ap()
    d_ewin = din("e_w_in", [1024, 3072]).ap()
    d_rpb = din("e_rpb", [8, 15, 31]).ap()
    d_convw = din("e_conv_w", [4, 512]).ap()
    d_convb = din("e_conv_b", [512]).ap()
    d_wa = din("e_lru_wa", [2, 8, 64, 64]).ap()
    d_ba = din("e_lru_ba", [2, 512]).ap()
    d_wx = din("e_lru_wx", [2, 8, 64, 64]).ap()
    d_bx = din("e_lru_bx", [2, 512]).ap()
    d_lam = din("e_lru_lam", [2, 512]).ap()
    d_ewout = din("e_w_out", [1024, 1024]).ap()
    d_onorm = din("o_norm", [1024]).ap()
    d_oadaw = din("o_ada_w", [1024, 3072]).ap()
    d_oadab = din("o_ada_b", [3072]).ap()
    d_owin = din("o_w_in", [1024, 4096]).ap()
    d_lq1 = din("o_lq1", [64]).ap()
    d_lk1 = din("o_lk1", [64]).ap()
    d_lq2 = din("o_lq2", [64]).ap()
    d_lk2 = din("o_lk2", [64]).ap()
    d_subg = din("o_sub_g", [128]).ap()
    d_owout = din("o_w_out", [1024, 1024]).ap()
    d_fnorm = din("final_norm", [1024]).ap()
    d_ident = din("c_ident", [128, 128]).ap()
    d_ropeC = din("c_ropeC", [128, 1024]).ap()
    d_ropeS = din("c_ropeS", [128, 1024]).ap()
    d_perm = din("c_perm", [128, 128]).ap()
    d_colneg = din("c_colneg", [128, 64]).ap()
    d_rowB = din("c_rowB", [128, 12, 8]).ap()
    d_halfsel = din("c_halfsel", [128, 128]).ap()
    d_anti = din("c_anti", [128, 128]).ap()

    o_ys = dout("y_s", [1024, 1024]).ap()
    o_yp = dout("y_p", [512, 1024]).ap()
    o_nak = dout("na_k", [512, 512]).ap()
    o_nav = dout("na_v", [512, 512]).ap()
    o_lru = dout("lru_o", [2, 2, 512]).ap()
    o_dfk = dout("df_k", [512, 1024]).ap()
    o_dfv = dout("df_v", [512, 1024]).ap()

    padr_t = nc.dram_tensor("padr", [8, 24, 128], F32)

    class Mem:
        def __init__(self, base, limit):
            self.p = base
            self.limit = limit
            self.n = 0

        def at(self, off, shape, dt, name=None):
            self.n += 1
            nm = "%s_%d" % (name or "t", self.n)
            n = int(np.prod(shape[1:])) * (4 if dt == F32 else 2)
            assert off % 4 == 0 and off + n <= self.limit, (nm, off, n, self.limit)
            return nc.alloc_sbuf_tensor_at(nm, list(shape), dt, offset=off)

        def new(self, shape, dt, name=None):
            n = int(np.prod(shape[1:])) * (4 if dt == F32 else 2)
            n = (n + 63) // 64 * 64
            off = self.p
            self.p += n
            return self.at(off, shape, dt, name)

        def region(self, nbytes):
            off = self.p
            self.p += (nbytes + 63) // 64 * 64
            assert self.p <= self.limit, self.p
            return off

    M = Mem(16512, 229312)
    xT = M.new([128, 8, T], F32, "xT")
    ring = [M.new([128, 8, 512], BF16, "ring") for _ in range(3)]
    hT = M.new([128, 8, T], BF16, "hT")
    oT = M.new([128, 8, T], BF16, "oT")
    ident = M.new([128, 128], F32, "ident")
    identb = M.new([128, 128], BF16, "identb")
    onesb = M.new([128, 128], BF16, "onesb")
    ones32 = M.new([128, 64], F32, "ones32")
    permb = M.new([128, 128], BF16, "permb")
    antib = M.new([128, 128], BF16, "antib")
    sel64 = M.new([128, 64], BF16, "sel64")
    halfsel = M.new([128, 128], BF16, "halfsel")
    rowB = M.new([128, 12, 8], BF16, "rowB")
    colneg = M.new([128, 64], BF16, "colneg")
    wblk = M.new([128, 16, 128], BF16, "wblk")
    pvec = M.new([128, 84], F32, "pvec")
    lstT = M.new([16, 128], F32, "lstT")
    pvec2 = M.new([128, 2, 24], F32, "pvec2")
    cTb = M.new([128, 8, 2], BF16, "cTb")
    normg = [pvec[:, 0:8], pvec[:, 8:16]]
    mT = [M.new([128, 24, 2], F32, "mT") for _ in range(2)]
    gsc = [M.new([128, 8, 2], F32, "gsc") for _ in range(2)]
    cw = pvec[:, 16:32].rearrange("p (j c) -> p c j", c=4)
    cb = pvec[:, 32:36]
    lba = pvec[:, 36:44].rearrange("p (d c) -> p d c", c=4)
    lbx = pvec[:, 44:52].rearrange("p (d c) -> p d c", c=4)
    llam = pvec[:, 52:60].rearrange("p (d c) -> p d c", c=4)
    cT = pvec[:, 68:84].rearrange("p (s c) -> p c s", c=8)
    m8sp = M.new([128, 2, 4], F32, "m8sp")
    hm8 = M.new([128, 2, 4], F32, "hm8")
    hba = M.new([128, 2, 4], F32, "hba")
    hbx = M.new([128, 2, 4], F32, "hbx")
    dg = M.new([128, 16, 128], BF16, "dg")
    h0 = pvec[:, 60:68].rearrange("p (d c) -> p d c", c=4)
    lst = M.new([128, 16], F32, "lst")
    lsum = M.new([128, 2], F32, "lsum")
    neglam = M.new([128, 1], F32, "neglam")
    sgs = M.new([128, 1], F32, "sgs")
    sstat = M.new([128, 8], F32, "sstat")
    U1 = M.region(26624)
    U2 = M.region(28672)
    U3 = M.region(12288)
    W = M.p
    WSZ = M.limit - W
    assert WSZ >= 8192, WSZ
    lqk = M.at(U3, [128, 4, 64], F32, "lqk")
    lprod = M.at(U3 + 1024, [128, 2, 64], F32, "lprod")
    zt = M.at(U3 + 1536, [128, 192], F32, "zt")
    rp = M.at(U3 + 2304, [8, 15, 31], F32, "rp")
    pstg = M.at(U3 + 4224, [84, 128], F32, "pstg")
    rpr = M.at(U3 + 4736, [8, 15, 31], F32, "rpr")
    pstg2 = M.at(U3 + 6656, [48, 128], F32, "pstg2")
    xstage = [M.at(U1 + i * 4096, [128, 1024], F32, "xstage") for i in range(2)]
    fnbc = M.at(U1 + 8192, [128, 1024], F32, "fnbc")
    adab = [M.at(U2 + i * 2048, [2, 512], F32, "adab") for i in range(2)]
    msb = [M.at(U2 + 4096 + i * 2048, [2, 512], F32, "msb") for i in range(2)]
    rstd = M.at(U1 + 12288, [128, T], F32, "rstd")
    sq = [M.at(U1 + 18432 + i * 1024, [128, 512], BF16, "sq") for i in range(3)]
    ntmp = [M.at(U1 + 21504 + i * 2048, [128, 512], F32, "ntmp") for i in range(2)] + [None]
    QT = M.at(U1, [128, 4, T], BF16, "QT")
    KT = M.at(U1 + 12288, [128, 4, 1792], BF16, "KT")
    VA = M.at(U2, [128, 14, 8, 66], BF16, "VA")
    Tb = [M.at(U2 + 14848 + i * 3072, [128, 24, 64], BF16, "Tb") for i in range(2)]
    xcb = M.at(U2 + 20992, [128, T], BF16, "xcb")
    Traw = M.at(U2 + 24064, [128, 24, 64], BF16, "Traw")
    xbT = M.at(U3, [128, 4, T], BF16, "xbT")
    hT_off = nc.lookup_mloc(hT).addr
    la = M.at(hT_off, [128, T], F32, "la")
    lu = M.at(hT_off + 6144, [128, T], F32, "lu")
    lhf = M.at(hT_off + 12288, [128, T], F32, "lhf")
    lhb = M.at(hT_off + 18432, [128, T], F32, "lhb")
    V1 = M.at(U2, [128, 14, 1024], BF16, "V1")
    QTh = [M.at(U1 + i * 3072, [128, T], BF16, "QTh") for i in range(2)]
    KTh = [M.at(U1 + 6144 + i * 3072, [128, T], BF16, "KTh") for i in range(2)]
    KTc = M.at(U1 + 12288, [128, 8, 256], BF16, "KTc")
    ptmp = [M.at(U1 + 16384 + i * 2048, [128, 512], F32, "ptmp") for i in range(5)]
    ropeC = M.at(U3, [128, 1024], F32, "ropeC")
    ropeS = M.at(U3 + 4096, [128, 1024], F32, "ropeS")
    rawb = [M.at(U3 + 8192 + i * 1024, [128, 512], BF16, "rawb") for i in range(2)]
    kvst = [M.at(W + i * 2048, [128, 512], F32, "kvst") for i in range(3)]
    ckst0 = M.at(U2 + 24064, [128, 2, 512], F32, "ckst0")
    ckst1 = M.at(U1 + 16384, [128, 2, 1024], F32, "ckst1")
    pbuf = [M.at(W + i * 1024, [128, 512], BF16, "pbuf") for i in range(4)]
    rden = [M.at(W + 4096 + i * 2048, [128, 512], F32, "rden") for i in range(2)]
    rdb = [M.at(W + 4096 + i * 1024, [128, 512], BF16, "rdb") for i in range(2)]
    g2 = [M.at(W + 6144, [128, 512], F32, "g2")]
    qm = [M.at(W + 3072, [128, 512], BF16, "qm"), M.at(W + 8192, [128, 512], BF16, "qm")]

    PS = [es.enter_context(nc.psum_tensor("ps%d" % i, [128, 512], F32)) for i in range(8)]
    sems = {e: es.enter_context(nc.semaphore("s_" + e)) for e in ENGS}
    dsems = [es.enter_context(nc.semaphore("d%d" % i)) for i in range(Sched.N_DMA_SEMS)]

    rr = {"s": 0, "o": 0, "ring": 0, "pb": 0, "kv": 0, "sq": 0, "nt": 0, "g2": 0, "rd": 0, "pt": 0, "rb": 0}

    def sbank():
        rr["s"] = (rr["s"] + 1) % 4
        return rr["s"]

    def rot(name, n):
        rr[name] = (rr[name] + 1) % n
        return rr[name]

    def pk(b):
        return "ps%d" % b

    alt = {"n": 0}

    def evac_eng():
        alt["n"] += 1
        return "dve" if alt["n"] % 2 else "act"

    def copy_op(eng, out, in_, reads, writes):
        if eng == "act":
            S.op("act", lambda e: e.copy(out, in_), reads=reads, writes=writes)
        elif eng == "dve":
            S.op("dve", lambda e: e.tensor_copy(out, in_), reads=reads, writes=writes)
        else:
            S.op("pool", lambda e: e.tensor_copy(out, in_), reads=reads, writes=writes)

    def dma(q, out, in_, reads=(), writes=()):
        return S.dma(q, lambda e: e.dma_start(out=out, in_=in_), reads=reads, writes=writes)

    wstate = {"n": 0}

    def wload(parts):
        slot = wstate["n"] % 3
        wstate["n"] += 1
        for (src, q0) in parts:
            ncols = src.shape[1]
            nq = (ncols + 127) // 128
            keys = ["ring%d.q%d" % (slot, q0 + i) for i in range(nq)]
            dma("pool", ring[slot][:, :, q0 * 128:q0 * 128 + ncols], src.rearrange("(c p) n -> p c n", p=128), writes=keys)
        return slot

    def rkeys(slot, qs=(0, 1, 2, 3)):
        return ["ring%d.q%d" % (slot, q) for q in qs]


    def norm_stats(tc):
        b = sbank()
        for c in range(8):
            i = rot("sq", 3)
            S.op("act", lambda e, i=i, c=c, tc=tc: e.activation(sq[i][:], xT[:, c, tc * 512:(tc + 1) * 512], AF.Square),
                 reads=["xT.%d.%d" % (c, tc)], writes=["sq%d" % i])
            S.op("pe", lambda e, i=i, c=c, b=b: e.matmul(PS[b][:], onesb[:], sq[i][:], start=(c == 0), stop=(c == 7)),
                 reads=["sq%d" % i, "onesb"], writes=[pk(b)])
        rs = rstd[:, tc * 512:(tc + 1) * 512]
        S.op("dve", lambda e, rs=rs, b=b: e.tensor_scalar(rs, PS[b][:], 1.0 / 1024.0, EPS, ALU.mult, ALU.add),
             reads=[pk(b)], writes=["rstd.%d" % tc])
        S.op("act", lambda e, rs=rs: e.activation(rs, rs, AF.Sqrt), reads=["rstd.%d" % tc], writes=["rstd.%d" % tc])
        S.op("dve", lambda e, rs=rs: e.reciprocal(rs, rs), reads=["rstd.%d" % tc], writes=["rstd.%d" % tc])

    def norm_apply(L, tc):
        s = 0 if tc < 2 else 1
        rs = rstd[:, tc * 512:(tc + 1) * 512]
        for c in range(8):
            i = rot("nt", 2)
            S.op("dve", lambda e, i=i, c=c, tc=tc, rs=rs: e.tensor_tensor(ntmp[i][:], xT[:, c, tc * 512:(tc + 1) * 512], rs, ALU.mult),
                 reads=["xT.%d.%d" % (c, tc), "rstd.%d" % tc], writes=["ntmp%d" % i])
            dst = hT[:, c, tc * 512:(tc + 1) * 512]
            S.op("act", lambda e, i=i, dst=dst, c=c, s=s: e.activation(dst, ntmp[i][:], AF.Identity, bias=mT[L][:, c, s:s + 1],
                                                                      scale=gsc[L][:, c, s:s + 1]),
                 reads=["ntmp%d" % i, "mT%d" % L, "gsc%d.%d" % (L, s)], writes=["hT.%d.%d" % (c, tc)])

    def norm_phase(L, tcs=(0, 1, 2)):
        for tc in tcs:
            norm_stats(tc)
            norm_apply(L, tc)

    def mod_units(L, d_adaw):
        mtb = 4 + L
        units = []
        for g in range(6):
            def u(g=g):
                slot = wload([(d_adaw[:, g * 512:(g + 1) * 512], 0)])

                def mm(e):
                    ins = None
                    for j in range(4):
                        col = (g * 4 + j) * 2
                        for c in range(8):
                            ins = e.matmul(PS[mtb][:, col:col + 2], ring[slot][:, c, j * 128:(j + 1) * 128], cTb[:, c, :],
                                           start=(c == 0), stop=(c == 7))
                    return ins
                S.op("pe", mm, reads=["cTb"] + rkeys(slot), writes=[pk(mtb)])
            units.append(u)
        return units

    def mod_finish(L):
        mtb = 4 + L
        bb_ = pvec2[:, L, :].unsqueeze(2).to_broadcast([128, 24, 2])
        S.op("dve", lambda e: e.tensor_tensor(mT[L][:], PS[mtb][:, 0:48].rearrange("p (a s) -> p a s", s=2), bb_, ALU.add),
             reads=[pk(mtb), "pvec2"], writes=["mT%d" % L])
        for s_ in range(2):
            S.op("dve", lambda e, s_=s_: e.scalar_tensor_tensor(gsc[L][:, :, s_], mT[L][:, 8:16, s_], 1.0, normg[L],
                                                               ALU.add, ALU.mult),
                 reads=["mT%d" % L, "pvec"], writes=["gsc%d.%d" % (L, s_)])

    dma("sp", ident[:], d_ident, writes=["ident"])
    rows = [(0, d_enorm.rearrange("(c p) -> c p", p=128), 8), (8, d_onorm.rearrange("(c p) -> c p", p=128), 8),
            (16, d_convw.rearrange("j (c p) -> (j c) p", p=128), 16), (32, d_convb.rearrange("(c p) -> c p", p=128), 4),
            (36, d_ba.rearrange("d (c p) -> (d c) p", p=128), 8), (44, d_bx.rearrange("d (c p) -> (d c) p", p=128), 8),
            (52, d_lam.rearrange("d (c p) -> (d c) p", p=128), 8), (60, d_slru.rearrange("d (c p) -> (d c) p", p=128), 8),
            (68, d_cvec.rearrange("s (c p) -> (s c) p", p=128), 16)]
    for (r0, src_, nr) in rows:
        dma("sp", pstg[r0:r0 + nr, :], src_, writes=["pstg.%d" % r0])
    S.op("pe", lambda e: e.transpose(PS[0][:, 0:84], pstg[0:84, :], ident[0:84, 0:84]),
         reads=["pstg.%d" % r0 for (r0, _, _) in rows] + ["ident"], writes=[pk(0)])
    S.op("dve", lambda e: e.tensor_copy(pvec[:], PS[0][:, 0:84]), reads=[pk(0)], writes=["pvec"])
    dma("sp", pstg2[0:24, :], d_eadab.rearrange("(c p) -> c p", p=128), writes=["pstg2.0"])
    dma("sp", pstg2[24:48, :], d_oadab.rearrange("(c p) -> c p", p=128), writes=["pstg2.1"])
    S.op("pe", lambda e: e.transpose(PS[1][:, 0:48], pstg2[0:48, :], ident[0:48, 0:48]),
         reads=["pstg2.0", "pstg2.1", "ident"], writes=[pk(1)])
    S.op("dve", lambda e: e.tensor_copy(pvec2[:].rearrange("p l a -> p (l a)"), PS[1][:, 0:48]), reads=[pk(1)], writes=["pvec2"])
    S.op("act", lambda e: e.activation(cTb[:], cT, AF.Silu), reads=["pvec"], writes=["cTb"])
    S.op("pool", lambda e: e.memset(onesb[:], 1.0), writes=["onesb"])
    m0 = mod_units(0, d_eadaw)
    for t in range(12):
        src = d_xs[t * 128:(t + 1) * 128, :] if t < 8 else d_xp[(t - 8) * 128:(t - 7) * 128, :]
        st = t % 2
        dma("sp", xstage[st][:], src, writes=["xstage%d" % st])
        for half in range(2):
            b = sbank()

            def tr(e, st=st, half=half, b=b):
                for j in range(4):
                    c = half * 4 + j
                    ins = e.transpose(PS[b][:, j * 128:(j + 1) * 128], xstage[st][:, c * 128:(c + 1) * 128], ident[:])
                return ins
            S.op("pe", tr, reads=["xstage%d" % st, "ident"], writes=[pk(b)])
            dst = xT[:, half * 4:half * 4 + 4, t * 128:(t + 1) * 128]
            srcp = PS[b][:].rearrange("p (j n) -> p j n", n=128)
            copy_op("dve" if half == 0 else "act", dst, srcp, [pk(b)], ["xT.%d.%d" % (half * 4 + j, t // 4) for j in range(4)])
        if t % 2 == 1:
            m0[t // 2]()
        if t % 4 == 3:
            norm_stats(t // 4)
    mod_finish(0)
    S.op("dve", lambda e: e.tensor_copy(identb[:], ident[:]), reads=["ident"], writes=["identb"])
    S.op("pool", lambda e: e.memset(ones32[:], 1.0), writes=["ones32"])
    S.op("pool", lambda e: e.memset(zt[:], 0.0), writes=["zt"])
    S.op("pool", lambda e: e.memset(wblk[:], 0.0), writes=["wblk"])
    dma("sp", rp[:], d_rpb, writes=["rp"])
    for i, dq in enumerate((d_lq1, d_lk1, d_lq2, d_lk2)):
        dma("sp", lqk[:, i, :], dq.partition_broadcast(128), writes=["lqk%d" % i])
    dma("sp", sgs[:], d_subg.rearrange("(p o) -> p o", o=1), writes=["sgs"])
    dma("pool", permb[:], d_perm, writes=["permb"])
    dma("pool", antib[:], d_anti, writes=["antib"])
    dma("pool", halfsel[:], d_halfsel, writes=["halfsel"])
    dma("pool", rowB[:], d_rowB, writes=["rowB"])
    dma("pool", colneg[:], d_colneg, writes=["colneg"])
    for g, dw in enumerate((d_wa, d_wx)):
        for par in range(2):
            for d in range(2):
                src = dw[d].rearrange("(c two) k m -> two k c m", two=2)[par]
                dst = wblk[par * 64:(par + 1) * 64, g * 8 + d * 4:g * 8 + d * 4 + 4, par * 64:(par + 1) * 64]
                dma("pool", dst, src, reads=["wblk"], writes=["wblk.%d%d%d" % (g, par, d)])
    WBK = ["wblk"] + ["wblk.%d%d%d" % (g, par, d) for g in range(2) for par in range(2) for d in range(2)]

    S.op("act", lambda e: e.activation(m8sp[:], llam, AF.Exp, scale=-1.0), reads=["pvec"], writes=["m8sp"])
    S.op("act", lambda e: e.activation(m8sp[:], m8sp[:], AF.Ln, bias=1.0), reads=["m8sp"], writes=["m8sp"])
    S.op("dve", lambda e: e.tensor_scalar(m8sp[:], m8sp[:], -8.0, None, ALU.mult), reads=["m8sp"], writes=["m8sp"])
    S.op("dve", lambda e: e.tensor_scalar(hm8[:], m8sp[:], 0.5, None, ALU.mult), reads=["m8sp"], writes=["hm8"])
    S.op("dve", lambda e: e.tensor_scalar(hba[:], lba, 0.5, None, ALU.mult), reads=["pvec"], writes=["hba"])
    S.op("dve", lambda e: e.tensor_scalar(hbx[:], lbx, 0.5, None, ALU.mult), reads=["pvec"], writes=["hbx"])
    for c_ in range(4):
        for j_ in range(4):
            S.op("dve", lambda e, c_=c_, j_=j_: e.tensor_scalar(dg[:, c_ * 4 + j_, :], ident[:], cw[:, c_, j_:j_ + 1], None, ALU.mult),
                 reads=["ident", "pvec"], writes=["dg.%d" % (c_ * 4 + j_)])

    S.op("dve", lambda e: e.tensor_tensor(lprod[:, 0, :], lqk[:, 0, :], lqk[:, 1, :], ALU.mult), reads=["lqk0", "lqk1"], writes=["lprod0"])
    S.op("dve", lambda e: e.tensor_tensor(lprod[:, 1, :], lqk[:, 2, :], lqk[:, 3, :], ALU.mult), reads=["lqk2", "lqk3"], writes=["lprod1"])
    S.op("dve", lambda e: e.reduce_sum(lsum[:], lprod[:], axis=AX.X), reads=["lprod0", "lprod1"], writes=["lsum"])
    S.op("act", lambda e: e.activation(lsum[:], lsum[:], AF.Exp), reads=["lsum"], writes=["lsum"])
    S.op("dve", lambda e: e.tensor_tensor(neglam[:], lsum[:, 1:2], lsum[:, 0:1], ALU.subtract), reads=["lsum"], writes=["neglam"])
    S.op("dve", lambda e: e.tensor_scalar(neglam[:], neglam[:], -LAM_INIT, None, ALU.add), reads=["neglam"], writes=["neglam"])
    S.op("dve", lambda e: e.tensor_scalar(sgs[:], sgs[:], 0.5 * (1.0 - LAM_INIT), None, ALU.mult), reads=["sgs"], writes=["sgs"])

    S.op("dve", lambda e: e.tensor_scalar(rpr[:], rp[:, :, ::-1], 8.0, None, ALU.mult), reads=["rp"], writes=["rpr"])
    padr_flat = bass.AP(padr_t, 0, [[192, 128], [1, 192]])
    dma("sp", padr_flat, zt[:], reads=["zt"], writes=["padr"])
    padr_mid = bass.AP(padr_t, 4 * 128 + 48, [[24 * 128, 8], [128, 15], [1, 31]])
    dma("sp", padr_mid, rpr[:], reads=["rpr", "padr"], writes=["padr.0"])
    PADK = ["padr", "padr.0"]

    def load_tb(h):
        slot = h % 2
        k = "Tb%d" % slot
        src0 = bass.AP(padr_t, h * 24 * 128, [[1, 64], [128, 24], [1, 64]])
        src1 = bass.AP(padr_t, h * 24 * 128 + 128, [[1, 64], [128, 23], [1, 64]])
        dma("pool", Traw[0:64, :, :], src0, reads=PADK + ["Traw"], writes=["Traw.a"])
        dma("pool", Traw[64:128, 0:23, :], src1, reads=PADK + ["Traw"], writes=["Traw.b"])
        cn = colneg[:].unsqueeze(1).to_broadcast([128, 8, 64])
        for n3 in range(3):
            b = sbank()
            S.op("pe", lambda e, b=b, n3=n3: e.matmul(PS[b][:], antib[:], Traw[:, n3 * 8:(n3 + 1) * 8, :], start=True, stop=True),
                 reads=["Traw", "Traw.a", "Traw.b", "antib"], writes=[pk(b)])
            S.op("dve", lambda e, b=b, n3=n3: e.tensor_tensor(Tb[slot][:, n3 * 8:(n3 + 1) * 8, :],
                                                             PS[b][:].rearrange("p (a q) -> p a q", q=64), cn, ALU.add),
                 reads=[pk(b), "colneg"], writes=[k + ".%d" % n3])
        return slot

    HKEYS = lambda tc: ["hT.%d.%d" % (c, tc) for c in range(8)]

    def proj_fm(slot, q, tc, b, ncols=128):
        def mm(e):
            for c in range(8):
                ins = e.matmul(PS[b][:], ring[slot][:, c, q * 128:(q + 1) * 128], hT[:, c, tc * 512:(tc + 1) * 512],
                               start=(c == 0), stop=(c == 7))
            return ins
        S.op("pe", mm, reads=HKEYS(tc) + rkeys(slot, (q,)), writes=[pk(b)])

    def proj_tm(slot, t, b):
        def mm(e):
            for c in range(8):
                ins = e.matmul(PS[b][:], hT[:, c, t * 128:(t + 1) * 128], ring[slot][:, c, :], start=(c == 0), stop=(c == 7))
            return ins
        S.op("pe", mm, reads=HKEYS(t // 4) + rkeys(slot), writes=[pk(b)])

    def stage_out(b, dst, extra=()):
        i = rot("kv", 3)
        S.op("act", lambda e, i=i, b=b: e.copy(kvst[i][:], PS[b][:]), reads=[pk(b)] + list(extra), writes=["kvst%d" % i])
        return dma("sp", dst, kvst[i][:], reads=["kvst%d" % i])

    out_toks = []
    KSTOP = float(os.environ.get("KSTOP", "99"))

    def ckpt(k):
        if KSTOP <= k:
            S.frozen = True

    ckpt(3)
    for tc_ in range(3):
        norm_apply(0, tc_)
    ckpt(4)
    S.alias(["QT", "KT"], ["rstd", "sq0", "sq1", "sq2", "ntmp0", "ntmp1", "xstage0", "xstage1", "fnbc"])
    S.alias(["xbT"], ["lqk0", "lqk1", "lqk2", "lqk3", "lprod0", "lprod1", "zt", "rp", "rpr", "pstg", "pstg2"])

    S.op("pool", lambda e: e.memset(VA[:, :, :, 64:65], 1.0), writes=["VA.ones"])
    for t in range(2):
        dma("pool", VA[:, 12 + t, :, 0:64], d_cnv[t * 128:(t + 1) * 128, :].rearrange("p (h d) -> p h d", d=64), writes=["VA.%d" % (12 + t)])
    dma("sp", ckst0[:], d_cnk.rearrange("(t p) f -> p t f", p=128), writes=["ckst0"])
    for t in range(2):
        b = sbank()

        def trk(e, t=t, b=b):
            for oc in range(4):
                ins = e.transpose(PS[b][:, oc * 128:(oc + 1) * 128], ckst0[:, t, oc * 128:(oc + 1) * 128], ident[:])
            return ins
        S.op("pe", trk, reads=["ckst0", "ident"], writes=[pk(b)])
        copy_op("dve", KT[:, :, 1536 + t * 128:1536 + (t + 1) * 128], PS[b][:].rearrange("p (j n) -> p j n", n=128),
                [pk(b)], ["KT.c%d" % t])

    ckpt(4.1)
    m1 = mod_units(1, d_oadaw)
    for gi, gname in enumerate(("q", "k", "v", "ga", "xb", "gb")):
        slot = wload([(d_ewin[:, gi * 512:(gi + 1) * 512], 0)])
        if gname in ("q", "k", "xb", "ga", "gb"):
            for oc in range(4):
                for tc in range(3):
                    b = sbank()
                    proj_fm(slot, oc, tc, b)
                    cs = slice(tc * 512, (tc + 1) * 512)
                    if gname == "q":
                        copy_op(evac_eng(), QT[:, oc, cs], PS[b][:], [pk(b)], ["QT.%d.%d" % (oc, tc)])
                    elif gname == "k":
                        copy_op(evac_eng(), KT[:, oc, cs], PS[b][:], [pk(b)], ["KT.%d.%d" % (oc, tc)])
                    elif gname == "xb":
                        copy_op(evac_eng(), xbT[:, oc, cs], PS[b][:], [pk(b)], ["xbT.%d.%d" % (oc, tc)])
                    else:
                        cc = oc if gname == "ga" else 4 + oc
                        S.op("act", lambda e, cc=cc, cs=cs, b=b: e.activation(oT[:, cc, cs], PS[b][:], AF.Silu),
                             reads=[pk(b)], writes=["oT.%d.%d" % (cc, tc)])
        if gname == "k":
            for t in range(8, 12):
                b = sbank()
                proj_tm(slot, t, b)
                out_toks.append(stage_out(b, o_nak[(t - 8) * 128:(t - 7) * 128, :]))
        if gname == "v":
            for t in range(12):
                b = sbank()
                proj_tm(slot, t, b)
                if os.environ.get("KVAR", "") != "A":
                    S.op("dve", lambda e, t=t, b=b: e.tensor_copy(VA[:, t, :, 0:64], PS[b][:].rearrange("p (h d) -> p h d", d=64)),
                         reads=[pk(b)], writes=["VA.%d" % t])
                if t >= 8:
                    out_toks.append(stage_out(b, o_nav[(t - 8) * 128:(t - 7) * 128, :], extra=["VA.%d" % t]))
        m1[gi]()
        ckpt(4.2 + 0.1 * gi)
    mod_finish(1)

    ckpt(5)
    S.alias(["la", "lu", "lhf", "lhb"], ["hT"])
    S.alias(["pbuf0", "pbuf1", "pbuf2", "pbuf3", "rden0", "rden1", "g20", "qm0", "qm1", "qmz", "rdb0", "rdb1"], ["kvst0", "kvst1", "kvst2"])

    SEGS = [(0, 1024), (1024, 1280), (1280, 1536)]
    CWK = ["pvec"]

    def lru_chunk(c):
        XB = ["xbT.%d.%d" % (c, tc) for tc in range(3)]
        DG = ["dg.%d" % (c * 4 + j) for j in range(4)]
        for tc in range(3):
            c0 = tc * 512
            pieces = [(c0, c0 + 512, 0, 1024)] if tc < 2 else [(1024, 1280, 1024, 1280), (1280, 1536, 1280, 1536)]
            b = sbank()

            def cv(e, pieces=pieces, c0=c0, b=b):
                ins = None
                for (a, bb, lo, hi) in pieces:
                    e.matmul(PS[b][:, a - c0:bb - c0], dg[:, c * 4 + 2, :], xbT[:, c, a:bb], start=True, stop=False)
                    taps = ((0, -2), (1, -1), (3, 1))
                    for ti, (j, off) in enumerate(taps):
                        a2 = max(a, lo - off)
                        b2 = min(bb, hi - off)
                        ins = e.matmul(PS[b][:, a2 - c0:b2 - c0], dg[:, c * 4 + j, :], xbT[:, c, a2 + off:b2 + off],
                                       start=False, stop=(ti == 2))
                return ins
            S.op("pe", cv, reads=XB + DG, writes=[pk(b)])
            S.op("act", lambda e, b=b, c0=c0: e.activation(xcb[:, c0:c0 + 512], PS[b][:], AF.Identity, bias=cb[:, c:c + 1]),
                 reads=[pk(b), "pvec"], writes=["xcb"])
            yield
        for d in range(2):
            for tc in range(3):
                cs = slice(tc * 512, (tc + 1) * 512)
                b1 = sbank()
                S.op("pe", lambda e, b1=b1, cs=cs, d=d: e.matmul(PS[b1][:], wblk[:, 0 * 8 + d * 4 + c, :], xcb[:, cs], start=True, stop=True),
                     reads=["xcb"] + WBK, writes=[pk(b1)])
                S.op("act", lambda e, b1=b1, cs=cs, d=d: e.activation(la[:, cs], PS[b1][:], AF.Tanh, bias=hba[:, d, c:c + 1], scale=0.5),
                     reads=[pk(b1), "hba"], writes=["la"])
                yield
                b2 = sbank()
                S.op("pe", lambda e, b2=b2, cs=cs, d=d: e.matmul(PS[b2][:], wblk[:, 1 * 8 + d * 4 + c, :], xcb[:, cs], start=True, stop=True),
                     reads=["xcb"] + WBK, writes=[pk(b2)])
                S.op("act", lambda e, b2=b2, cs=cs, d=d: e.activation(lu[:, cs], PS[b2][:], AF.Tanh, bias=hbx[:, d, c:c + 1], scale=0.5),
                     reads=[pk(b2), "hbx"], writes=["lu"])
                yield
            dst = lhf if d == 0 else lhb
            dk = "lhf" if d == 0 else "lhb"
            S.op("act", lambda e, d=d: e.activation(la[:], la[:], AF.Exp, scale=hm8[:, d, c:c + 1], bias=hm8[:, d, c:c + 1]),
                 reads=["la", "hm8"], writes=["la"])
            yield
            S.op("dve", lambda e: e.scalar_tensor_tensor(lu[:], lu[:], 1.0, xcb[:], ALU.add, ALU.mult), reads=["lu", "xcb"], writes=["lu"])
            yield
            S.op("act", lambda e, dst=dst: e.activation(dst[:], la[:], AF.Square), reads=["la"], writes=[dk])
            yield
            S.op("act", lambda e, dst=dst: e.activation(dst[:], dst[:], AF.Sqrt, scale=-0.25, bias=0.25), reads=[dk], writes=[dk])
            yield
            S.op("dve", lambda e, dst=dst: e.tensor_tensor(lu[:], lu[:], dst[:], ALU.mult), reads=["lu", dk], writes=["lu"])
            yield
            for si, (lo, hi) in enumerate(SEGS):
                init = h0[:, d, c:c + 1] if si == 0 else 0.0
                if d == 0:
                    S.op("dve", lambda e, lo=lo, hi=hi, init=init, dst=dst: e.tensor_tensor_scan(
                        dst[:, lo:hi], la[:, lo:hi], lu[:, lo:hi], init, ALU.mult, ALU.add),
                        reads=["la", "lu", "pvec", dk], writes=[dk])
                    yield
                else:
                    S.op("dve", lambda e, lo=lo, hi=hi, init=init, dst=dst: e.tensor_tensor_scan(
                        dst[:, lo:hi][:, ::-1], la[:, lo:hi][:, ::-1], lu[:, lo:hi][:, ::-1], init, ALU.mult, ALU.add),
                        reads=["la", "lu", "pvec", dk], writes=[dk])
                    yield
            for s in range(2):
                lo, hi = SEGS[1 + s]
                col = hi - 1 if d == 0 else lo
                S.op("pool", lambda e, s=s, d=d, col=col, dst=dst: e.tensor_copy(lst[:, (s * 2 + d) * 4 + c:(s * 2 + d) * 4 + c + 1], dst[:, col:col + 1]),
                     reads=[dk], writes=["lst.%d%d%d" % (c, s, d)])
            yield
        S.op("pool", lambda e: e.tensor_tensor(lhf[:], lhf[:], lhb[:], ALU.add), reads=["lhf", "lhb"], writes=["lhf"])
        yield
        OK_ = ["oT.%d.%d" % (4 + c, tc) for tc in range(3)]
        S.op("pool", lambda e: e.tensor_tensor(oT[:, 4 + c, :], oT[:, 4 + c, :], lhf[:], ALU.mult), reads=["lhf"] + OK_, writes=OK_)
        yield

    def lru_all():
        for c in range(4):
            yield from lru_chunk(c)
    lru_it = lru_all()

    def pump():
        try:
            next(lru_it)
        except StopIteration:
            pass

    from collections import deque
    LQ = deque()

    def attn64(h, segs, tc, na=None):
        ch, pb = h // 2, (h % 2) * 64
        ob = 4 + rot("o", 2)
        qi = h % 2
        N = sum(q_.stop - q_.start for (q_, _, _) in segs)
        qcols = slice(segs[0][0].start, segs[-1][0].stop)
        S.op("pool", lambda e: e.tensor_copy(qm[qi][pb:pb + 64, 0:N], QT[pb:pb + 64, ch, qcols]),
             reads=["QT.%d.%d" % (ch, tc), "qmz"], writes=["qm%d" % qi])
        flat = []
        for (q_, off, tiles) in segs:
            for i, t_ in enumerate(tiles):
                flat.append((off, q_.stop - q_.start, t_, i == 0, i == len(tiles) - 1))
        n = len(flat)
        sb_ = [None] * n

        def s_mm(i):
            off, Ns, (kc0, vt, j), _, _ = flat[i]
            b = sbank()
            sb_[i] = b
            kkey = "KT.c%d" % ((kc0 - 1536) // 128) if kc0 >= 1536 else "KT.%d.%d" % (ch, kc0 // 512)

            def mm(e):
                last = (na is None) or (j is None)
                ins = e.matmul(PS[b][:, 0:Ns], KT[:, ch, kc0:kc0 + 128], qm[qi][:, off:off + Ns], start=True, stop=last)
                if not last:
                    tbs, qc = na
                    ai0 = 2 * j - 8 * qc + 11
                    rhs = Tb[tbs][:, ai0 - 7:ai0 + 1, :][:, ::-1, :]
                    e.matmul(PS[b][:, 0:Ns], identb[:], rhs, start=False, stop=False)
                    pidx = qc * 6 + NA_J[qc].index(j)
                    rb = rowB[:, pidx, :].unsqueeze(2).to_broadcast([128, 8, 64])
                    ins = e.matmul(PS[b][:, 0:Ns], halfsel[:], rb, start=False, stop=True)
                return ins
            rd = ["qm%d" % qi, kkey]
            if na is not None and j is not None:
                rd += ["Tb%d.%d" % (na[0], n3) for n3 in range(3)] + ["identb", "halfsel", "rowB"]
            S.op("pe", mm, reads=rd, writes=[pk(b)])

        s_mm(0)
        if n > 1:
            s_mm(1)
        for i in range(n):
            off, Ns, (kc0, vt, j), st, sp_ = flat[i]
            b = sb_[i]
            pi = rot("pb", 3)
            S.op("act", lambda e, b=b, pi=pi, Ns=Ns: e.activation(pbuf[pi][:, 0:Ns], PS[b][:, 0:Ns], AF.Exp, scale=0.125),
                 reads=[pk(b)], writes=["pbuf%d" % pi])
            if i + 1 < n:
                pump()
                if LQ and i >= 1:
                    LQ.popleft()()
                if i + 2 < n:
                    s_mm(i + 2)
            S.op("pe", lambda e, pi=pi, vt=vt, st=st, sp_=sp_, off=off, Ns=Ns: e.matmul(
                PS[ob][0:65, off:off + Ns], VA[:, vt, h, 0:65], pbuf[pi][:, 0:Ns], start=st, stop=sp_),
                reads=["pbuf%d" % pi, "VA.%d" % vt, "VA.ones"], writes=[pk(ob)])
        ri = ob - 4
        S.op("dve", lambda e: e.reciprocal(rdb[ri][64:65, 0:N], PS[ob][64:65, 0:N]), reads=[pk(ob)], writes=["rdb%d" % ri])
        bb = 6 + (ob - 4)
        ok = "oT.%d.%d" % (ch, tc)

        def post():
            S.op("pe", lambda e: e.matmul(PS[bb][0:64, 0:N], sel64[:], rdb[ri][:, 0:N], start=True, stop=True),
                 reads=["rdb%d" % ri, "sel64"], writes=[pk(bb)])
            S.op("dve", lambda e: e.tensor_tensor(g2[0][pb:pb + 64, 0:N], oT[pb:pb + 64, ch, qcols], PS[bb][0:64, 0:N], ALU.mult),
                 reads=[ok, pk(bb)], writes=["g20"])
            S.op("dve", lambda e: e.tensor_tensor(oT[pb:pb + 64, ch, qcols], g2[0][pb:pb + 64, 0:N], PS[ob][0:64, 0:N], ALU.mult),
                 reads=["g20", pk(ob), ok], writes=[ok])
        LQ.append(post)

    S.alias(["Traw"], ["ckst0"])
    S.op("pool", lambda e: e.memset(Traw[:], 0.0), writes=["Traw"])
    S.op("pool", lambda e: e.memset(sel64[:], 0.0), writes=["sel64"])
    S.op("pool", lambda e: e.memset(sel64[64:65, :], 1.0), reads=["sel64"], writes=["sel64"])
    S.op("pool", lambda e: e.memset(rdb[0][:], 0.0), writes=["rdb0"])
    S.op("pool", lambda e: e.memset(rdb[1][:], 0.0), writes=["rdb1"])
    S.op("pool", lambda e: e.memset(qm[0][:], 0.0), writes=["qmz", "qm0"])
    S.op("pool", lambda e: e.memset(qm[1][:], 0.0), writes=["qmz", "qm1"])
    load_tb(0)
    for h in range(8):
        tbs = h % 2
        if h + 1 < 8:
            load_tb(h + 1)
        psegs = []
        for s in range(2):
            q0 = 1024 + s * 256
            psegs.append((slice(q0, q0 + 256), s * 256, [(q0 + t * 128, 8 + 2 * s + t, None) for t in range(2)]))
        attn64(h, psegs, 2)
        for qc in range(2):
            tiles = [(j * 128, j, j) for j in NA_J[qc]] + [(1536 + t * 128, 12 + t, None) for t in range(2)]
            attn64(h, [(slice(qc * 512, (qc + 1) * 512), 0, tiles)], qc, na=(tbs, qc))
    while LQ:
        LQ.popleft()()
    for _ in range(400):
        pump()
    bl = sbank()
    S.op("pe", lambda e: e.transpose(PS[bl][0:16, 0:128], lst[:, 0:16], ident[:]),
         reads=["lst.%d%d%d" % (c, s, d) for c in range(4) for s in range(2) for d in range(2)] + ["ident"], writes=[pk(bl)])
    S.op("dve", lambda e: e.tensor_copy(lstT[:], PS[bl][0:16, 0:128]), reads=[pk(bl)], writes=["lstT"])
    out_toks.append(dma("sp", o_lru.rearrange("s d (c p) -> (s d c) p", p=128), lstT[:], reads=["lstT"]))

    def wout_phase(L, d_wout, after_tc=None):
        slots = [wload([(d_wout[:, g * 512:(g + 1) * 512], 0)]) for g in range(2)]
        for tc in range(3):
            s = 0 if tc < 2 else 1
            for oc in range(8):
                slot, q = slots[oc // 4], oc % 4
                b = sbank()

                def mm(e, slot=slot, q=q, tc=tc, b=b):
                    for c in range(8):
                        ins = e.matmul(PS[b][:], ring[slot][:, c, q * 128:(q + 1) * 128], oT[:, c, tc * 512:(tc + 1) * 512],
                                       start=(c == 0), stop=(c == 7))
                    return ins
                S.op("pe", mm, reads=["oT.%d.%d" % (c, tc) for c in range(8)] + rkeys(slot, (q,)), writes=[pk(b)])
                xs_ = xT[:, oc, tc * 512:(tc + 1) * 512]
                S.op("dve", lambda e, xs_=xs_, b=b, oc=oc, s=s: e.scalar_tensor_tensor(
                    xs_, PS[b][:], mT[L][:, 16 + oc, s:s + 1], xs_, ALU.mult, ALU.add),
                    reads=[pk(b), "mT%d" % L, "xT.%d.%d" % (oc, tc)], writes=["xT.%d.%d" % (oc, tc)])
            if after_tc is not None:
                after_tc(tc)

    ckpt(6)
    S.alias(["hT"], ["la", "lu", "lhf", "lhb"])
    S.alias(["rstd", "sq0", "sq1", "sq2", "ntmp0", "ntmp1"], ["QT", "KT"])
    S.alias(["kvst0", "kvst1", "kvst2"], ["pbuf0", "pbuf1", "pbuf2", "pbuf3", "rden0", "rden1", "g20", "qm0", "qm1", "qmz", "rdb0", "rdb1"])
    wout_phase(0, d_ewout, after_tc=lambda tc: norm_phase(1, (tc,)))
    ckpt(8)

    S.alias(["V1"], ["VA", "Tb0", "Tb1", "xcb", "ckst0", "Traw"])
    S.alias(["QTh0", "QTh1", "KTh0", "KTh1", "KTc", "ptmp0", "ptmp1", "ptmp2", "ptmp3", "ptmp4", "ckst1"],
            ["rstd", "sq0", "sq1", "sq2", "ntmp0", "ntmp1", "QT", "KT"])
    S.alias(["ropeC", "ropeS", "rawb0", "rawb1"], ["xbT"])
    dma("sp", ropeC[:], d_ropeC, writes=["ropeC"])
    dma("sp", ropeS[:], d_ropeS, writes=["ropeS"])
    dma("pool", V1[:, 12:14, :], d_cdv.rearrange("(t p) f -> p t f", p=128), writes=["V1.12", "V1.13"])
    dma("sp", ckst1[:], d_cdk.rearrange("(t p) f -> p t f", p=128), writes=["ckst1"])
    for t in range(2):
        for hh in range(2):
            b = sbank()

            def trk1(e, t=t, hh=hh, b=b):
                for j in range(4):
                    h = hh * 4 + j
                    ins = e.transpose(PS[b][:, j * 128:(j + 1) * 128], ckst1[:, t, h * 128:(h + 1) * 128], ident[:])
                return ins
            S.op("pe", trk1, reads=["ckst1", "ident"], writes=[pk(b)])
            copy_op(evac_eng(), KTc[:, hh * 4:hh * 4 + 4, t * 128:(t + 1) * 128], PS[b][:].rearrange("p (j n) -> p j n", n=128),
                    [pk(b)], ["KTc.%d%d" % (t, hh)])
    KTCK = ["KTc.%d%d" % (t, hh) for t in range(2) for hh in range(2)]

    for g in range(2):
        slot = wload([(d_owin[:, 2048 + g * 512:2048 + (g + 1) * 512], 0)])
        for t in range(12):
            b = sbank()
            proj_tm(slot, t, b)
            S.op("dve", lambda e, t=t, b=b, g=g: e.tensor_copy(V1[:, t, g * 512:(g + 1) * 512], PS[b][:]),
                 reads=[pk(b)], writes=["V1.%d.%d" % (t, g)])
            if t >= 8:
                out_toks.append(stage_out(b, o_dfv[(t - 8) * 128:(t - 7) * 128, g * 512:(g + 1) * 512], extra=["V1.%d.%d" % (t, g)]))
    S.alias(["ptmp0", "ptmp1", "ptmp2", "ptmp3", "ptmp4"], ["ckst1"])

    def rope_evac(b, dst, dkey, tc):
        ri = rot("rb", 2)
        S.op("act", lambda e: e.copy(rawb[ri][:], PS[b][:]), reads=[pk(b)], writes=["rawb%d" % ri])
        b2 = sbank()
        S.op("pe", lambda e: e.matmul(PS[b2][:], permb[:], rawb[ri][:], start=True, stop=True),
             reads=["rawb%d" % ri, "permb"], writes=[pk(b2)])
        p1 = rot("pt", 3)
        S.op("dve", lambda e: e.tensor_tensor(ptmp[p1][:], PS[b][:], ropeC[:, tc * 512:(tc + 1) * 512], ALU.mult),
             reads=[pk(b), "ropeC"], writes=["ptmp%d" % p1])
        p2 = rot("pt", 3)
        S.op("dve", lambda e: e.tensor_tensor(ptmp[p2][:], PS[b2][:], ropeS[:, tc * 512:(tc + 1) * 512], ALU.mult),
             reads=[pk(b2), "ropeS"], writes=["ptmp%d" % p2])
        S.op("pool", lambda e: e.tensor_tensor(dst, ptmp[p1][:], ptmp[p2][:], ALU.add),
             reads=["ptmp%d" % p1, "ptmp%d" % p2], writes=[dkey])

    from collections import deque
    PQ = deque()
    TQ = deque()

    def tick(first=False):
        if TQ:
            TQ.popleft()()
        if first:
            for _ in range(3):
                if PQ:
                    PQ.popleft()()

    def drain(q):
        while q:
            q.popleft()()

    def diff_attn(h, hs, segs, tc):
        qk = "QTh%d.%d" % (hs, tc)
        flat = []
        for (qc_, off, tiles) in segs:
            for i, t_ in enumerate(tiles):
                flat.append((qc_, off, qc_.stop - qc_.start, t_, i == 0, i == len(tiles) - 1))
        n = len(flat)
        sb1 = [None] * n
        sb2 = [None] * n

        def s_mm(i):
            qc_, off, Ns, (kfn, kkey, vt), _, _ = flat[i]
            b1 = sbank()
            b2 = sbank()
            sb1[i], sb2[i] = b1, b2

            def mm(e):
                e.matmul(PS[b1][:, 0:Ns], kfn(slice(0, 64)), QTh[hs][0:64, qc_], start=True, stop=True)
                return e.matmul(PS[b2][:, 0:Ns], kfn(slice(64, 128)), QTh[hs][64:128, qc_], start=True, stop=True)
            S.op("pe", mm, reads=[qk] + kkey, writes=[pk(b1), pk(b2)])

        s_mm(0)
        for i in range(n):
            qc_, off, Ns, (kfn, kkey, vt), st, sp_ = flat[i]
            b1, b2 = sb1[i], sb2[i]
            p1 = rot("pb", 4)
            S.op("act", lambda e, b1=b1, p1=p1, Ns=Ns: e.activation(pbuf[p1][:, 0:Ns], PS[b1][:, 0:Ns], AF.Exp, scale=0.125),
                 reads=[pk(b1)], writes=["pbuf%d" % p1])
            p2 = rot("pb", 4)
            S.op("act", lambda e, b2=b2, p2=p2, Ns=Ns: e.activation(pbuf[p2][:, 0:Ns], PS[b2][:, 0:Ns], AF.Exp, scale=0.125),
                 reads=[pk(b2)], writes=["pbuf%d" % p2])
            if i + 1 < n:
                tick(i == 0)
                s_mm(i + 1)
            vk = ["V1.%d" % vt] if vt >= 12 else ["V1.%d.%d" % (vt, h // 4)]

            def pv(e, p1=p1, p2=p2, vt=vt, st=st, sp_=sp_, off=off, Ns=Ns):
                vv = V1[:, vt, h * 128:(h + 1) * 128]
                e.matmul(PS[4][:, off:off + Ns], vv, pbuf[p1][:, 0:Ns], start=st, stop=sp_)
                e.matmul(PS[5][:, off:off + Ns], vv, pbuf[p2][:, 0:Ns], start=st, stop=sp_)
                e.matmul(PS[6][:, off:off + Ns], onesb[:], pbuf[p1][:, 0:Ns], start=st, stop=sp_)
                return e.matmul(PS[7][:, off:off + Ns], onesb[:], pbuf[p2][:, 0:Ns], start=st, stop=sp_)
            S.op("pe", pv, reads=["pbuf%d" % p1, "pbuf%d" % p2, "onesb"] + vk, writes=[pk(4), pk(5), pk(6), pk(7)])
        N = sum(q_.stop - q_.start for (q_, _, _) in segs)
        qcols = slice(segs[0][0].start, segs[-1][0].stop)
        r1 = rden[0][:, 0:N]
        r2 = rden[1][:, 0:N]
        a1 = ptmp[3][:, 0:N]
        a2 = ptmp[4][:, 0:N]
        drain(TQ)
        S.op("act", lambda e: e.copy(a1, PS[4][:, 0:N]), reads=[pk(4)], writes=["ptmp3"])
        S.op("dve", lambda e: e.tensor_copy(r1, PS[6][:, 0:N]), reads=[pk(6)], writes=["rden0"])
        S.op("act", lambda e: e.copy(a2, PS[5][:, 0:N]), reads=[pk(5)], writes=["ptmp4"])
        S.op("dve", lambda e: e.tensor_copy(r2, PS[7][:, 0:N]), reads=[pk(7)], writes=["rden1"])
        ok = "oT.%d.%d" % (h, tc)

        def stB():
            S.op("dve", lambda e: e.reciprocal(r2, r2), reads=["rden1"], writes=["rden1"])
            S.op("dve", lambda e: e.scalar_tensor_tensor(r2, r1, neglam[:, 0:1], r2, ALU.mult, ALU.mult),
                 reads=["rden0", "rden1", "neglam"], writes=["rden1"])
            S.op("dve", lambda e: e.tensor_tensor(a2, a2, r2, ALU.mult), reads=["ptmp4", "rden1"], writes=["ptmp4"])
            S.op("dve", lambda e: e.tensor_tensor(a1, a1, a2, ALU.add), reads=["ptmp3", "ptmp4"], writes=["ptmp3"])

        def stC():
            pi = rot("pb", 4)
            S.op("act", lambda e: e.activation(pbuf[pi][:, 0:N], a1, AF.Square), reads=["ptmp3"], writes=["pbuf%d" % pi])
            bs = sbank()
            S.op("pe", lambda e: e.matmul(PS[bs][:, 0:N], onesb[:], pbuf[pi][:, 0:N], start=True, stop=True),
                 reads=["pbuf%d" % pi, "onesb"], writes=[pk(bs)])
            S.op("dve", lambda e: e.scalar_tensor_tensor(a2, r1, EPS, r1, ALU.mult, ALU.mult), reads=["rden0", "ptmp4"], writes=["ptmp4"])
            S.op("dve", lambda e: e.scalar_tensor_tensor(r1, PS[bs][:, 0:N], 1.0 / 128.0, a2, ALU.mult, ALU.add),
                 reads=[pk(bs), "ptmp4", "rden0"], writes=["rden0"])

        I32 = mybir.dt.int32

        def stD():
            S.op("dve", lambda e: e.tensor_scalar(a2.bitcast(I32), r1.bitcast(I32), 1, None, ALU.arith_shift_right),
                 reads=["rden0", "ptmp4"], writes=["ptmp4"])
            S.op("dve", lambda e: e.tensor_scalar(r2.bitcast(I32), a2.bitcast(I32), -1.0, float(0x5f3759df), ALU.mult, ALU.add),
                 reads=["ptmp4", "rden1"], writes=["rden1"])
            for _ in range(2):
                S.op("dve", lambda e: e.tensor_tensor(a2, r2, r2, ALU.mult), reads=["rden1", "ptmp4"], writes=["ptmp4"])
                S.op("dve", lambda e: e.scalar_tensor_tensor(a2, a2, -0.5, r1, ALU.mult, ALU.mult), reads=["ptmp4", "rden0"], writes=["ptmp4"])
                S.op("dve", lambda e: e.scalar_tensor_tensor(r2, a2, 1.5, r2, ALU.add, ALU.mult), reads=["ptmp4", "rden1"], writes=["rden1"])
            S.op("dve", lambda e: e.scalar_tensor_tensor(a1, a1, sgs[:, 0:1], r2, ALU.mult, ALU.mult),
                 reads=["ptmp3", "sgs", "rden1"], writes=["ptmp3"])
            S.op("pool", lambda e: e.tensor_tensor(oT[:, h, qcols], oT[:, h, qcols], a1, ALU.mult), reads=["ptmp3", ok], writes=[ok])
        TQ.append(stB)
        TQ.append(stC)
        TQ.append(stD)

    def proj_units(h):
        hs = h % 2
        slot = wload([(d_owin[:, h * 128:(h + 1) * 128], 0), (d_owin[:, 1024 + h * 128:1024 + (h + 1) * 128], 1),
                      (d_owin[:, 3072 + h * 128:3072 + (h + 1) * 128], 2)])
        units = []
        for tc in range(3):
            cs = slice(tc * 512, (tc + 1) * 512)
            for q, (dstT, nm) in enumerate(((QTh[hs], "QTh%d" % hs), (KTh[hs], "KTh%d" % hs))):
                def u(q=q, dstT=dstT, nm=nm, tc=tc, cs=cs):
                    b = sbank()
                    proj_fm(slot, q, tc, b)
                    if tc < 2:
                        rope_evac(b, dstT[:, cs], "%s.%d" % (nm, tc), tc)
                    else:
                        copy_op("dve", dstT[:, cs], PS[b][:], [pk(b)], ["%s.%d" % (nm, tc)])
                units.append(u)

            def ug(tc=tc, cs=cs):
                b = sbank()
                proj_fm(slot, 2, tc, b)
                p1 = rot("pt", 3)
                S.op("act", lambda e: e.activation(ptmp[p1][:], PS[b][:], AF.Tanh, scale=0.5), reads=[pk(b)], writes=["ptmp%d" % p1])
                S.op("dve", lambda e: e.scalar_tensor_tensor(oT[:, h, cs], ptmp[p1][:], 1.0, PS[b][:], ALU.add, ALU.mult),
                     reads=[pk(b), "ptmp%d" % p1], writes=["oT.%d.%d" % (h, tc)])
            units.append(ug)
        return units

    kslots = [wload([(d_owin[:, 1024 + g * 512:1024 + (g + 1) * 512], 0)]) for g in range(2)]
    PQ.extend(proj_units(0))
    for g in range(2):
        for t in range(8, 12):
            b = sbank()
            proj_tm(kslots[g], t, b)
            out_toks.append(stage_out(b, o_dfk[(t - 8) * 128:(t - 7) * 128, g * 512:(g + 1) * 512]))
            if PQ:
                PQ.popleft()()
    drain(PQ)
    ckpt(9)
    S.alias(["pbuf0", "pbuf1", "pbuf2", "pbuf3", "rden0", "rden1", "g20", "qm0", "qm1", "qmz", "rdb0", "rdb1"], ["kvst0", "kvst1", "kvst2"])
    for h in range(8):
        hs = h % 2
        if h + 1 < 8:
            PQ.extend(proj_units(h + 1))
        psegs = []
        for s in range(2):
            q0 = 1024 + s * 256
            tiles = [((lambda ps_, c0=q0 + t * 128, hs=hs: KTh[hs][ps_, c0:c0 + 128]), ["KTh%d.2" % hs], 8 + 2 * s + t) for t in range(2)]
            psegs.append((slice(q0, q0 + 256), s * 256, tiles))
        diff_attn(h, hs, psegs, 2)
        for qc in range(2):
            tiles = [((lambda ps_, t=t, h=h: KTc[ps_, h, t * 128:(t + 1) * 128]), KTCK, 12 + t) for t in range(2)]
            tiles += [((lambda ps_, j=j, hs=hs: KTh[hs][ps_, j * 128:(j + 1) * 128]), ["KTh%d.%d" % (hs, j // 4)], j) for j in range(8)]
            diff_attn(h, hs, [(slice(qc * 512, (qc + 1) * 512), 0, tiles)], qc)
        drain(PQ)
        ckpt(9.1 + 0.1 * h)
    drain(TQ)

    ckpt(10)
    S.alias(["xstage0", "xstage1", "fnbc"], ["QTh0", "QTh1", "KTh0", "KTh1", "KTc", "ptmp0", "ptmp1", "ptmp2", "ptmp3", "ptmp4", "ckst1"])
    dma("sp", fnbc[:], d_fnorm.partition_broadcast(128), writes=["fnbc"])

    def final_tc(tc):
        for t in range(4 * tc, 4 * tc + 4):
            st = t % 2
            bA, bB = sbank(), sbank()
            for half, b in ((0, bA), (1, bB)):
                def trf(e, half=half, b=b, t=t):
                    for j in range(4):
                        c = half * 4 + j
                        ins = e.transpose(PS[b][:, j * 128:(j + 1) * 128], xT[:, c, t * 128:(t + 1) * 128], ident[:])
                    return ins
                S.op("pe", trf, reads=["xT.%d.%d" % (half * 4 + j, tc) for j in range(4)] + ["ident"], writes=[pk(b)])
            sk = "sstat%d" % st
            so = st * 4
            S.op("act", lambda e, bA=bA, st=st, so=so: e.activation(xstage[st][:, 0:512], PS[bA][:], AF.Square, accum_out=sstat[:, so:so + 1]),
                 reads=[pk(bA)], writes=["xstage%d" % st, sk + ".0"])
            S.op("act", lambda e, bB=bB, st=st, so=so: e.activation(xstage[st][:, 512:1024], PS[bB][:], AF.Square, accum_out=sstat[:, so + 1:so + 2]),
                 reads=[pk(bB), "xstage%d" % st], writes=["xstage%d" % st, sk + ".1"])
            S.op("dve", lambda e, so=so: e.tensor_tensor(sstat[:, so + 2:so + 3], sstat[:, so:so + 1], sstat[:, so + 1:so + 2], ALU.add),
                 reads=[sk + ".0", sk + ".1"], writes=[sk + ".2"])
            S.op("dve", lambda e, so=so: e.tensor_scalar(sstat[:, so + 2:so + 3], sstat[:, so + 2:so + 3], 1.0 / 1024.0, EPS, ALU.mult, ALU.add),
                 reads=[sk + ".2"], writes=[sk + ".2"])
            S.op("act", lambda e, so=so: e.activation(sstat[:, so + 2:so + 3], sstat[:, so + 2:so + 3], AF.Sqrt), reads=[sk + ".2"], writes=[sk + ".2"])
            S.op("dve", lambda e, so=so: e.reciprocal(sstat[:, so + 3:so + 4], sstat[:, so + 2:so + 3]), reads=[sk + ".2"], writes=[sk + ".3"])
            S.op("dve", lambda e, bA=bA, st=st, so=so: e.scalar_tensor_tensor(xstage[st][:, 0:512], PS[bA][:], sstat[:, so + 3:so + 4], fnbc[:, 0:512], ALU.mult, ALU.mult),
                 reads=[pk(bA), sk + ".3", "fnbc", "xstage%d" % st], writes=["xstage%d" % st])
            S.op("dve", lambda e, bB=bB, st=st, so=so: e.scalar_tensor_tensor(xstage[st][:, 512:1024], PS[bB][:], sstat[:, so + 3:so + 4], fnbc[:, 512:1024], ALU.mult, ALU.mult),
                 reads=[pk(bB), sk + ".3", "fnbc", "xstage%d" % st], writes=["xstage%d" % st])
            dst = o_ys[t * 128:(t + 1) * 128, :] if t < 8 else o_yp[(t - 8) * 128:(t - 7) * 128, :]
            out_toks.append(dma("sp", dst, xstage[st][:], reads=["xstage%d" % st]))

    wout_phase(1, d_owout, after_tc=final_tc)
    ckpt(11)

    S.wait_all("sp", out_toks)
    with nc.allow_non_contiguous_dma(reason="small strided parameter / state vectors"), \
            nc.allow_low_precision(reason="bf16 copies of fp32-computed values that feed bf16 matmul operands"):
        S.emit(nc, sems, dsems)
    es.close()
    return nc


_CONST = {}


def _consts():
    if not _CONST:
        C, Sg, perm = _rope_tables()
        colneg, rowB, halfsel, anti = _na_tables()
        _CONST.update(dict(c_ident=np.eye(128, dtype=np.float32), c_ropeC=C, c_ropeS=Sg, c_perm=perm,
                           c_colneg=colneg, c_rowB=rowB, c_halfsel=halfsel, c_anti=anti))
    return _CONST


def kernel(x_prompt, x_sample, c, cache_na_k, cache_na_v, state_lru, cache_diff_k, cache_diff_v, c_ctx,
           e_norm, e_ada_w, e_ada_b, e_w_in, e_rpb, e_conv_w, e_conv_b, e_lru_wa, e_lru_ba, e_lru_wx, e_lru_bx,
           e_lru_lam, e_w_out, o_norm, o_ada_w, o_ada_b, o_w_in, o_lq1, o_lk1, o_lq2, o_lk2, o_sub_g, o_w_out,
           final_norm):
    f = lambda a: np.ascontiguousarray(np.asarray(a, dtype=np.float32))
    x_prompt, x_sample, c = f(x_prompt), f(x_sample), f(c)
    shared = dict(
        e_norm=f(e_norm)[0], e_ada_w=f(e_ada_w)[0], e_ada_b=f(e_ada_b)[0], e_w_in=f(e_w_in)[0], e_rpb=f(e_rpb)[0],
        e_conv_w=f(e_conv_w)[0], e_conv_b=f(e_conv_b)[0], e_lru_wa=f(e_lru_wa)[0], e_lru_ba=f(e_lru_ba)[0],
        e_lru_wx=f(e_lru_wx)[0], e_lru_bx=f(e_lru_bx)[0], e_lru_lam=f(e_lru_lam)[0], e_w_out=f(e_w_out)[0],
        o_norm=f(o_norm)[0], o_ada_w=f(o_ada_w)[0], o_ada_b=f(o_ada_b)[0], o_w_in=f(o_w_in)[0],
        o_lq1=f(o_lq1)[0], o_lk1=f(o_lk1)[0], o_lq2=f(o_lq2)[0], o_lk2=f(o_lk2)[0], o_sub_g=f(o_sub_g)[0],
        o_w_out=f(o_w_out)[0], final_norm=f(final_norm))
    shared = {k: np.ascontiguousarray(v) for k, v in shared.items()}
    shared.update(_consts())
    cna_k, cna_v, slru = f(cache_na_k), f(cache_na_v), f(state_lru)
    cdk, cdv, cctx = f(cache_diff_k), f(cache_diff_v), f(c_ctx)
    in_maps = []
    for i in range(NCORES):
        m = dict(shared)
        m["xs"] = x_sample[i]
        m["xp"] = np.ascontiguousarray(x_prompt[2 * i:2 * i + 2].reshape(512, 1024))
        m["cvec"] = np.ascontiguousarray(np.stack([c[i], cctx], axis=0))
        m["cnk"] = np.ascontiguousarray(cna_k[i, 0].reshape(256, 512))
        m["cnv"] = np.ascontiguousarray(cna_v[i, 0].reshape(256, 512))
        m["slru"] = np.ascontiguousarray(slru[i, 0])
        m["cdk"] = np.ascontiguousarray(cdk[i, 0].reshape(256, 1024))
        m["cdv"] = np.ascontiguousarray(cdv[i, 0].reshape(256, 1024))
        in_maps.append(m)
    nc = build_nc()
    res = run_bass_kernel_spmd(nc, in_maps, core_ids=list(range(NCORES)))
    R = res.results
    y_prompt = np.concatenate([R[i]["y_p"].reshape(2, 256, 1024) for i in range(NCORES)], axis=0)
    y_sample = np.stack([R[i]["y_s"] for i in range(NCORES)], axis=0)
    na_k = np.concatenate([R[i]["na_k"].reshape(2, 1, 256, 8, 64) for i in range(NCORES)], axis=0)
    na_v = np.concatenate([R[i]["na_v"].reshape(2, 1, 256, 8, 64) for i in range(NCORES)], axis=0)
    lru = np.concatenate([R[i]["lru_o"].reshape(2, 1, 2, 512) for i in range(NCORES)], axis=0)
    df_k = np.concatenate([R[i]["df_k"].reshape(2, 1, 256, 8, 128) for i in range(NCORES)], axis=0)
    df_v = np.concatenate([R[i]["df_v"].reshape(2, 1, 256, 8, 128) for i in range(NCORES)], axis=0)
    return (y_prompt.astype(np.float32), y_sample.astype(np.float32), na_k.astype(np.float32), na_v.astype(np.float32),
            lru.astype(np.float32), df_k.astype(np.float32), df_v.astype(np.float32))
```

```python
import math
import os
from contextlib import ExitStack

import numpy as np
import concourse.bass as bass
import concourse.mybir as mybir
from concourse.bass_utils import run_bass_kernel_spmd

F32 = mybir.dt.float32
BF16 = mybir.dt.bfloat16
AF = mybir.ActivationFunctionType
ALU = mybir.AluOpType
AX = mybir.AxisListType

ENGS = ("pe", "act", "dve", "pool", "sp")
EPS = 1e-6
NEG = -30000.0
T = 1536
TS = 1024
NCORES = 8
LAM_INIT = 0.8 - 0.6 * math.exp(-0.3 * 1)


class Sched:
    N_DMA_SEMS = 32
    QSEMS = {"sp": list(range(0, 16)), "pool": list(range(16, 28)), "act": list(range(28, 32))}

    def __init__(self):
        self.ops = {e: [] for e in ENGS}
        self.cnt = {e: 0 for e in ENGS}
        self.obs = {e: {f: 0 for f in ENGS} for e in ENGS}
        self.obs_dma = {e: 0 for e in ENGS}
        self.last_w = {}
        self.readers = {}
        self.n_dma = 0
        self.dma_tok = []
        self.q_hist = {q: [] for q in self.QSEMS}
        self.inherit = {}
        self.seen = set()
        self.frozen = False

    def _need(self, eng, tok, waits):
        if tok is None:
            return
        if tok[0] == "e":
            _, e2, n, vc, dm = tok
            if eng == "pe" and e2 == "pe":
                return
            if self.obs[eng][e2] >= n:
                return
            waits.append(("e", e2, n))
            o = self.obs[eng]
            for f in ENGS:
                if vc[f] > o[f]:
                    o[f] = vc[f]
            if o[e2] < n:
                o[e2] = n
            self.obs_dma[eng] |= dm
        else:
            _, did, vc, dm = tok
            if (self.obs_dma[eng] >> did) & 1:
                return
            waits.append(("d", did))
            o = self.obs[eng]
            for f in ENGS:
                if vc[f] > o[f]:
                    o[f] = vc[f]
            self.obs_dma[eng] |= dm | (1 << did)

    def _deps(self, eng, reads, writes):
        waits = []
        for k in list(reads) + list(writes):
            if k not in self.seen:
                self.seen.add(k)
                p = k.split(".")[0]
                for t in self.inherit.get(p, ()):
                    self._need(eng, t, waits)
        for k in reads:
            self._need(eng, self.last_w.get(k), waits)
        for k in writes:
            self._need(eng, self.last_w.get(k), waits)
            for t in self.readers.get(k, ()):
                self._need(eng, t, waits)
        return waits

    def _commit(self, tok, reads, writes):
        for k in reads:
            self.readers.setdefault(k, []).append(tok)
        for k in writes:
            self.last_w[k] = tok
            self.readers[k] = []

    def op(self, eng, fn, reads=(), writes=()):
        if self.frozen:
            return None
        pr = [k for k in reads if k.startswith("ps") and k not in writes]
        if pr:
            writes = list(writes) + pr
        waits = self._deps(eng, reads, writes)
        self.cnt[eng] += 1
        n = self.cnt[eng]
        vc = dict(self.obs[eng])
        vc[eng] = n
        tok = ("e", eng, n, vc, self.obs_dma[eng])
        self.ops[eng].append((waits, fn, ("e", n)))
        self._commit(tok, reads, writes)
        return tok

    def dma(self, eng, fn, reads=(), writes=()):
        if self.frozen:
            return None
        waits = self._deps(eng, reads, writes)
        did = self.n_dma
        self.n_dma += 1
        hist = self.q_hist[eng]
        qs = self.QSEMS[eng]
        k = len(hist)
        if k >= len(qs):
            prev = hist[k - len(qs)]
            if not ((self.obs_dma[eng] >> prev) & 1):
                waits.append(("d", prev))
                self.obs_dma[eng] |= (1 << prev)
        hist.append(did)
        self.dma_tok.append((qs[k % len(qs)], 16 * (k // len(qs) + 1)))
        tok = ("d", did, dict(self.obs[eng]), self.obs_dma[eng])
        self.ops[eng].append((waits, fn, ("d", did)))
        self._commit(tok, reads, writes)
        return tok

    def alias(self, new_prefixes, old_prefixes):
        toks = []
        olds = tuple(old_prefixes)
        for k, t in self.last_w.items():
            if k.split(".")[0] in olds and t is not None:
                toks.append(t)
        for k, ts in self.readers.items():
            if k.split(".")[0] in olds:
                toks.extend(ts)
        best = {}
        dm = []
        for t in toks:
            if t[0] == "e":
                if t[1] not in best or best[t[1]][2] < t[2]:
                    best[t[1]] = t
            else:
                dm.append(t)
        toks = list(best.values()) + dm
        for p in new_prefixes:
            self.inherit[p] = self.inherit.get(p, []) + toks
            for k in [k for k in self.seen if k.split(".")[0] == p]:
                self.seen.discard(k)

    def wait_all(self, eng, toks):
        waits = []
        for t in toks:
            self._need(eng, t, waits)
        self.ops[eng].append((waits, None, None))

    def emit(self, nc, sems, dsems):
        engmap = {"pe": "tensor", "act": "scalar", "dve": "vector", "pool": "gpsimd", "sp": "sync"}
        sched = self
        with nc.Block() as block:
            for e in ENGS:
                def body(eobj, e=e):
                    for waits, fn, inc in sched.ops[e]:
                        for w in waits:
                            if w[0] == "e":
                                eobj.wait_ge(sems[w[1]], w[2])
                            else:
                                si, tgt = sched.dma_tok[w[1]]
                                eobj.wait_ge(dsems[si], tgt)
                        if fn is None:
                            continue
                        ins = fn(eobj)
                        if inc[0] == "e":
                            ins.then_inc(sems[e], 1)
                        else:
                            si, tgt = sched.dma_tok[inc[1]]
                            ins.then_inc(dsems[si], 16)
                getattr(block, engmap[e])(body)


def _rope_tables():
    p = np.arange(128)
    d = p % 64
    half = d // 32
    idx = d % 32
    m = idx % 16
    first = idx < 16
    t = np.arange(TS)
    rows = (t // 64).astype(np.float32)
    cols = (t % 64).astype(np.float32)
    inv = (10000.0 ** (-np.arange(16, dtype=np.float32) / 16.0)).astype(np.float32)
    pos = np.where(half[:, None] == 0, rows[None, :], cols[None, :]).astype(np.float32)
    ang = (pos * inv[m][:, None]).astype(np.float32)
    C = np.cos(ang).astype(np.float32)
    Sg = np.sin(ang).astype(np.float32)
    Sg = np.where(first[:, None], -Sg, Sg).astype(np.float32)
    partner = np.where(first, p + 16, p - 16)
    perm = np.zeros((128, 128), np.float32)
    perm[partner, p] = 1.0
    return C, Sg, perm


NA_J = {0: list(range(0, 6)), 1: list(range(2, 8))}


def _na_tables():
    qc = np.arange(64)
    cs = np.clip(qc - 8, 0, 48)
    kc = np.arange(64)
    inwin = (kc[:, None] >= cs[None, :]) & (kc[:, None] < cs[None, :] + 16)
    colneg = np.where(inwin, 0.0, NEG).astype(np.float32)
    colneg = np.concatenate([colneg, colneg], axis=0)
    rowB = np.zeros((128, 12, 8), np.float32)
    for q in (0, 1):
        for ji, j in enumerate(NA_J[q]):
            for krl in range(2):
                for qrl in range(8):
                    kr = 2 * j + krl
                    qr = 8 * q + qrl
                    rs = min(max(qr - 4, 0), 8)
                    ok = rs <= kr <= rs + 7
                    rowB[krl, q * 6 + ji, qrl] = 0.0 if ok else NEG
    halfsel = np.zeros((128, 128), np.float32)
    halfsel[0, :64] = 1.0
    halfsel[1, 64:] = 1.0
    anti = np.zeros((128, 128), np.float32)
    for m in range(128):
        anti[(m // 64) * 64 + 63 - (m % 64), m] = 1.0
    return colneg, rowB, halfsel, anti


def build_nc():
    nc = bass.Bass("TRN2", target_bir_lowering=False)
    S = Sched()
    es = ExitStack()

    def din(name, shape):
        return nc.dram_tensor(name, list(shape), F32, kind="ExternalInput")

    def dout(name, shape):
        return nc.dram_tensor(name, list(shape), F32, kind="ExternalOutput")

    d_xs = din("xs", [1024, 1024]).ap()
    d_xp = din("xp", [512, 1024]).ap()
    d_cvec = din("cvec", [2, 1024]).ap()
    d_cnk = din("cnk", [256, 512]).ap()
    d_cnv = din("cnv", [256, 512]).ap()
    d_slru = din("slru", [2, 512]).ap()
    d_cdk = din("cdk", [256, 1024]).ap()
    d_cdv = din("cdv", [256, 1024]).ap()
    d_enorm = din("e_norm", [1024]).ap()
    d_eadaw = din("e_ada_w", [1024, 3072]).ap()
    d_eadab = din("e_ada_b", [3072]).ap()
    d_ewin = din("e_w_in", [1024, 3072]).ap()
    d_rpb = din("e_rpb", [8, 15, 31]).ap()
    d_convw = din("e_conv_w", [4, 512]).ap()
    d_convb = din("e_conv_b", [512]).ap()
    d_wa = din("e_lru_wa", [2, 8, 64, 64]).ap()
    d_ba = din("e_lru_ba", [2, 512]).ap()
    d_wx = din("e_lru_wx", [2, 8, 64, 64]).ap()
    d_bx = din("e_lru_bx", [2, 512]).ap()
    d_lam = din("e_lru_lam", [2, 512]).ap()
    d_ewout = din("e_w_out", [1024, 1024]).ap()
    d_onorm = din("o_norm", [1024]).ap()
    d_oadaw = din("o_ada_w", [1024, 3072]).ap()
    d_oadab = din("o_ada_b", [3072]).ap()
    d_owin = din("o_w_in", [1024, 4096]).ap()
    d_lq1 = din("o_lq1", [64]).ap()
    d_lk1 = din("o_lk1", [64]).ap()
    d_lq2 = din("o_lq2", [64]).ap()
    d_lk2 = din("o_lk2", [64]).ap()
    d_subg = din("o_sub_g", [128]).ap()
    d_owout = din("o_w_out", [1024, 1024]).ap()
    d_fnorm = din("final_norm", [1024]).ap()
    d_ident = din("c_ident", [128, 128]).ap()
    d_ropeC = din("c_ropeC", [128, 1024]).ap()
    d_ropeS = din("c_ropeS", [128, 1024]).ap()
    d_perm = din("c_perm", [128, 128]).ap()
    d_colneg = din("c_colneg", [128, 64]).ap()
    d_rowB = din("c_rowB", [128, 12, 8]).ap()
    d_halfsel = din("c_halfsel", [128, 128]).ap()
    d_anti = din("c_anti", [128, 128]).ap()

    o_ys = dout("y_s", [1024, 1024]).ap()
    o_yp = dout("y_p", [512, 1024]).ap()
    o_nak = dout("na_k", [512, 512]).ap()
    o_nav = dout("na_v", [512, 512]).ap()
    o_lru = dout("lru_o", [2, 2, 512]).ap()
    o_dfk = dout("df_k", [512, 1024]).ap()
    o_dfv = dout("df_v", [512, 1024]).ap()

    padr_t = nc.dram_tensor("padr", [8, 24, 128], F32)

    class Mem:
        def __init__(self, base, limit):
            self.p = base
            self.limit = limit
            self.n = 0

        def at(self, off, shape, dt, name=None):
            self.n += 1
            nm = "%s_%d" % (name or "t", self.n)
            n = int(np.prod(shape[1:])) * (4 if dt == F32 else 2)
            assert off % 4 == 0 and off + n <= self.limit, (nm, off, n, self.limit)
            return nc.alloc_sbuf_tensor_at(nm, list(shape), dt, offset=off)

        def new(self, shape, dt, name=None):
            n = int(np.prod(shape[1:])) * (4 if dt == F32 else 2)
            n = (n + 63) // 64 * 64
            off = self.p
            self.p += n
            return self.at(off, shape, dt, name)

        def region(self, nbytes):
            off = self.p
            self.p += (nbytes + 63) // 64 * 64
            assert self.p <= self.limit, self.p
            return off

    M = Mem(16512, 229312)
    xT = M.new([128, 8, T], F32, "xT")
    ring = [M.new([128, 8, 512], BF16, "ring") for _ in range(3)]
    hT = M.new([128, 8, T], BF16, "hT")
    oT = M.new([128, 8, T], BF16, "oT")
    ident = M.new([128, 128], F32, "ident")
    identb = M.new([128, 128], BF16, "identb")
    onesb = M.new([128, 128], BF16, "onesb")
    ones32 = M.new([128, 64], F32, "ones32")
    permb = M.new([128, 128], BF16, "permb")
    antib = M.new([128, 128], BF16, "antib")
    sel64 = M.new([128, 64], BF16, "sel64")
    halfsel = M.new([128, 128], BF16, "halfsel")
    rowB = M.new([128, 12, 8], BF16, "rowB")
    colneg = M.new([128, 64], BF16, "colneg")
    wblk = M.new([128, 16, 128], BF16, "wblk")
    pvec = M.new([128, 84], F32, "pvec")
    lstT = M.new([16, 128], F32, "lstT")
    pvec2 = M.new([128, 2, 24], F32, "pvec2")
    cTb = M.new([128, 8, 2], BF16, "cTb")
    normg = [pvec[:, 0:8], pvec[:, 8:16]]
    mT = [M.new([128, 24, 2], F32, "mT") for _ in range(2)]
    gsc = [M.new([128, 8, 2], F32, "gsc") for _ in range(2)]
    cw = pvec[:, 16:32].rearrange("p (j c) -> p c j", c=4)
    cb = pvec[:, 32:36]
    lba = pvec[:, 36:44].rearrange("p (d c) -> p d c", c=4)
    lbx = pvec[:, 44:52].rearrange("p (d c) -> p d c", c=4)
    llam = pvec[:, 52:60].rearrange("p (d c) -> p d c", c=4)
    cT = pvec[:, 68:84].rearrange("p (s c) -> p c s", c=8)
    m8sp = M.new([128, 2, 4], F32, "m8sp")
    hm8 = M.new([128, 2, 4], F32, "hm8")
    hba = M.new([128, 2, 4], F32, "hba")
    hbx = M.new([128, 2, 4], F32, "hbx")
    dg = M.new([128, 16, 128], BF16, "dg")
    h0 = pvec[:, 60:68].rearrange("p (d c) -> p d c", c=4)
    lst = M.new([128, 16], F32, "lst")
    lsum = M.new([128, 2], F32, "lsum")
    neglam = M.new([128, 1], F32, "neglam")
    sgs = M.new([128, 1], F32, "sgs")
    sstat = M.new([128, 8], F32, "sstat")
    U1 = M.region(26624)
    U2 = M.region(28672)
    U3 = M.region(12288)
    W = M.p
    WSZ = M.limit - W
    assert WSZ >= 8192, WSZ
    lqk = M.at(U3, [128, 4, 64], F32, "lqk")
    lprod = M.at(U3 + 1024, [128, 2, 64], F32, "lprod")
    zt = M.at(U3 + 1536, [128, 192], F32, "zt")
    rp = M.at(U3 + 2304, [8, 15, 31], F32, "rp")
    pstg = M.at(U3 + 4224, [84, 128], F32, "pstg")
    rpr = M.at(U3 + 4736, [8, 15, 31], F32, "rpr")
    pstg2 = M.at(U3 + 6656, [48, 128], F32, "pstg2")
    xstage = [M.at(U1 + i * 4096, [128, 1024], F32, "xstage") for i in range(2)]
    fnbc = M.at(U1 + 8192, [128, 1024], F32, "fnbc")
    adab = [M.at(U2 + i * 2048, [2, 512], F32, "adab") for i in range(2)]
    msb = [M.at(U2 + 4096 + i * 2048, [2, 512], F32, "msb") for i in range(2)]
    rstd = M.at(U1 + 12288, [128, T], F32, "rstd")
    sq = [M.at(U1 + 18432 + i * 1024, [128, 512], BF16, "sq") for i in range(3)]
    ntmp = [M.at(U1 + 21504 + i * 2048, [128, 512], F32, "ntmp") for i in range(2)] + [None]
    QT = M.at(U1, [128, 4, T], BF16, "QT")
    KT = M.at(U1 + 12288, [128, 4, 1792], BF16, "KT")
    VA = M.at(U2, [128, 14, 8, 66], BF16, "VA")
    Tb = [M.at(U2 + 14848 + i * 3072, [128, 24, 64], BF16, "Tb") for i in range(2)]
    xcb = M.at(U2 + 20992, [128, T], BF16, "xcb")
    Traw = M.at(U2 + 24064, [128, 24, 64], BF16, "Traw")
    xbT = M.at(U3, [128, 4, T], BF16, "xbT")
    hT_off = nc.lookup_mloc(hT).addr
    la = M.at(hT_off, [128, T], F32, "la")
    lu = M.at(hT_off + 6144, [128, T], F32, "lu")
    lhf = M.at(hT_off + 12288, [128, T], F32, "lhf")
    lhb = M.at(hT_off + 18432, [128, T], F32, "lhb")
    V1 = M.at(U2, [128, 14, 1024], BF16, "V1")
    QTh = [M.at(U1 + i * 3072, [128, T], BF16, "QTh") for i in range(2)]
    KTh = [M.at(U1 + 6144 + i * 3072, [128, T], BF16, "KTh") for i in range(2)]
    KTc = M.at(U1 + 12288, [128, 8, 256], BF16, "KTc")
    ptmp = [M.at(U1 + 16384 + i * 2048, [128, 512], F32, "ptmp") for i in range(5)]
    ropeC = M.at(U3, [128, 1024], F32, "ropeC")
    ropeS = M.at(U3 + 4096, [128, 1024], F32, "ropeS")
    rawb = [M.at(U3 + 8192 + i * 1024, [128, 512], BF16, "rawb") for i in range(2)]
    kvst = [M.at(W + i * 2048, [128, 512], F32, "kvst") for i in range(3)]
    ckst0 = M.at(U2 + 24064, [128, 2, 512], F32, "ckst0")
    ckst1 = M.at(U1 + 16384, [128, 2, 1024], F32, "ckst1")
    pbuf = [M.at(W + i * 1024, [128, 512], BF16, "pbuf") for i in range(4)]
    rden = [M.at(W + 4096 + i * 2048, [128, 512], F32, "rden") for i in range(2)]
    rdb = [M.at(W + 4096 + i * 1024, [128, 512], BF16, "rdb") for i in range(2)]
    g2 = [M.at(W + 6144, [128, 512], F32, "g2")]
    qm = [M.at(W + 3072, [128, 512], BF16, "qm"), M.at(W + 8192, [128, 512], BF16, "qm")]

    PS = [es.enter_context(nc.psum_tensor("ps%d" % i, [128, 512], F32)) for i in range(8)]
    sems = {e: es.enter_context(nc.semaphore("s_" + e)) for e in ENGS}
    dsems = [es.enter_context(nc.semaphore("d%d" % i)) for i in range(Sched.N_DMA_SEMS)]

    rr = {"s": 0, "o": 0, "ring": 0, "pb": 0, "kv": 0, "sq": 0, "nt": 0, "g2": 0, "rd": 0, "pt": 0, "rb": 0}

    def sbank():
        rr["s"] = (rr["s"] + 1) % 4
        return rr["s"]

    def rot(name, n):
        rr[name] = (rr[name] + 1) % n
        return rr[name]

    def pk(b):
        return "ps%d" % b

    alt = {"n": 0}

    def evac_eng():
        alt["n"] += 1
        return "dve" if alt["n"] % 2 else "act"

    def copy_op(eng, out, in_, reads, writes):
        if eng == "act":
            S.op("act", lambda e: e.copy(out, in_), reads=reads, writes=writes)
        elif eng == "dve":
            S.op("dve", lambda e: e.tensor_copy(out, in_), reads=reads, writes=writes)
        else:
            S.op("pool", lambda e: e.tensor_copy(out, in_), reads=reads, writes=writes)

    def dma(q, out, in_, reads=(), writes=()):
        return S.dma(q, lambda e: e.dma_start(out=out, in_=in_), reads=reads, writes=writes)

    wstate = {"n": 0}

    def wload(parts):
        slot = wstate["n"] % 3
        wstate["n"] += 1
        for (src, q0) in parts:
            ncols = src.shape[1]
            nq = (ncols + 127) // 128
            keys = ["ring%d.q%d" % (slot, q0 + i) for i in range(nq)]
            dma("pool", ring[slot][:, :, q0 * 128:q0 * 128 + ncols], src.rearrange("(c p) n -> p c n", p=128), writes=keys)
        return slot

    def rkeys(slot, qs=(0, 1, 2, 3)):
        return ["ring%d.q%d" % (slot, q) for q in qs]


    def norm_stats(tc):
        b = sbank()
        for c in range(8):
            i = rot("sq", 3)
            S.op("act", lambda e, i=i, c=c, tc=tc: e.activation(sq[i][:], xT[:, c, tc * 512:(tc + 1) * 512], AF.Square),
                 reads=["xT.%d.%d" % (c, tc)], writes=["sq%d" % i])
            S.op("pe", lambda e, i=i, c=c, b=b: e.matmul(PS[b][:], onesb[:], sq[i][:], start=(c == 0), stop=(c == 7)),
                 reads=["sq%d" % i, "onesb"], writes=[pk(b)])
        rs = rstd[:, tc * 512:(tc + 1) * 512]
        S.op("dve", lambda e, rs=rs, b=b: e.tensor_scalar(rs, PS[b][:], 1.0 / 1024.0, EPS, ALU.mult, ALU.add),
             reads=[pk(b)], writes=["rstd.%d" % tc])
        S.op("act", lambda e, rs=rs: e.activation(rs, rs, AF.Sqrt), reads=["rstd.%d" % tc], writes=["rstd.%d" % tc])
        S.op("dve", lambda e, rs=rs: e.reciprocal(rs, rs), reads=["rstd.%d" % tc], writes=["rstd.%d" % tc])

    def norm_apply(L, tc):
        s = 0 if tc < 2 else 1
        rs = rstd[:, tc * 512:(tc + 1) * 512]
        for c in range(8):
            i = rot("nt", 2)
            S.op("dve", lambda e, i=i, c=c, tc=tc, rs=rs: e.tensor_tensor(ntmp[i][:], xT[:, c, tc * 512:(tc + 1) * 512], rs, ALU.mult),
                 reads=["xT.%d.%d" % (c, tc), "rstd.%d" % tc], writes=["ntmp%d" % i])
            dst = hT[:, c, tc * 512:(tc + 1) * 512]
            S.op("act", lambda e, i=i, dst=dst, c=c, s=s: e.activation(dst, ntmp[i][:], AF.Identity, bias=mT[L][:, c, s:s + 1],
                                                                      scale=gsc[L][:, c, s:s + 1]),
                 reads=["ntmp%d" % i, "mT%d" % L, "gsc%d.%d" % (L, s)], writes=["hT.%d.%d" % (c, tc)])

    def norm_phase(L, tcs=(0, 1, 2)):
        for tc in tcs:
            norm_stats(tc)
            norm_apply(L, tc)

    def mod_units(L, d_adaw):
        mtb = 4 + L
        units = []
        for g in range(6):
            def u(g=g):
                slot = wload([(d_adaw[:, g * 512:(g + 1) * 512], 0)])

                def mm(e):
                    ins = None
                    for j in range(4):
                        col = (g * 4 + j) * 2
                        for c in range(8):
                            ins = e.matmul(PS[mtb][:, col:col + 2], ring[slot][:, c, j * 128:(j + 1) * 128], cTb[:, c, :],
                                           start=(c == 0), stop=(c == 7))
                    return ins
                S.op("pe", mm, reads=["cTb"] + rkeys(slot), writes=[pk(mtb)])
            units.append(u)
        return units

    def mod_finish(L):
        mtb = 4 + L
        bb_ = pvec2[:, L, :].unsqueeze(2).to_broadcast([128, 24, 2])
        S.op("dve", lambda e: e.tensor_tensor(mT[L][:], PS[mtb][:, 0:48].rearrange("p (a s) -> p a s", s=2), bb_, ALU.add),
             reads=[pk(mtb), "pvec2"], writes=["mT%d" % L])
        for s_ in range(2):
            S.op("dve", lambda e, s_=s_: e.scalar_tensor_tensor(gsc[L][:, :, s_], mT[L][:, 8:16, s_], 1.0, normg[L],
                                                               ALU.add, ALU.mult),
                 reads=["mT%d" % L, "pvec"], writes=["gsc%d.%d" % (L, s_)])

    dma("sp", ident[:], d_ident, writes=["ident"])
    rows = [(0, d_enorm.rearrange("(c p) -> c p", p=128), 8), (8, d_onorm.rearrange("(c p) -> c p", p=128), 8),
            (16, d_convw.rearrange("j (c p) -> (j c) p", p=128), 16), (32, d_convb.rearrange("(c p) -> c p", p=128), 4),
            (36, d_ba.rearrange("d (c p) -> (d c) p", p=128), 8), (44, d_bx.rearrange("d (c p) -> (d c) p", p=128), 8),
            (52, d_lam.rearrange("d (c p) -> (d c) p", p=128), 8), (60, d_slru.rearrange("d (c p) -> (d c) p", p=128), 8),
            (68, d_cvec.rearrange("s (c p) -> (s c) p", p=128), 16)]
    for (r0, src_, nr) in rows:
        dma("sp", pstg[r0:r0 + nr, :], src_, writes=["pstg.%d" % r0])
    S.op("pe", lambda e: e.transpose(PS[0][:, 0:84], pstg[0:84, :], ident[0:84, 0:84]),
         reads=["pstg.%d" % r0 for (r0, _, _) in rows] + ["ident"], writes=[pk(0)])
    S.op("dve", lambda e: e.tensor_copy(pvec[:], PS[0][:, 0:84]), reads=[pk(0)], writes=["pvec"])
    dma("sp", pstg2[0:24, :], d_eadab.rearrange("(c p) -> c p", p=128), writes=["pstg2.0"])
    dma("sp", pstg2[24:48, :], d_oadab.rearrange("(c p) -> c p", p=128), writes=["pstg2.1"])
    S.op("pe", lambda e: e.transpose(PS[1][:, 0:48], pstg2[0:48, :], ident[0:48, 0:48]),
         reads=["pstg2.0", "pstg2.1", "ident"], writes=[pk(1)])
    S.op("dve", lambda e: e.tensor_copy(pvec2[:].rearrange("p l a -> p (l a)"), PS[1][:, 0:48]), reads=[pk(1)], writes=["pvec2"])
    S.op("act", lambda e: e.activation(cTb[:], cT, AF.Silu), reads=["pvec"], writes=["cTb"])
    S.op("pool", lambda e: e.memset(onesb[:], 1.0), writes=["onesb"])
    m0 = mod_units(0, d_eadaw)
    for t in range(12):
        src = d_xs[t * 128:(t + 1) * 128, :] if t < 8 else d_xp[(t - 8) * 128:(t - 7) * 128, :]
        st = t % 2
        dma("sp", xstage[st][:], src, writes=["xstage%d" % st])
        for half in range(2):
            b = sbank()

            def tr(e, st=st, half=half, b=b):
                for j in range(4):
                    c = half * 4 + j
                    ins = e.transpose(PS[b][:, j * 128:(j + 1) * 128], xstage[st][:, c * 128:(c + 1) * 128], ident[:])
                return ins
            S.op("pe", tr, reads=["xstage%d" % st, "ident"], writes=[pk(b)])
            dst = xT[:, half * 4:half * 4 + 4, t * 128:(t + 1) * 128]
            srcp = PS[b][:].rearrange("p (j n) -> p j n", n=128)
            copy_op("dve" if half == 0 else "act", dst, srcp, [pk(b)], ["xT.%d.%d" % (half * 4 + j, t // 4) for j in range(4)])
        if t % 2 == 1:
            m0[t // 2]()
        if t % 4 == 3:
            norm_stats(t // 4)
    mod_finish(0)
    S.op("dve", lambda e: e.tensor_copy(identb[:], ident[:]), reads=["ident"], writes=["identb"])
    S.op("pool", lambda e: e.memset(ones32[:], 1.0), writes=["ones32"])
    S.op("pool", lambda e: e.memset(zt[:], 0.0), writes=["zt"])
    S.op("pool", lambda e: e.memset(wblk[:], 0.0), writes=["wblk"])
    dma("sp", rp[:], d_rpb, writes=["rp"])
    for i, dq in enumerate((d_lq1, d_lk1, d_lq2, d_lk2)):
        dma("sp", lqk[:, i, :], dq.partition_broadcast(128), writes=["lqk%d" % i])
    dma("sp", sgs[:], d_subg.rearrange("(p o) -> p o", o=1), writes=["sgs"])
    dma("pool", permb[:], d_perm, writes=["permb"])
    dma("pool", antib[:], d_anti, writes=["antib"])
    dma("pool", halfsel[:], d_halfsel, writes=["halfsel"])
    dma("pool", rowB[:], d_rowB, writes=["rowB"])
    dma("pool", colneg[:], d_colneg, writes=["colneg"])
    for g, dw in enumerate((d_wa, d_wx)):
        for par in range(2):
            for d in range(2):
                src = dw[d].rearrange("(c two) k m -> two k c m", two=2)[par]
                dst = wblk[par * 64:(par + 1) * 64, g * 8 + d * 4:g * 8 + d * 4 + 4, par * 64:(par + 1) * 64]
                dma("pool", dst, src, reads=["wblk"], writes=["wblk.%d%d%d" % (g, par, d)])
    WBK = ["wblk"] + ["wblk.%d%d%d" % (g, par, d) for g in range(2) for par in range(2) for d in range(2)]

    S.op("act", lambda e: e.activation(m8sp[:], llam, AF.Exp, scale=-1.0), reads=["pvec"], writes=["m8sp"])
    S.op("act", lambda e: e.activation(m8sp[:], m8sp[:], AF.Ln, bias=1.0), reads=["m8sp"], writes=["m8sp"])
    S.op("dve", lambda e: e.tensor_scalar(m8sp[:], m8sp[:], -8.0, None, ALU.mult), reads=["m8sp"], writes=["m8sp"])
    S.op("dve", lambda e: e.tensor_scalar(hm8[:], m8sp[:], 0.5, None, ALU.mult), reads=["m8sp"], writes=["hm8"])
    S.op("dve", lambda e: e.tensor_scalar(hba[:], lba, 0.5, None, ALU.mult), reads=["pvec"], writes=["hba"])
    S.op("dve", lambda e: e.tensor_scalar(hbx[:], lbx, 0.5, None, ALU.mult), reads=["pvec"], writes=["hbx"])
    for c_ in range(4):
        for j_ in range(4):
            S.op("dve", lambda e, c_=c_, j_=j_: e.tensor_scalar(dg[:, c_ * 4 + j_, :], ident[:], cw[:, c_, j_:j_ + 1], None, ALU.mult),
                 reads=["ident", "pvec"], writes=["dg.%d" % (c_ * 4 + j_)])

    S.op("dve", lambda e: e.tensor_tensor(lprod[:, 0, :], lqk[:, 0, :], lqk[:, 1, :], ALU.mult), reads=["lqk0", "lqk1"], writes=["lprod0"])
    S.op("dve", lambda e: e.tensor_tensor(lprod[:, 1, :], lqk[:, 2, :], lqk[:, 3, :], ALU.mult), reads=["lqk2", "lqk3"], writes=["lprod1"])
    S.op("dve", lambda e: e.reduce_sum(lsum[:], lprod[:], axis=AX.X), reads=["lprod0", "lprod1"], writes=["lsum"])
    S.op("act", lambda e: e.activation(lsum[:], lsum[:], AF.Exp), reads=["lsum"], writes=["lsum"])
    S.op("dve", lambda e: e.tensor_tensor(neglam[:], lsum[:, 1:2], lsum[:, 0:1], ALU.subtract), reads=["lsum"], writes=["neglam"])
    S.op("dve", lambda e: e.tensor_scalar(neglam[:], neglam[:], -LAM_INIT, None, ALU.add), reads=["neglam"], writes=["neglam"])
    S.op("dve", lambda e: e.tensor_scalar(sgs[:], sgs[:], 0.5 * (1.0 - LAM_INIT), None, ALU.mult), reads=["sgs"], writes=["sgs"])

    S.op("dve", lambda e: e.tensor_scalar(rpr[:], rp[:, :, ::-1], 8.0, None, ALU.mult), reads=["rp"], writes=["rpr"])
    padr_flat = bass.AP(padr_t, 0, [[192, 128], [1, 192]])
    dma("sp", padr_flat, zt[:], reads=["zt"], writes=["padr"])
    padr_mid = bass.AP(padr_t, 4 * 128 + 48, [[24 * 128, 8], [128, 15], [1, 31]])
    dma("sp", padr_mid, rpr[:], reads=["rpr", "padr"], writes=["padr.0"])
    PADK = ["padr", "padr.0"]

    def load_tb(h):
        slot = h % 2
        k = "Tb%d" % slot
        src0 = bass.AP(padr_t, h * 24 * 128, [[1, 64], [128, 24], [1, 64]])
        src1 = bass.AP(padr_t, h * 24 * 128 + 128, [[1, 64], [128, 23], [1, 64]])
        dma("pool", Traw[0:64, :, :], src0, reads=PADK + ["Traw"], writes=["Traw.a"])
        dma("pool", Traw[64:128, 0:23, :], src1, reads=PADK + ["Traw"], writes=["Traw.b"])
        cn = colneg[:].unsqueeze(1).to_broadcast([128, 8, 64])
        for n3 in range(3):
            b = sbank()
            S.op("pe", lambda e, b=b, n3=n3: e.matmul(PS[b][:], antib[:], Traw[:, n3 * 8:(n3 + 1) * 8, :], start=True, stop=True),
                 reads=["Traw", "Traw.a", "Traw.b", "antib"], writes=[pk(b)])
            S.op("dve", lambda e, b=b, n3=n3: e.tensor_tensor(Tb[slot][:, n3 * 8:(n3 + 1) * 8, :],
                                                             PS[b][:].rearrange("p (a q) -> p a q", q=64), cn, ALU.add),
                 reads=[pk(b), "colneg"], writes=[k + ".%d" % n3])
        return slot

    HKEYS = lambda tc: ["hT.%d.%d" % (c, tc) for c in range(8)]

    def proj_fm(slot, q, tc, b, ncols=128):
        def mm(e):
            for c in range(8):
                ins = e.matmul(PS[b][:], ring[slot][:, c, q * 128:(q + 1) * 128], hT[:, c, tc * 512:(tc + 1) * 512],
                               start=(c == 0), stop=(c == 7))
            return ins
        S.op("pe", mm, reads=HKEYS(tc) + rkeys(slot, (q,)), writes=[pk(b)])

    def proj_tm(slot, t, b):
        def mm(e):
            for c in range(8):
                ins = e.matmul(PS[b][:], hT[:, c, t * 128:(t + 1) * 128], ring[slot][:, c, :], start=(c == 0), stop=(c == 7))
            return ins
        S.op("pe", mm, reads=HKEYS(t // 4) + rkeys(slot), writes=[pk(b)])

    def stage_out(b, dst, extra=()):
        i = rot("kv", 3)
        S.op("act", lambda e, i=i, b=b: e.copy(kvst[i][:], PS[b][:]), reads=[pk(b)] + list(extra), writes=["kvst%d" % i])
        return dma("sp", dst, kvst[i][:], reads=["kvst%d" % i])

    out_toks = []
    KSTOP = float(os.environ.get("KSTOP", "99"))

    def ckpt(k):
        if KSTOP <= k:
            S.frozen = True

    ckpt(3)
    for tc_ in range(3):
        norm_apply(0, tc_)
    ckpt(4)
    S.alias(["QT", "KT"], ["rstd", "sq0", "sq1", "sq2", "ntmp0", "ntmp1", "xstage0", "xstage1", "fnbc"])
    S.alias(["xbT"], ["lqk0", "lqk1", "lqk2", "lqk3", "lprod0", "lprod1", "zt", "rp", "rpr", "pstg", "pstg2"])

    S.op("pool", lambda e: e.memset(VA[:, :, :, 64:65], 1.0), writes=["VA.ones"])
    for t in range(2):
        dma("pool", VA[:, 12 + t, :, 0:64], d_cnv[t * 128:(t + 1) * 128, :].rearrange("p (h d) -> p h d", d=64), writes=["VA.%d" % (12 + t)])
    dma("sp", ckst0[:], d_cnk.rearrange("(t p) f -> p t f", p=128), writes=["ckst0"])
    for t in range(2):
        b = sbank()

        def trk(e, t=t, b=b):
            for oc in range(4):
                ins = e.transpose(PS[b][:, oc * 128:(oc + 1) * 128], ckst0[:, t, oc * 128:(oc + 1) * 128], ident[:])
            return ins
        S.op("pe", trk, reads=["ckst0", "ident"], writes=[pk(b)])
        copy_op("dve", KT[:, :, 1536 + t * 128:1536 + (t + 1) * 128], PS[b][:].rearrange("p (j n) -> p j n", n=128),
                [pk(b)], ["KT.c%d" % t])

    ckpt(4.1)
    m1 = mod_units(1, d_oadaw)
    for gi, gname in enumerate(("q", "k", "v", "ga", "xb", "gb")):
        slot = wload([(d_ewin[:, gi * 512:(gi + 1) * 512], 0)])
        if gname in ("q", "k", "xb", "ga", "gb"):
            for oc in range(4):
                for tc in range(3):
                    b = sbank()
                    proj_fm(slot, oc, tc, b)
                    cs = slice(tc * 512, (tc + 1) * 512)
                    if gname == "q":
                        copy_op(evac_eng(), QT[:, oc, cs], PS[b][:], [pk(b)], ["QT.%d.%d" % (oc, tc)])
                    elif gname == "k":
                        copy_op(evac_eng(), KT[:, oc, cs], PS[b][:], [pk(b)], ["KT.%d.%d" % (oc, tc)])
                    elif gname == "xb":
                        copy_op(evac_eng(), xbT[:, oc, cs], PS[b][:], [pk(b)], ["xbT.%d.%d" % (oc, tc)])
                    else:
                        cc = oc if gname == "ga" else 4 + oc
                        S.op("act", lambda e, cc=cc, cs=cs, b=b: e.activation(oT[:, cc, cs], PS[b][:], AF.Silu),
                             reads=[pk(b)], writes=["oT.%d.%d" % (cc, tc)])
        if gname == "k":
            for t in range(8, 12):
                b = sbank()
                proj_tm(slot, t, b)
                out_toks.append(stage_out(b, o_nak[(t - 8) * 128:(t - 7) * 128, :]))
        if gname == "v":
            for t in range(12):
                b = sbank()
                proj_tm(slot, t, b)
                if os.environ.get("KVAR", "") != "A":
                    S.op("dve", lambda e, t=t, b=b: e.tensor_copy(VA[:, t, :, 0:64], PS[b][:].rearrange("p (h d) -> p h d", d=64)),
                         reads=[pk(b)], writes=["VA.%d" % t])
                if t >= 8:
                    out_toks.append(stage_out(b, o_nav[(t - 8) * 128:(t - 7) * 128, :], extra=["VA.%d" % t]))
        m1[gi]()
        ckpt(4.2 + 0.1 * gi)
    mod_finish(1)

    ckpt(5)
    S.alias(["la", "lu", "lhf", "lhb"], ["hT"])
    S.alias(["pbuf0", "pbuf1", "pbuf2", "pbuf3", "rden0", "rden1", "g20", "qm0", "qm1", "qmz", "rdb0", "rdb1"], ["kvst0", "kvst1", "kvst2"])

    SEGS = [(0, 1024), (1024, 1280), (1280, 1536)]
    CWK = ["pvec"]

    def lru_chunk(c):
        XB = ["xbT.%d.%d" % (c, tc) for tc in range(3)]
        DG = ["dg.%d" % (c * 4 + j) for j in range(4)]
        for tc in range(3):
            c0 = tc * 512
            pieces = [(c0, c0 + 512, 0, 1024)] if tc < 2 else [(1024, 1280, 1024, 1280), (1280, 1536, 1280, 1536)]
            b = sbank()

            def cv(e, pieces=pieces, c0=c0, b=b):
                ins = None
                for (a, bb, lo, hi) in pieces:
                    e.matmul(PS[b][:, a - c0:bb - c0], dg[:, c * 4 + 2, :], xbT[:, c, a:bb], start=True, stop=False)
                    taps = ((0, -2), (1, -1), (3, 1))
                    for ti, (j, off) in enumerate(taps):
                        a2 = max(a, lo - off)
                        b2 = min(bb, hi - off)
                        ins = e.matmul(PS[b][:, a2 - c0:b2 - c0], dg[:, c * 4 + j, :], xbT[:, c, a2 + off:b2 + off],
                                       start=False, stop=(ti == 2))
                return ins
            S.op("pe", cv, reads=XB + DG, writes=[pk(b)])
            S.op("act", lambda e, b=b, c0=c0: e.activation(xcb[:, c0:c0 + 512], PS[b][:], AF.Identity, bias=cb[:, c:c + 1]),
                 reads=[pk(b), "pvec"], writes=["xcb"])
            yield
        for d in range(2):
            for tc in range(3):
                cs = slice(tc * 512, (tc + 1) * 512)
                b1 = sbank()
                S.op("pe", lambda e, b1=b1, cs=cs, d=d: e.matmul(PS[b1][:], wblk[:, 0 * 8 + d * 4 + c, :], xcb[:, cs], start=True, stop=True),
                     reads=["xcb"] + WBK, writes=[pk(b1)])
                S.op("act", lambda e, b1=b1, cs=cs, d=d: e.activation(la[:, cs], PS[b1][:], AF.Tanh, bias=hba[:, d, c:c + 1], scale=0.5),
                     reads=[pk(b1), "hba"], writes=["la"])
                yield
                b2 = sbank()
                S.op("pe", lambda e, b2=b2, cs=cs, d=d: e.matmul(PS[b2][:], wblk[:, 1 * 8 + d * 4 + c, :], xcb[:, cs], start=True, stop=True),
                     reads=["xcb"] + WBK, writes=[pk(b2)])
                S.op("act", lambda e, b2=b2, cs=cs, d=d: e.activation(lu[:, cs], PS[b2][:], AF.Tanh, bias=hbx[:, d, c:c + 1], scale=0.5),
                     reads=[pk(b2), "hbx"], writes=["lu"])
                yield
            dst = lhf if d == 0 else lhb
            dk = "lhf" if d == 0 else "lhb"
            S.op("act", lambda e, d=d: e.activation(la[:], la[:], AF.Exp, scale=hm8[:, d, c:c + 1], bias=hm8[:, d, c:c + 1]),
                 reads=["la", "hm8"], writes=["la"])
            yield
            S.op("dve", lambda e: e.scalar_tensor_tensor(lu[:], lu[:], 1.0, xcb[:], ALU.add, ALU.mult), reads=["lu", "xcb"], writes=["lu"])
            yield
            S.op("act", lambda e, dst=dst: e.activation(dst[:], la[:], AF.Square), reads=["la"], writes=[dk])
            yield
            S.op("act", lambda e, dst=dst: e.activation(dst[:], dst[:], AF.Sqrt, scale=-0.25, bias=0.25), reads=[dk], writes=[dk])
            yield
            S.op("dve", lambda e, dst=dst: e.tensor_tensor(lu[:], lu[:], dst[:], ALU.mult), reads=["lu", dk], writes=["lu"])
            yield
            for si, (lo, hi) in enumerate(SEGS):
                init = h0[:, d, c:c + 1] if si == 0 else 0.0
                if d == 0:
                    S.op("dve", lambda e, lo=lo, hi=hi, init=init, dst=dst: e.tensor_tensor_scan(
                        dst[:, lo:hi], la[:, lo:hi], lu[:, lo:hi], init, ALU.mult, ALU.add),
                        reads=["la", "lu", "pvec", dk], writes=[dk])
                    yield
                else:
                    S.op("dve", lambda e, lo=lo, hi=hi, init=init, dst=dst: e.tensor_tensor_scan(
                        dst[:, lo:hi][:, ::-1], la[:, lo:hi][:, ::-1], lu[:, lo:hi][:, ::-1], init, ALU.mult, ALU.add),
                        reads=["la", "lu", "pvec", dk], writes=[dk])
                    yield
            for s in range(2):
                lo, hi = SEGS[1 + s]
                col = hi - 1 if d == 0 else lo
                S.op("pool", lambda e, s=s, d=d, col=col, dst=dst: e.tensor_copy(lst[:, (s * 2 + d) * 4 + c:(s * 2 + d) * 4 + c + 1], dst[:, col:col + 1]),
                     reads=[dk], writes=["lst.%d%d%d" % (c, s, d)])
            yield
        S.op("pool", lambda e: e.tensor_tensor(lhf[:], lhf[:], lhb[:], ALU.add), reads=["lhf", "lhb"], writes=["lhf"])
        yield
        OK_ = ["oT.%d.%d" % (4 + c, tc) for tc in range(3)]
        S.op("pool", lambda e: e.tensor_tensor(oT[:, 4 + c, :], oT[:, 4 + c, :], lhf[:], ALU.mult), reads=["lhf"] + OK_, writes=OK_)
        yield

    def lru_all():
        for c in range(4):
            yield from lru_chunk(c)
    lru_it = lru_all()

    def pump():
        try:
            next(lru_it)
        except StopIteration:
            pass

    from collections import deque
    LQ = deque()

    def attn64(h, segs, tc, na=None):
        ch, pb = h // 2, (h % 2) * 64
        ob = 4 + rot("o", 2)
        qi = h % 2
        N = sum(q_.stop - q_.start for (q_, _, _) in segs)
        qcols = slice(segs[0][0].start, segs[-1][0].stop)
        S.op("pool", lambda e: e.tensor_copy(qm[qi][pb:pb + 64, 0:N], QT[pb:pb + 64, ch, qcols]),
             reads=["QT.%d.%d" % (ch, tc), "qmz"], writes=["qm%d" % qi])
        flat = []
        for (q_, off, tiles) in segs:
            for i, t_ in enumerate(tiles):
                flat.append((off, q_.stop - q_.start, t_, i == 0, i == len(tiles) - 1))
        n = len(flat)
        sb_ = [None] * n

        def s_mm(i):
            off, Ns, (kc0, vt, j), _, _ = flat[i]
            b = sbank()
            sb_[i] = b
            kkey = "KT.c%d" % ((kc0 - 1536) // 128) if kc0 >= 1536 else "KT.%d.%d" % (ch, kc0 // 512)

            def mm(e):
                last = (na is None) or (j is None)
                ins = e.matmul(PS[b][:, 0:Ns], KT[:, ch, kc0:kc0 + 128], qm[qi][:, off:off + Ns], start=True, stop=last)
                if not last:
                    tbs, qc = na
                    ai0 = 2 * j - 8 * qc + 11
                    rhs = Tb[tbs][:, ai0 - 7:ai0 + 1, :][:, ::-1, :]
                    e.matmul(PS[b][:, 0:Ns], identb[:], rhs, start=False, stop=False)
                    pidx = qc * 6 + NA_J[qc].index(j)
                    rb = rowB[:, pidx, :].unsqueeze(2).to_broadcast([128, 8, 64])
                    ins = e.matmul(PS[b][:, 0:Ns], halfsel[:], rb, start=False, stop=True)
                return ins
            rd = ["qm%d" % qi, kkey]
            if na is not None and j is not None:
                rd += ["Tb%d.%d" % (na[0], n3) for n3 in range(3)] + ["identb", "halfsel", "rowB"]
            S.op("pe", mm, reads=rd, writes=[pk(b)])

        s_mm(0)
        if n > 1:
            s_mm(1)
        for i in range(n):
            off, Ns, (kc0, vt, j), st, sp_ = flat[i]
            b = sb_[i]
            pi = rot("pb", 3)
            S.op("act", lambda e, b=b, pi=pi, Ns=Ns: e.activation(pbuf[pi][:, 0:Ns], PS[b][:, 0:Ns], AF.Exp, scale=0.125),
                 reads=[pk(b)], writes=["pbuf%d" % pi])
            if i + 1 < n:
                pump()
                if LQ and i >= 1:
                    LQ.popleft()()
                if i + 2 < n:
                    s_mm(i + 2)
            S.op("pe", lambda e, pi=pi, vt=vt, st=st, sp_=sp_, off=off, Ns=Ns: e.matmul(
                PS[ob][0:65, off:off + Ns], VA[:, vt, h, 0:65], pbuf[pi][:, 0:Ns], start=st, stop=sp_),
                reads=["pbuf%d" % pi, "VA.%d" % vt, "VA.ones"], writes=[pk(ob)])
        ri = ob - 4
        S.op("dve", lambda e: e.reciprocal(rdb[ri][64:65, 0:N], PS[ob][64:65, 0:N]), reads=[pk(ob)], writes=["rdb%d" % ri])
        bb = 6 + (ob - 4)
        ok = "oT.%d.%d" % (ch, tc)

        def post():
            S.op("pe", lambda e: e.matmul(PS[bb][0:64, 0:N], sel64[:], rdb[ri][:, 0:N], start=True, stop=True),
                 reads=["rdb%d" % ri, "sel64"], writes=[pk(bb)])
            S.op("dve", lambda e: e.tensor_tensor(g2[0][pb:pb + 64, 0:N], oT[pb:pb + 64, ch, qcols], PS[bb][0:64, 0:N], ALU.mult),
                 reads=[ok, pk(bb)], writes=["g20"])
            S.op("dve", lambda e: e.tensor_tensor(oT[pb:pb + 64, ch, qcols], g2[0][pb:pb + 64, 0:N], PS[ob][0:64, 0:N], ALU.mult),
                 reads=["g20", pk(ob), ok], writes=[ok])
        LQ.append(post)

    S.alias(["Traw"], ["ckst0"])
    S.op("pool", lambda e: e.memset(Traw[:], 0.0), writes=["Traw"])
    S.op("pool", lambda e: e.memset(sel64[:], 0.0), writes=["sel64"])
    S.op("pool", lambda e: e.memset(sel64[64:65, :], 1.0), reads=["sel64"], writes=["sel64"])
    S.op("pool", lambda e: e.memset(rdb[0][:], 0.0), writes=["rdb0"])
    S.op("pool", lambda e: e.memset(rdb[1][:], 0.0), writes=["rdb1"])
    S.op("pool", lambda e: e.memset(qm[0][:], 0.0), writes=["qmz", "qm0"])
    S.op("pool", lambda e: e.memset(qm[1][:], 0.0), writes=["qmz", "qm1"])
    load_tb(0)
    for h in range(8):
        tbs = h % 2
        if h + 1 < 8:
            load_tb(h + 1)
        psegs = []
        for s in range(2):
            q0 = 1024 + s * 256
            psegs.append((slice(q0, q0 + 256), s * 256, [(q0 + t * 128, 8 + 2 * s + t, None) for t in range(2)]))
        attn64(h, psegs, 2)
        for qc in range(2):
            tiles = [(j * 128, j, j) for j in NA_J[qc]] + [(1536 + t * 128, 12 + t, None) for t in range(2)]
            attn64(h, [(slice(qc * 512, (qc + 1) * 512), 0, tiles)], qc, na=(tbs, qc))
    while LQ:
        LQ.popleft()()
    for _ in range(400):
        pump()
    bl = sbank()
    S.op("pe", lambda e: e.transpose(PS[bl][0:16, 0:128], lst[:, 0:16], ident[:]),
         reads=["lst.%d%d%d" % (c, s, d) for c in range(4) for s in range(2) for d in range(2)] + ["ident"], writes=[pk(bl)])
    S.op("dve", lambda e: e.tensor_copy(lstT[:], PS[bl][0:16, 0:128]), reads=[pk(bl)], writes=["lstT"])
    out_toks.append(dma("sp", o_lru.rearrange("s d (c p) -> (s d c) p", p=128), lstT[:], reads=["lstT"]))

    def wout_phase(L, d_wout, after_tc=None):
        slots = [wload([(d_wout[:, g * 512:(g + 1) * 512], 0)]) for g in range(2)]
        for tc in range(3):
            s = 0 if tc < 2 else 1
            for oc in range(8):
                slot, q = slots[oc // 4], oc % 4
                b = sbank()

                def mm(e, slot=slot, q=q, tc=tc, b=b):
                    for c in range(8):
                        ins = e.matmul(PS[b][:], ring[slot][:, c, q * 128:(q + 1) * 128], oT[:, c, tc * 512:(tc + 1) * 512],
                                       start=(c == 0), stop=(c == 7))
                    return ins
                S.op("pe", mm, reads=["oT.%d.%d" % (c, tc) for c in range(8)] + rkeys(slot, (q,)), writes=[pk(b)])
                xs_ = xT[:, oc, tc * 512:(tc + 1) * 512]
                S.op("dve", lambda e, xs_=xs_, b=b, oc=oc, s=s: e.scalar_tensor_tensor(
                    xs_, PS[b][:], mT[L][:, 16 + oc, s:s + 1], xs_, ALU.mult, ALU.add),
                    reads=[pk(b), "mT%d" % L, "xT.%d.%d" % (oc, tc)], writes=["xT.%d.%d" % (oc, tc)])
            if after_tc is not None:
                after_tc(tc)

    ckpt(6)
    S.alias(["hT"], ["la", "lu", "lhf", "lhb"])
    S.alias(["rstd", "sq0", "sq1", "sq2", "ntmp0", "ntmp1"], ["QT", "KT"])
    S.alias(["kvst0", "kvst1", "kvst2"], ["pbuf0", "pbuf1", "pbuf2", "pbuf3", "rden0", "rden1", "g20", "qm0", "qm1", "qmz", "rdb0", "rdb1"])
    wout_phase(0, d_ewout, after_tc=lambda tc: norm_phase(1, (tc,)))
    ckpt(8)

    S.alias(["V1"], ["VA", "Tb0", "Tb1", "xcb", "ckst0", "Traw"])
    S.alias(["QTh0", "QTh1", "KTh0", "KTh1", "KTc", "ptmp0", "ptmp1", "ptmp2", "ptmp3", "ptmp4", "ckst1"],
            ["rstd", "sq0", "sq1", "sq2", "ntmp0", "ntmp1", "QT", "KT"])
    S.alias(["ropeC", "ropeS", "rawb0", "rawb1"], ["xbT"])
    dma("sp", ropeC[:], d_ropeC, writes=["ropeC"])
    dma("sp", ropeS[:], d_ropeS, writes=["ropeS"])
    dma("pool", V1[:, 12:14, :], d_cdv.rearrange("(t p) f -> p t f", p=128), writes=["V1.12", "V1.13"])
    dma("sp", ckst1[:], d_cdk.rearrange("(t p) f -> p t f", p=128), writes=["ckst1"])
    for t in range(2):
        for hh in range(2):
            b = sbank()

            def trk1(e, t=t, hh=hh, b=b):
                for j in range(4):
                    h = hh * 4 + j
                    ins = e.transpose(PS[b][:, j * 128:(j + 1) * 128], ckst1[:, t, h * 128:(h + 1) * 128], ident[:])
                return ins
            S.op("pe", trk1, reads=["ckst1", "ident"], writes=[pk(b)])
            copy_op(evac_eng(), KTc[:, hh * 4:hh * 4 + 4, t * 128:(t + 1) * 128], PS[b][:].rearrange("p (j n) -> p j n", n=128),
                    [pk(b)], ["KTc.%d%d" % (t, hh)])
    KTCK = ["KTc.%d%d" % (t, hh) for t in range(2) for hh in range(2)]

    for g in range(2):
        slot = wload([(d_owin[:, 2048 + g * 512:2048 + (g + 1) * 512], 0)])
        for t in range(12):
            b = sbank()
            proj_tm(slot, t, b)
            S.op("dve", lambda e, t=t, b=b, g=g: e.tensor_copy(V1[:, t, g * 512:(g + 1) * 512], PS[b][:]),
                 reads=[pk(b)], writes=["V1.%d.%d" % (t, g)])
            if t >= 8:
                out_toks.append(stage_out(b, o_dfv[(t - 8) * 128:(t - 7) * 128, g * 512:(g + 1) * 512], extra=["V1.%d.%d" % (t, g)]))
    S.alias(["ptmp0", "ptmp1", "ptmp2", "ptmp3", "ptmp4"], ["ckst1"])

    def rope_evac(b, dst, dkey, tc):
        ri = rot("rb", 2)
        S.op("act", lambda e: e.copy(rawb[ri][:], PS[b][:]), reads=[pk(b)], writes=["rawb%d" % ri])
        b2 = sbank()
        S.op("pe", lambda e: e.matmul(PS[b2][:], permb[:], rawb[ri][:], start=True, stop=True),
             reads=["rawb%d" % ri, "permb"], writes=[pk(b2)])
        p1 = rot("pt", 3)
        S.op("dve", lambda e: e.tensor_tensor(ptmp[p1][:], PS[b][:], ropeC[:, tc * 512:(tc + 1) * 512], ALU.mult),
             reads=[pk(b), "ropeC"], writes=["ptmp%d" % p1])
        p2 = rot("pt", 3)
        S.op("dve", lambda e: e.tensor_tensor(ptmp[p2][:], PS[b2][:], ropeS[:, tc * 512:(tc + 1) * 512], ALU.mult),
             reads=[pk(b2), "ropeS"], writes=["ptmp%d" % p2])
        S.op("pool", lambda e: e.tensor_tensor(dst, ptmp[p1][:], ptmp[p2][:], ALU.add),
             reads=["ptmp%d" % p1, "ptmp%d" % p2], writes=[dkey])

    from collections import deque
    PQ = deque()
    TQ = deque()

    def tick(first=False):
        if TQ:
            TQ.popleft()()
        if first:
            for _ in range(3):
                if PQ:
                    PQ.popleft()()

    def drain(q):
        while q:
            q.popleft()()

    def diff_attn(h, hs, segs, tc):
        qk = "QTh%d.%d" % (hs, tc)
        flat = []
        for (qc_, off, tiles) in segs:
            for i, t_ in enumerate(tiles):
                flat.append((qc_, off, qc_.stop - qc_.start, t_, i == 0, i == len(tiles) - 1))
        n = len(flat)
        sb1 = [None] * n
        sb2 = [None] * n

        def s_mm(i):
            qc_, off, Ns, (kfn, kkey, vt), _, _ = flat[i]
            b1 = sbank()
            b2 = sbank()
            sb1[i], sb2[i] = b1, b2

            def mm(e):
                e.matmul(PS[b1][:, 0:Ns], kfn(slice(0, 64)), QTh[hs][0:64, qc_], start=True, stop=True)
                return e.matmul(PS[b2][:, 0:Ns], kfn(slice(64, 128)), QTh[hs][64:128, qc_], start=True, stop=True)
            S.op("pe", mm, reads=[qk] + kkey, writes=[pk(b1), pk(b2)])

        s_mm(0)
        for i in range(n):
            qc_, off, Ns, (kfn, kkey, vt), st, sp_ = flat[i]
            b1, b2 = sb1[i], sb2[i]
            p1 = rot("pb", 4)
            S.op("act", lambda e, b1=b1, p1=p1, Ns=Ns: e.activation(pbuf[p1][:, 0:Ns], PS[b1][:, 0:Ns], AF.Exp, scale=0.125),
                 reads=[pk(b1)], writes=["pbuf%d" % p1])
            p2 = rot("pb", 4)
            S.op("act", lambda e, b2=b2, p2=p2, Ns=Ns: e.activation(pbuf[p2][:, 0:Ns], PS[b2][:, 0:Ns], AF.Exp, scale=0.125),
                 reads=[pk(b2)], writes=["pbuf%d" % p2])
            if i + 1 < n:
                tick(i == 0)
                s_mm(i + 1)
            vk = ["V1.%d" % vt] if vt >= 12 else ["V1.%d.%d" % (vt, h // 4)]

            def pv1(e, p1=p1, vt=vt, st=st, sp_=sp_, off=off, Ns=Ns):
                vv = V1[:, vt, h * 128:(h + 1) * 128]
                e.matmul(PS[4][:, off:off + Ns], vv, pbuf[p1][:, 0:Ns], start=st, stop=sp_)
                return e.matmul(PS[6][:, off:off + Ns], onesb[:], pbuf[p1][:, 0:Ns], start=st, stop=sp_)

            def pv2(e, p2=p2, vt=vt, st=st, sp_=sp_, off=off, Ns=Ns):
                vv = V1[:, vt, h * 128:(h + 1) * 128]
                e.matmul(PS[5][:, off:off + Ns], vv, pbuf[p2][:, 0:Ns], start=st, stop=sp_)
                return e.matmul(PS[7][:, off:off + Ns], onesb[:], pbuf[p2][:, 0:Ns], start=st, stop=sp_)
            S.op("pe", pv1, reads=["pbuf%d" % p1, "onesb"] + vk, writes=[pk(4), pk(6)])
            S.op("pe", pv2, reads=["pbuf%d" % p2, "onesb"] + vk, writes=[pk(5), pk(7)])
        N = sum(q_.stop - q_.start for (q_, _, _) in segs)
        qcols = slice(segs[0][0].start, segs[-1][0].stop)
        r1 = rden[0][:, 0:N]
        r2 = rden[1][:, 0:N]
        a1 = ptmp[3][:, 0:N]
        a2 = ptmp[4][:, 0:N]
        drain(TQ)
        S.op("act", lambda e: e.copy(a1, PS[4][:, 0:N]), reads=[pk(4)], writes=["ptmp3"])
        S.op("dve", lambda e: e.tensor_copy(r1, PS[6][:, 0:N]), reads=[pk(6)], writes=["rden0"])
        S.op("act", lambda e: e.copy(a2, PS[5][:, 0:N]), reads=[pk(5)], writes=["ptmp4"])
        S.op("dve", lambda e: e.tensor_copy(r2, PS[7][:, 0:N]), reads=[pk(7)], writes=["rden1"])
        ok = "oT.%d.%d" % (h, tc)

        def stB():
            S.op("dve", lambda e: e.reciprocal(r2, r2), reads=["rden1"], writes=["rden1"])
            S.op("dve", lambda e: e.scalar_tensor_tensor(r2, r1, neglam[:, 0:1], r2, ALU.mult, ALU.mult),
                 reads=["rden0", "rden1", "neglam"], writes=["rden1"])
            S.op("dve", lambda e: e.tensor_tensor(a2, a2, r2, ALU.mult), reads=["ptmp4", "rden1"], writes=["ptmp4"])
            S.op("dve", lambda e: e.tensor_tensor(a1, a1, a2, ALU.add), reads=["ptmp3", "ptmp4"], writes=["ptmp3"])

        def stC():
            pi = rot("pb", 4)
            S.op("act", lambda e: e.activation(pbuf[pi][:, 0:N], a1, AF.Square), reads=["ptmp3"], writes=["pbuf%d" % pi])
            bs = sbank()
            S.op("pe", lambda e: e.matmul(PS[bs][:, 0:N], onesb[:], pbuf[pi][:, 0:N], start=True, stop=True),
                 reads=["pbuf%d" % pi, "onesb"], writes=[pk(bs)])
            S.op("dve", lambda e: e.scalar_tensor_tensor(a2, r1, EPS, r1, ALU.mult, ALU.mult), reads=["rden0", "ptmp4"], writes=["ptmp4"])
            S.op("dve", lambda e: e.scalar_tensor_tensor(r1, PS[bs][:, 0:N], 1.0 / 128.0, a2, ALU.mult, ALU.add),
                 reads=[pk(bs), "ptmp4", "rden0"], writes=["rden0"])

        I32 = mybir.dt.int32

        def stD():
            S.op("dve", lambda e: e.tensor_scalar(a2.bitcast(I32), r1.bitcast(I32), 1, None, ALU.arith_shift_right),
                 reads=["rden0", "ptmp4"], writes=["ptmp4"])
            S.op("dve", lambda e: e.tensor_scalar(r2.bitcast(I32), a2.bitcast(I32), -1.0, float(0x5f3759df), ALU.mult, ALU.add),
                 reads=["ptmp4", "rden1"], writes=["rden1"])
            for _ in range(2):
                S.op("dve", lambda e: e.tensor_tensor(a2, r2, r2, ALU.mult), reads=["rden1", "ptmp4"], writes=["ptmp4"])
                S.op("dve", lambda e: e.scalar_tensor_tensor(a2, a2, -0.5, r1, ALU.mult, ALU.mult), reads=["ptmp4", "rden0"], writes=["ptmp4"])
                S.op("dve", lambda e: e.scalar_tensor_tensor(r2, a2, 1.5, r2, ALU.add, ALU.mult), reads=["ptmp4", "rden1"], writes=["rden1"])
            S.op("dve", lambda e: e.scalar_tensor_tensor(a1, a1, sgs[:, 0:1], r2, ALU.mult, ALU.mult),
                 reads=["ptmp3", "sgs", "rden1"], writes=["ptmp3"])
            S.op("pool", lambda e: e.tensor_tensor(oT[:, h, qcols], oT[:, h, qcols], a1, ALU.mult), reads=["ptmp3", ok], writes=[ok])
        TQ.append(stB)
        TQ.append(stC)
        TQ.append(stD)

    def proj_units(h):
        hs = h % 2
        slot = wload([(d_owin[:, h * 128:(h + 1) * 128], 0), (d_owin[:, 1024 + h * 128:1024 + (h + 1) * 128], 1),
                      (d_owin[:, 3072 + h * 128:3072 + (h + 1) * 128], 2)])
        units = []
        for tc in range(3):
            cs = slice(tc * 512, (tc + 1) * 512)
            for q, (dstT, nm) in enumerate(((QTh[hs], "QTh%d" % hs), (KTh[hs], "KTh%d" % hs))):
                def u(q=q, dstT=dstT, nm=nm, tc=tc, cs=cs):
                    b = sbank()
                    proj_fm(slot, q, tc, b)
                    if tc < 2:
                        rope_evac(b, dstT[:, cs], "%s.%d" % (nm, tc), tc)
                    else:
                        copy_op("dve", dstT[:, cs], PS[b][:], [pk(b)], ["%s.%d" % (nm, tc)])
                units.append(u)

            def ug(tc=tc, cs=cs):
                b = sbank()
                proj_fm(slot, 2, tc, b)
                p1 = rot("pt", 3)
                S.op("act", lambda e: e.activation(ptmp[p1][:], PS[b][:], AF.Tanh, scale=0.5), reads=[pk(b)], writes=["ptmp%d" % p1])
                S.op("dve", lambda e: e.scalar_tensor_tensor(oT[:, h, cs], ptmp[p1][:], 1.0, PS[b][:], ALU.add, ALU.mult),
                     reads=[pk(b), "ptmp%d" % p1], writes=["oT.%d.%d" % (h, tc)])
            units.append(ug)
        return units

    kslots = [wload([(d_owin[:, 1024 + g * 512:1024 + (g + 1) * 512], 0)]) for g in range(2)]
    PQ.extend(proj_units(0))
    for g in range(2):
        for t in range(8, 12):
            b = sbank()
            proj_tm(kslots[g], t, b)
            out_toks.append(stage_out(b, o_dfk[(t - 8) * 128:(t - 7) * 128, g * 512:(g + 1) * 512]))
            if PQ:
                PQ.popleft()()
    drain(PQ)
    ckpt(9)
    S.alias(["pbuf0", "pbuf1", "pbuf2", "pbuf3", "rden0", "rden1", "g20", "qm0", "qm1", "qmz", "rdb0", "rdb1"], ["kvst0", "kvst1", "kvst2"])
    for h in range(8):
        hs = h % 2
        if h + 1 < 8:
            PQ.extend(proj_units(h + 1))
        psegs = []
        for s in range(2):
            q0 = 1024 + s * 256
            tiles = [((lambda ps_, c0=q0 + t * 128, hs=hs: KTh[hs][ps_, c0:c0 + 128]), ["KTh%d.2" % hs], 8 + 2 * s + t) for t in range(2)]
            psegs.append((slice(q0, q0 + 256), s * 256, tiles))
        diff_attn(h, hs, psegs, 2)
        for qc in range(2):
            tiles = [((lambda ps_, t=t, h=h: KTc[ps_, h, t * 128:(t + 1) * 128]), KTCK, 12 + t) for t in range(2)]
            tiles += [((lambda ps_, j=j, hs=hs: KTh[hs][ps_, j * 128:(j + 1) * 128]), ["KTh%d.%d" % (hs, j // 4)], j) for j in range(8)]
            diff_attn(h, hs, [(slice(qc * 512, (qc + 1) * 512), 0, tiles)], qc)
        drain(PQ)
        ckpt(9.1 + 0.1 * h)
    drain(TQ)

    ckpt(10)
    S.alias(["xstage0", "xstage1", "fnbc"], ["QTh0", "QTh1", "KTh0", "KTh1", "KTc", "ptmp0", "ptmp1", "ptmp2", "ptmp3", "ptmp4", "ckst1"])
    dma("sp", fnbc[:], d_fnorm.partition_broadcast(128), writes=["fnbc"])

    def final_tc(tc):
        for t in range(4 * tc, 4 * tc + 4):
            st = t % 2
            bA, bB = sbank(), sbank()
            for half, b in ((0, bA), (1, bB)):
                def trf(e, half=half, b=b, t=t):
                    for j in range(4):
                        c = half * 4 + j
                        ins = e.transpose(PS[b][:, j * 128:(j + 1) * 128], xT[:, c, t * 128:(t + 1) * 128], ident[:])
                    return ins
                S.op("pe", trf, reads=["xT.%d.%d" % (half * 4 + j, tc) for j in range(4)] + ["ident"], writes=[pk(b)])
            sk = "sstat%d" % st
            so = st * 4
            S.op("act", lambda e, bA=bA, st=st, so=so: e.activation(xstage[st][:, 0:512], PS[bA][:], AF.Square, accum_out=sstat[:, so:so + 1]),
                 reads=[pk(bA)], writes=["xstage%d" % st, sk + ".0"])
            S.op("act", lambda e, bB=bB, st=st, so=so: e.activation(xstage[st][:, 512:1024], PS[bB][:], AF.Square, accum_out=sstat[:, so + 1:so + 2]),
                 reads=[pk(bB), "xstage%d" % st], writes=["xstage%d" % st, sk + ".1"])
            S.op("dve", lambda e, so=so: e.tensor_tensor(sstat[:, so + 2:so + 3], sstat[:, so:so + 1], sstat[:, so + 1:so + 2], ALU.add),
                 reads=[sk + ".0", sk + ".1"], writes=[sk + ".2"])
            S.op("dve", lambda e, so=so: e.tensor_scalar(sstat[:, so + 2:so + 3], sstat[:, so + 2:so + 3], 1.0 / 1024.0, EPS, ALU.mult, ALU.add),
                 reads=[sk + ".2"], writes=[sk + ".2"])
            S.op("act", lambda e, so=so: e.activation(sstat[:, so + 2:so + 3], sstat[:, so + 2:so + 3], AF.Sqrt), reads=[sk + ".2"], writes=[sk + ".2"])
            S.op("dve", lambda e, so=so: e.reciprocal(sstat[:, so + 3:so + 4], sstat[:, so + 2:so + 3]), reads=[sk + ".2"], writes=[sk + ".3"])
            S.op("dve", lambda e, bA=bA, st=st, so=so: e.scalar_tensor_tensor(xstage[st][:, 0:512], PS[bA][:], sstat[:, so + 3:so + 4], fnbc[:, 0:512], ALU.mult, ALU.mult),
                 reads=[pk(bA), sk + ".3", "fnbc", "xstage%d" % st], writes=["xstage%d" % st])
            S.op("dve", lambda e, bB=bB, st=st, so=so: e.scalar_tensor_tensor(xstage[st][:, 512:1024], PS[bB][:], sstat[:, so + 3:so + 4], fnbc[:, 512:1024], ALU.mult, ALU.mult),
                 reads=[pk(bB), sk + ".3", "fnbc", "xstage%d" % st], writes=["xstage%d" % st])
            dst = o_ys[t * 128:(t + 1) * 128, :] if t < 8 else o_yp[(t - 8) * 128:(t - 7) * 128, :]
            out_toks.append(dma("sp", dst, xstage[st][:], reads=["xstage%d" % st]))

    wout_phase(1, d_owout, after_tc=final_tc)
    ckpt(11)

    S.wait_all("sp", out_toks)
    with nc.allow_non_contiguous_dma(reason="small strided parameter / state vectors"), \
            nc.allow_low_precision(reason="bf16 copies of fp32-computed values that feed bf16 matmul operands"):
        S.emit(nc, sems, dsems)
    es.close()
    return nc


_CONST = {}


def _consts():
    if not _CONST:
        C, Sg, perm = _rope_tables()
        colneg, rowB, halfsel, anti = _na_tables()
        _CONST.update(dict(c_ident=np.eye(128, dtype=np.float32), c_ropeC=C, c_ropeS=Sg, c_perm=perm,
                           c_colneg=colneg, c_rowB=rowB, c_halfsel=halfsel, c_anti=anti))
    return _CONST


def kernel(x_prompt, x_sample, c, cache_na_k, cache_na_v, state_lru, cache_diff_k, cache_diff_v, c_ctx,
           e_norm, e_ada_w, e_ada_b, e_w_in, e_rpb, e_conv_w, e_conv_b, e_lru_wa, e_lru_ba, e_lru_wx, e_lru_bx,
           e_lru_lam, e_w_out, o_norm, o_ada_w, o_ada_b, o_w_in, o_lq1, o_lk1, o_lq2, o_lk2, o_sub_g, o_w_out,
           final_norm):
    f = lambda a: np.ascontiguousarray(np.asarray(a, dtype=np.float32))
    x_prompt, x_sample, c = f(x_prompt), f(x_sample), f(c)
    shared = dict(
        e_norm=f(e_norm)[0], e_ada_w=f(e_ada_w)[0], e_ada_b=f(e_ada_b)[0], e_w_in=f(e_w_in)[0], e_rpb=f(e_rpb)[0],
        e_conv_w=f(e_conv_w)[0], e_conv_b=f(e_conv_b)[0], e_lru_wa=f(e_lru_wa)[0], e_lru_ba=f(e_lru_ba)[0],
        e_lru_wx=f(e_lru_wx)[0], e_lru_bx=f(e_lru_bx)[0], e_lru_lam=f(e_lru_lam)[0], e_w_out=f(e_w_out)[0],
        o_norm=f(o_norm)[0], o_ada_w=f(o_ada_w)[0], o_ada_b=f(o_ada_b)[0], o_w_in=f(o_w_in)[0],
        o_lq1=f(o_lq1)[0], o_lk1=f(o_lk1)[0], o_lq2=f(o_lq2)[0], o_lk2=f(o_lk2)[0], o_sub_g=f(o_sub_g)[0],
        o_w_out=f(o_w_out)[0], final_norm=f(final_norm))
    shared = {k: np.ascontiguousarray(v) for k, v in shared.items()}
    shared.update(_consts())
    cna_k, cna_v, slru = f(cache_na_k), f(cache_na_v), f(state_lru)
    cdk, cdv, cctx = f(cache_diff_k), f(cache_diff_v), f(c_ctx)
    in_maps = []
    for i in range(NCORES):
        m = dict(shared)
        m["xs"] = x_sample[i]
        m["xp"] = np.ascontiguousarray(x_prompt[2 * i:2 * i + 2].reshape(512, 1024))
        m["cvec"] = np.ascontiguousarray(np.stack([c[i], cctx], axis=0))
        m["cnk"] = np.ascontiguousarray(cna_k[i, 0].reshape(256, 512))
        m["cnv"] = np.ascontiguousarray(cna_v[i, 0].reshape(256, 512))
        m["slru"] = np.ascontiguousarray(slru[i, 0])
        m["cdk"] = np.ascontiguousarray(cdk[i, 0].reshape(256, 1024))
        m["cdv"] = np.ascontiguousarray(cdv[i, 0].reshape(256, 1024))
        in_maps.append(m)
    nc = build_nc()
    res = run_bass_kernel_spmd(nc, in_maps, core_ids=list(range(NCORES)))
    R = res.results
    y_prompt = np.concatenate([R[i]["y_p"].reshape(2, 256, 1024) for i in range(NCORES)], axis=0)
    y_sample = np.stack([R[i]["y_s"] for i in range(NCORES)], axis=0)
    na_k = np.concatenate([R[i]["na_k"].reshape(2, 1, 256, 8, 64) for i in range(NCORES)], axis=0)
    na_v = np.concatenate([R[i]["na_v"].reshape(2, 1, 256, 8, 64) for i in range(NCORES)], axis=0)
    lru = np.concatenate([R[i]["lru_o"].reshape(2, 1, 2, 512) for i in range(NCORES)], axis=0)
    df_k = np.concatenate([R[i]["df_k"].reshape(2, 1, 256, 8, 128) for i in range(NCORES)], axis=0)
    df_v = np.concatenate([R[i]["df_v"].reshape(2, 1, 256, 8, 128) for i in range(NCORES)], axis=0)
    return (y_prompt.astype(np.float32), y_sample.astype(np.float32), na_k.astype(np.float32), na_v.astype(np.float32),
            lru.astype(np.float32), df_k.astype(np.float32), df_v.astype(np.float32))
```
